# Optimizing a Trainium2 kernel written in Bass

```python
import jax, jax.numpy as jnp
from jax import lax
import numpy as np

D_MODEL = 1024
BATCH = 16
SEQ = 2048
DEPTH = 2

HEAD_DIM = 64
GROUP_WIDTH = D_MODEL // 4
N_GROUP_HEADS = GROUP_WIDTH // HEAD_DIM
D_MIX = 4 * GROUP_WIDTH

RWKV_DECAY_RANK = 32
RWKV_AAA_RANK = 32
RWKV_GATE_RANK = 64
RWKV_GN_EPS = 64e-5
RWKV_COLS = 3 * GROUP_WIDTH + RWKV_DECAY_RANK + RWKV_AAA_RANK + RWKV_GATE_RANK

LRU_CONV_WIDTH = 4
LRU_C = 8.0

MOBA_BLOCK = 256
MOBA_TOPK = 3
Q_BLOCK = 128
ROPE_THETA = 10000.0
NEG_INF = -1e30

RET_CHUNK = 128
GN_EPS = 1e-5

D_FF = 2816
FFN_CONV_WIDTH = 3
LN_EPS = 1e-5

DEEPNORM_ALPHA = (2.0 * DEPTH) ** 0.25
DEEPNORM_BETA = (8.0 * DEPTH) ** -0.25

IN_SIZES = (RWKV_COLS,
            GROUP_WIDTH, GROUP_WIDTH,
            GROUP_WIDTH, GROUP_WIDTH, GROUP_WIDTH,
            GROUP_WIDTH, GROUP_WIDTH, GROUP_WIDTH, GROUP_WIDTH)
IN_COLS = sum(IN_SIZES)
IN_SPLITS = tuple(int(s) for s in np.cumsum(IN_SIZES)[:-1])
RWKV_SIZES = (GROUP_WIDTH, GROUP_WIDTH, GROUP_WIDTH, RWKV_DECAY_RANK, RWKV_AAA_RANK, RWKV_GATE_RANK)
RWKV_SPLITS = tuple(int(s) for s in np.cumsum(RWKV_SIZES)[:-1])

kernel_name = 'hymba_style_rwkv7_rglru_moba_retnet_trunk'


def layer_norm(x, g, b):
    xf = x.astype(jnp.float32)
    mu = xf.mean(-1, keepdims=True)
    var = jnp.mean(jnp.square(xf - mu), -1, keepdims=True)
    return ((xf - mu) * lax.rsqrt(var + LN_EPS) * g + b).astype(x.dtype)


def head_norm(y, g, b, eps):
    yf = y.astype(jnp.float32)
    mu = yf.mean(-1, keepdims=True)
    var = jnp.mean(jnp.square(yf - mu), -1, keepdims=True)
    shape = y.shape[-2:]
    return (yf - mu) * lax.rsqrt(var + eps) * g.reshape(shape) + b.reshape(shape)


def causal_depthwise_conv(x, w, b):
    k_width, ch = w.shape
    y = lax.conv_general_dilated(x, w[:, None, :].astype(x.dtype), window_strides=(1,),
                                 padding=[(k_width - 1, 0)],
                                 dimension_numbers=('NWC', 'WIO', 'NWC'),
                                 feature_group_count=ch)
    return y + b


def token_shift(p):
    return jnp.pad(p[:, :-1], ((0, 0), (1, 0), (0, 0)))


def rope_tables(seq, dim):
    inv = ROPE_THETA ** (-jnp.arange(0, dim, 2, dtype=jnp.float32) / dim)
    ang = jnp.arange(seq, dtype=jnp.float32)[:, None] * inv[None, :]
    return jnp.cos(ang), jnp.sin(ang)


def apply_rope(x, cos, sin):
    x1, x2 = jnp.split(x.astype(jnp.float32), 2, axis=-1)
    return jnp.concatenate([x1 * cos - x2 * sin, x1 * sin + x2 * cos], axis=-1).astype(x.dtype)


def rwkv7_mix(pa, mu, w0, w_up, a0, a_up, g_up, k_k, k_a, r_k, gn_g, gn_b):
    bsz, seq, _ = pa.shape
    H, N = N_GROUP_HEADS, HEAD_DIM
    pa = pa + (token_shift(pa) - pa) * mu
    r, k, v, wd, ad, gd = jnp.split(pa, RWKV_SPLITS, axis=-1)
    w_log = -jax.nn.softplus(-(w0 + jnp.tanh(wd) @ w_up)) - 0.5
    decay = jnp.exp(-jnp.exp(w_log.astype(jnp.float32)))
    a = jax.nn.sigmoid(a0 + ad @ a_up)
    g = jax.nn.sigmoid(gd) @ g_up
    kk = k * k_k
    k = k * (1.0 + (a - 1.0) * k_a)
    heads = lambda t: t.reshape(bsz, seq, H, N).astype(jnp.float32)
    r, k, v, kk, decay, a = map(heads, (r, k, v, kk, decay, a))
    kk = kk / jnp.maximum(jnp.linalg.norm(kk, axis=-1, keepdims=True), 1e-12)
    b_vec = kk * a

    def step(state, inp):
        r_t, k_t, v_t, kk_t, b_t, w_t = inp
        sa = jnp.einsum('bhij,bhj->bhi', state, kk_t)
        state = (state * w_t[..., None, :] - sa[..., :, None] * b_t[..., None, :]
                 + v_t[..., :, None] * k_t[..., None, :])
        return state, jnp.einsum('bhij,bhj->bhi', state, r_t)

    xs = tuple(jnp.moveaxis(t, 1, 0) for t in (r, k, v, kk, b_vec, decay))
    _, y = lax.scan(step, jnp.zeros((bsz, H, N, N), jnp.float32), xs)
    y = head_norm(jnp.moveaxis(y, 0, 1), gn_g, gn_b, RWKV_GN_EPS)
    y = y + jnp.sum(r * k * r_k, axis=-1, keepdims=True) * v
    return (y.reshape(bsz, seq, H * N) * g).astype(pa.dtype)


def rglru_mix(xb, gb, conv_w, conv_b, w_r, b_r, w_i, b_i, lam):
    bsz, seq, ch = xb.shape
    H = N_GROUP_HEADS
    xc = causal_depthwise_conv(xb, conv_w, conv_b)
    xh = xc.reshape(bsz, seq, H, ch // H)
    gate_r = jax.nn.sigmoid(jnp.einsum('bshi,hij->bshj', xh, w_r).reshape(bsz, seq, ch) + b_r)
    gate_i = jax.nn.sigmoid(jnp.einsum('bshi,hij->bshj', xh, w_i).reshape(bsz, seq, ch) + b_i)
    log_a = (-LRU_C * gate_r * jax.nn.softplus(-lam)).astype(jnp.float32)
    a = jnp.exp(log_a)
    u = jnp.sqrt(-jnp.expm1(2.0 * log_a)) * (gate_i * xc).astype(jnp.float32)

    def combine(left, right):
        a_l, h_l = left
        a_r, h_r = right
        return a_l * a_r, a_r * h_l + h_r

    _, h = lax.associative_scan(combine, (a, u), axis=1)
    return (jax.nn.gelu(gb) * h).astype(xb.dtype)


def moba_attention(q, k, v, cos, sin):
    bsz, seq, ch = q.shape
    H, D = N_GROUP_HEADS, HEAD_DIM
    to_heads = lambda t: t.reshape(bsz, seq, H, D).transpose(0, 2, 1, 3)
    q = apply_rope(to_heads(q), cos, sin)
    k = apply_rope(to_heads(k), cos, sin)
    v = to_heads(v)
    n_blocks = -(-seq // MOBA_BLOCK)
    pad = n_blocks * MOBA_BLOCK - seq
    k_blocks = jnp.pad(k, ((0, 0), (0, 0), (0, pad), (0, 0))).reshape(bsz, H, n_blocks, MOBA_BLOCK, D)
    v_blocks = jnp.pad(v, ((0, 0), (0, 0), (0, pad), (0, 0))).reshape(bsz, H, n_blocks, MOBA_BLOCK, D)
    k_mean = k_blocks.astype(jnp.float32).mean(axis=3)
    top = min(MOBA_TOPK, n_blocks)
    scale = D ** -0.5
    b_idx = jnp.arange(bsz)[:, None, None]
    h_idx = jnp.arange(H)[None, :, None]

    def one_chunk(c):
        q0 = c * Q_BLOCK
        blk = q0 // MOBA_BLOCK
        qc = lax.dynamic_slice_in_dim(q, q0, Q_BLOCK, axis=2).astype(jnp.float32)
        q_pos = q0 + jnp.arange(Q_BLOCK)
        gate = jnp.einsum('bhqd,bhnd->bhqn', qc, k_mean)
        gate = jnp.where(jnp.arange(n_blocks) < blk, gate, -jnp.inf)
        _, sel = lax.top_k(gate, top)
        k_own = lax.dynamic_slice_in_dim(k_blocks, blk, 1, axis=2)[:, :, 0]
        v_own = lax.dynamic_slice_in_dim(v_blocks, blk, 1, axis=2)[:, :, 0]
        k_pos = blk * MOBA_BLOCK + jnp.arange(MOBA_BLOCK)
        s_own = jnp.einsum('bhqd,bhkd->bhqk', qc, k_own) * scale
        scores = [jnp.where(k_pos[None, :] <= q_pos[:, None], s_own, NEG_INF)]
        for s in range(top):
            kg = k_blocks[b_idx, h_idx, sel[..., s]]
            s_sel = jnp.einsum('bhqd,bhqkd->bhqk', qc, kg) * scale
            scores.append(jnp.where(s < blk, s_sel, NEG_INF))
        p = jax.nn.softmax(jnp.concatenate(scores, axis=-1), axis=-1)
        out = jnp.einsum('bhqk,bhkd->bhqd', p[..., :MOBA_BLOCK], v_own)
        for s in range(top):
            vg = v_blocks[b_idx, h_idx, sel[..., s]]
            out = out + jnp.einsum('bhqk,bhqkd->bhqd',
                                   p[..., (s + 1) * MOBA_BLOCK:(s + 2) * MOBA_BLOCK], vg)
        return out

    out = lax.map(one_chunk, jnp.arange(seq // Q_BLOCK))
    return out.transpose(1, 0, 3, 2, 4).reshape(bsz, seq, ch).astype(q.dtype)


def retention_mix(q, k, v, g, cos, sin, gn_g, gn_b):
    bsz, seq, ch = q.shape
    H, D, L = N_GROUP_HEADS, HEAD_DIM, RET_CHUNK
    n_chunks = seq // L
    to_heads = lambda t: t.reshape(bsz, seq, H, D).transpose(0, 2, 1, 3)
    qh = apply_rope(to_heads(q), cos, sin).astype(jnp.float32)
    kh = apply_rope(to_heads(k), cos, sin).astype(jnp.float32) * D ** -0.5
    vh = to_heads(v).astype(jnp.float32)
    qc = qh.reshape(bsz, H, n_chunks, L, D)
    kc = kh.reshape(bsz, H, n_chunks, L, D)
    vc = vh.reshape(bsz, H, n_chunks, L, D)
    log_gamma = jnp.log1p(-jnp.power(2.0, -5.0 - jnp.arange(H, dtype=jnp.float32)))
    idx = jnp.arange(L, dtype=jnp.float32)
    diff = idx[:, None] - idx[None, :]
    decay_mask = jnp.where(diff >= 0, jnp.exp(log_gamma[:, None, None] * jnp.maximum(diff, 0.0)), 0.0)
    intra = jnp.einsum('bhcid,bhcjd->bhcij', qc, kc) * decay_mask[None, :, None]
    out = jnp.einsum('bhcij,bhcje->bhcie', intra, vc)
    k_w = jnp.exp(log_gamma[:, None] * (L - 1 - idx))
    kv = jnp.einsum('bhcjd,bhcje->bhcde', kc * k_w[None, :, None, :, None], vc)
    chunk_decay = jnp.exp(log_gamma * L)[None, :, None, None]

    def step(state, kv_c):
        return state * chunk_decay + kv_c, state

    _, states = lax.scan(step, jnp.zeros((bsz, H, D, D), jnp.float32), jnp.moveaxis(kv, 2, 0))
    states = jnp.moveaxis(states, 0, 2)
    q_w = jnp.exp(log_gamma[:, None] * (idx + 1.0))
    out = out + jnp.einsum('bhcid,bhcde->bhcie', qc * q_w[None, :, None, :, None], states)
    out = out.reshape(bsz, H, seq, D).transpose(0, 2, 1, 3)
    out = head_norm(out, gn_g, gn_b, GN_EPS).reshape(bsz, seq, ch)
    return (jax.nn.silu(g) * out).astype(q.dtype)


def conv_ffn(x, w_up, conv_w, conv_b, w_down):
    u = causal_depthwise_conv(x @ w_up, conv_w, conv_b)
    gate, val = jnp.split(u, 2, axis=-1)
    return (jax.nn.gelu(gate) * val) @ w_down


def setup_inputs(seed: int = 0) -> dict:
    key = jax.random.key(seed)
    ks = jax.random.split(key, 32)
    L, G, H, N = DEPTH, GROUP_WIDTH, N_GROUP_HEADS, HEAD_DIM

    def nrm(k, shape, scale):
        return jax.random.normal(k, shape, jnp.float32) * scale

    def gain(k, shape):
        return 1.0 + nrm(k, shape, 0.02)

    lru_u = jax.random.uniform(ks[17], (L, G), jnp.float32, 0.9, 0.999)
    lru_a = lru_u ** (1.0 / LRU_C)
    return {
        'x': nrm(ks[0], (BATCH, SEQ, D_MODEL), 1.0),
        'w_in': nrm(ks[1], (L, D_MODEL, IN_COLS), D_MODEL ** -0.5),
        'tshift_mu': jax.random.uniform(ks[2], (L, RWKV_COLS), jnp.float32),
        'rwkv_w0': jax.random.uniform(ks[3], (L, G), jnp.float32, -6.0, 0.0),
        'rwkv_w_up': nrm(ks[4], (L, RWKV_DECAY_RANK, G), 0.1 * RWKV_DECAY_RANK ** -0.5),
        'rwkv_a0': nrm(ks[5], (L, G), 0.1),
        'rwkv_a_up': nrm(ks[6], (L, RWKV_AAA_RANK, G), RWKV_AAA_RANK ** -0.5),
        'rwkv_g_up': nrm(ks[7], (L, RWKV_GATE_RANK, G), RWKV_GATE_RANK ** -0.5),
        'rwkv_k_k': 0.85 + nrm(ks[8], (L, G), 0.02),
        'rwkv_k_a': 1.0 + nrm(ks[9], (L, G), 0.02),
        'rwkv_r_k': nrm(ks[10], (L, H, N), 0.1),
        'rwkv_gn_g': gain(ks[11], (L, G)),
        'rwkv_gn_b': nrm(ks[12], (L, G), 0.02),
        'lru_conv_w': nrm(ks[13], (L, LRU_CONV_WIDTH, G), LRU_CONV_WIDTH ** -0.5),
        'lru_conv_b': nrm(ks[14], (L, G), 0.02),
        'lru_w_r': nrm(ks[15], (L, H, N, N), N ** -0.5),
        'lru_b_r': nrm(ks[16], (L, G), 0.02),
        'lru_w_i': nrm(ks[18], (L, H, N, N), N ** -0.5),
        'lru_b_i': nrm(ks[19], (L, G), 0.02),
        'lru_lambda': jnp.log(lru_a) - jnp.log1p(-lru_a),
        'ret_gn_g': gain(ks[20], (L, G)),
        'ret_gn_b': nrm(ks[21], (L, G), 0.02),
        'w_out': nrm(ks[22], (L, D_MIX, D_MODEL), DEEPNORM_BETA * D_MIX ** -0.5),
        'ln1_g': gain(ks[23], (L, D_MODEL)),
        'ln1_b': nrm(ks[24], (L, D_MODEL), 0.02),
        'ffn_w_up': nrm(ks[25], (L, D_MODEL, 2 * D_FF), D_MODEL ** -0.5),
        'ffn_conv_w': nrm(ks[26], (L, FFN_CONV_WIDTH, 2 * D_FF), FFN_CONV_WIDTH ** -0.5),
        'ffn_conv_b': nrm(ks[27], (L, 2 * D_FF), 0.02),
        'ffn_w_down': nrm(ks[28], (L, D_FF, D_MODEL), DEEPNORM_BETA * D_FF ** -0.5),
        'ln2_g': gain(ks[29], (L, D_MODEL)),
        'ln2_b': nrm(ks[30], (L, D_MODEL), 0.02),
    }


def reference(x, w_in, tshift_mu, rwkv_w0, rwkv_w_up, rwkv_a0, rwkv_a_up, rwkv_g_up,
              rwkv_k_k, rwkv_k_a, rwkv_r_k, rwkv_gn_g, rwkv_gn_b, lru_conv_w, lru_conv_b,
              lru_w_r, lru_b_r, lru_w_i, lru_b_i, lru_lambda, ret_gn_g, ret_gn_b, w_out,
              ln1_g, ln1_b, ffn_w_up, ffn_conv_w, ffn_conv_b, ffn_w_down, ln2_g, ln2_b):
    cos, sin = rope_tables(x.shape[1], HEAD_DIM)
    for l in range(DEPTH):
        p = x @ w_in[l]
        (p_rwkv, lru_x, lru_gate, att_q, att_k, att_v,
         ret_q, ret_k, ret_v, ret_g) = jnp.split(p, IN_SPLITS, axis=-1)
        y_a = rwkv7_mix(p_rwkv, tshift_mu[l], rwkv_w0[l], rwkv_w_up[l], rwkv_a0[l], rwkv_a_up[l],
                        rwkv_g_up[l], rwkv_k_k[l], rwkv_k_a[l], rwkv_r_k[l], rwkv_gn_g[l], rwkv_gn_b[l])
        y_b = rglru_mix(lru_x, lru_gate, lru_conv_w[l], lru_conv_b[l], lru_w_r[l], lru_b_r[l],
                        lru_w_i[l], lru_b_i[l], lru_lambda[l])
        y_c = moba_attention(att_q, att_k, att_v, cos, sin)
        y_d = retention_mix(ret_q, ret_k, ret_v, ret_g, cos, sin, ret_gn_g[l], ret_gn_b[l])
        mixed = jnp.concatenate([y_a, y_b, y_c, y_d], axis=-1) @ w_out[l]
        x = layer_norm(DEEPNORM_ALPHA * x + mixed, ln1_g[l], ln1_b[l])
        ffn = conv_ffn(x, ffn_w_up[l], ffn_conv_w[l], ffn_conv_b[l], ffn_w_down[l])
        x = layer_norm(DEEPNORM_ALPHA * x + ffn, ln2_g[l], ln2_b[l])
    return x
```

```python
import numpy as np
import ml_dtypes
import concourse.bass as bass
import concourse.mybir as mybir
from concourse.bass_utils import run_bass_kernel_spmd
from contextlib import ExitStack

F32 = mybir.dt.float32
BF16 = mybir.dt.bfloat16
F32R = mybir.dt.float32r
AF = mybir.ActivationFunctionType
ALU = mybir.AluOpType
AX = mybir.AxisListType

D = 1024
T = 2048
NH = 4
HD = 64
GW = 256
DFF = 2816
INC = 3200
DEPTH = 2
NCORES = 8
ALPHA = (2.0 * DEPTH) ** 0.25
LN_EPS = 1e-5
SCOLS = 20480


def CH(k):
    return list(range(8 * k, 8 * k + 8))

ENGS = ['pe', 'act', 'dve', 'pool', 'sp']


class Buf:
    __slots__ = ('name', 'last_w', 'readers', 'excl')

    def __init__(self, name=''):
        self.name = name
        self.last_w = None
        self.readers = {}
        self.excl = False


class Ins:
    __slots__ = ('eng', 'fn', 'deps', 'is_dma', 'slot', 'val', 'needed')

    def __init__(self, eng, fn):
        self.eng = eng
        self.fn = fn
        self.deps = []
        self.is_dma = False
        self.slot = -1
        self.val = 0
        self.needed = False


class V:
    __slots__ = ('ap', 'bufs')

    def __init__(self, ap, bufs):
        self.ap = ap
        self.bufs = bufs


class Tl:
    def __init__(self, t, name='', nsub=0, hier=False):
        self.t = t
        self.buf = Buf(name)
        self.subs = [Buf(f"{name}.{i}") for i in range(nsub)]
        self.hier = hier

    def __getitem__(self, key):
        return V(self.t[key], self.subs if self.hier else [self.buf])

    def v(self, ap, sub=None):
        if sub is None:
            return V(ap, self.subs if self.hier else [self.buf])
        if not self.subs:
            return V(ap, [self.buf])
        if isinstance(sub, int):
            return V(ap, [self.subs[sub]])
        return V(ap, [self.subs[i] for i in sub])

    def all(self):
        return V(self.t[:], [self.buf] + self.subs)


def _bufs(*vs):
    out = []
    for v in vs:
        if isinstance(v, V):
            out.extend(v.bufs)
    return out


def _ap(v):
    return v.ap if isinstance(v, V) else v


class Prog:
    def __init__(self, nc, n_dma_sems=24):
        self.nc = nc
        self.streams = {e: [] for e in ENGS}
        self.n_dma = n_dma_sems
        self.dma_rr = 0
        self.dma_rr_sw = 0
        self.dma_last = [None] * n_dma_sems
        self.dma_order = []
        self.es = ExitStack()
        self.nbuf = 0

    def sb(self, shape, dtype, name=None, nsub=0):
        self.nbuf += 1
        name = "s_" + (name or f"sb{self.nbuf}")
        t = self.es.enter_context(self.nc.sbuf_tensor(name, list(shape), dtype))
        return Tl(t, name, nsub)

    def ps(self, shape, dtype, name=None, nsub=0):
        self.nbuf += 1
        name = "p_" + (name or f"ps{self.nbuf}")
        t = self.es.enter_context(self.nc.psum_tensor(name, list(shape), dtype))
        tl = Tl(t, name, nsub, hier=(nsub > 0))
        tl.buf.excl = True
        for b in tl.subs:
            b.excl = True
        return tl

    def op(self, eng, fn, reads=(), writes=()):
        ins = Ins(eng, fn)
        deps = []
        if any(b.excl for b in reads):
            writes = list(writes) + [b for b in reads if b.excl]
            reads = [b for b in reads if not b.excl]
        for b in reads:
            if b.last_w is not None:
                deps.append(b.last_w)
        for b in writes:
            if b.last_w is not None:
                deps.append(b.last_w)
            deps.extend(b.readers.values())
        seen = set()
        out = []
        for d in deps:
            if d is ins or id(d) in seen:
                continue
            seen.add(id(d))
            if eng == 'pe' and d.eng == 'pe' and not d.is_dma:
                continue
            out.append(d)
        ins.deps = out
        for b in writes:
            b.last_w = ins
            b.readers = {}
        for b in reads:
            if b.last_w is ins:
                continue
            b.readers[eng] = ins
        self.streams[eng].append(ins)
        return ins

    def dma(self, eng, out, in_, **kw):
        reads = _bufs(in_)
        writes = _bufs(out)
        oap, iap = _ap(out), _ap(in_)
        ins = self.op(eng, lambda e: e.dma_start(out=oap, in_=iap, **kw), reads, writes)
        ins.is_dma = True
        for b in reads:
            if b.readers.get(eng) is ins:
                del b.readers[eng]
            b.readers[('dma', id(ins))] = ins
        half = self.n_dma // 2
        if eng == 'pool':
            slot = half + self.dma_rr_sw % (self.n_dma - half)
            self.dma_rr_sw += 1
        else:
            slot = self.dma_rr % half
            self.dma_rr += 1
        prev = self.dma_last[slot]
        if prev is not None and prev not in ins.deps:
            ins.deps.append(prev)
        ins.slot = slot
        self.dma_last[slot] = ins
        self.dma_order.append(ins)
        return ins

    def mm(self, out, lhsT, rhs, start=True, stop=True):
        o, l, r = out.ap, lhsT.ap, rhs.ap
        return self.op('pe', lambda e: e.matmul(o, lhsT=l, rhs=r, start=start, stop=stop),
                       reads=_bufs(lhsT, rhs), writes=_bufs(out))

    def transpose(self, out, in_, ident):
        o, i, d = out.ap, in_.ap, ident.ap
        return self.op('pe', lambda e: e.transpose(o, i, d), reads=_bufs(in_, ident), writes=_bufs(out))

    def act(self, out, in_, func, bias=None, scale=None, accum=None):
        o, i = out.ap, in_.ap
        kw = {}
        if bias is not None:
            kw['bias'] = _ap(bias)
        if scale is not None:
            kw['scale'] = _ap(scale)
        if accum is not None:
            kw['accum_out'] = accum.ap
        return self.op('act', lambda e: e.activation(out=o, in_=i, func=func, **kw),
                       reads=_bufs(in_, bias, scale), writes=_bufs(out, accum))

    def copy(self, eng, out, in_):
        o, i = out.ap, in_.ap
        if eng == 'act':
            return self.op('act', lambda e: e.activation(out=o, in_=i, func=AF.Copy),
                           reads=_bufs(in_), writes=_bufs(out))
        return self.op(eng, lambda e: e.tensor_copy(out=o, in_=i), reads=_bufs(in_), writes=_bufs(out))

    def tt(self, eng, out, a, b, op):
        o, x, y = out.ap, a.ap, b.ap
        return self.op(eng, lambda e: e.tensor_tensor(out=o, in0=x, in1=y, op=op),
                       reads=_bufs(a, b), writes=_bufs(out))

    def ts(self, eng, out, a, s1, s2=None, op0=ALU.mult, op1=None, accum=None):
        o, x = out.ap, a.ap
        s1a, s2a = _ap(s1), _ap(s2)
        kw = {}
        if op1 is not None:
            kw['op1'] = op1
        if accum is not None:
            kw['accum_out'] = accum.ap
        return self.op(eng, lambda e: e.tensor_scalar(out=o, in0=x, scalar1=s1a, scalar2=s2a, op0=op0, **kw),
                       reads=_bufs(a, s1, s2), writes=_bufs(out, accum))

    def stt(self, eng, out, a, scalar, b, op0, op1):
        o, x, y = out.ap, a.ap, b.ap
        sa = _ap(scalar)
        return self.op(eng, lambda e: e.scalar_tensor_tensor(out=o, in0=x, scalar=sa, in1=y, op0=op0, op1=op1),
                       reads=_bufs(a, b, scalar), writes=_bufs(out))

    def memset(self, eng, out, val):
        o = out.ap
        return self.op(eng, lambda e: e.memset(o, val), reads=(), writes=_bufs(out))

    def finalize(self):
        nc = self.nc
        for e in ENGS:
            for ins in self.streams[e]:
                for d in ins.deps:
                    d.needed = True
        esem = {}
        for e in ['pe', 'act', 'dve', 'pool']:
            esem[e] = self.es.enter_context(nc.semaphore(f"sem_{e}"))
            c = 0
            for ins in self.streams[e]:
                if not ins.is_dma and ins.needed:
                    c += 1
                    ins.val = c
        dsem = [self.es.enter_context(nc.semaphore(f"sem_dma{i}")) for i in range(self.n_dma)]
        cnt = [0] * self.n_dma
        for ins in self.dma_order:
            cnt[ins.slot] += 16
            ins.val = cnt[ins.slot]
        final_waits = [(dsem[i], cnt[i]) for i in range(self.n_dma) if cnt[i] > 0]

        def emit(e, eng):
            waited = {}
            for ins in self.streams[e]:
                for d in ins.deps:
                    if d.is_dma:
                        key, sem, val = ('d', d.slot), dsem[d.slot], d.val
                    else:
                        key, sem, val = d.eng, esem[d.eng], d.val
                    if waited.get(key, 0) < val:
                        eng.wait_ge(sem, val)
                        waited[key] = val
                r = ins.fn(eng)
                if ins.is_dma:
                    r.then_inc(dsem[ins.slot], 16)
                elif ins.needed:
                    r.then_inc(esem[e], 1)
            if e == 'sp':
                for sem, val in final_waits:
                    eng.wait_ge(sem, val)

        with nc.Block() as block:
            @block.sync
            def _(eng):
                emit('sp', eng)

            @block.tensor
            def _(eng):
                emit('pe', eng)

            @block.scalar
            def _(eng):
                emit('act', eng)

            @block.vector
            def _(eng):
                emit('dve', eng)

            @block.gpsimd
            def _(eng):
                emit('pool', eng)
        self.es.close()


class Builder:
    def __init__(self, nseq=2, depth=DEPTH, mixers=('rwkv', 'lru', 'moba', 'ret'), debug=()):
        self.nseq = nseq
        self.depth = depth
        self.mixers = mixers
        self.debug = debug
        self.nc = bass.Bass("TRN2", target_bir_lowering=False)
        self.P = Prog(self.nc)
        self.dram = {}
        self.build()

    def din(self, name, shape, dtype=F32):
        t = self.nc.dram_tensor(name, list(shape), dtype, kind="ExternalInput")
        tl = Tl(t.ap(), name)
        self.dram[name] = tl
        return tl

    def dout(self, name, shape, dtype=F32, nsub=0):
        t = self.nc.dram_tensor(name, list(shape), dtype, kind="ExternalOutput")
        tl = Tl(t.ap(), name, nsub)
        self.dram[name] = tl
        return tl

    def dscr(self, name, shape, dtype=F32, nsub=0):
        t = self.nc.dram_tensor(name, list(shape), dtype, kind="Internal")
        tl = Tl(t.ap(), name, nsub)
        self.dram[name] = tl
        return tl

    def build(self):
        P = self.P
        L = self.depth
        NS = self.nseq
        x = self.din("x", [NS, T, D])
        w_in = self.din("w_in", [DEPTH, D, INC])
        w_out = self.din("w_out", [DEPTH, D, D])
        ffn_w_up = self.din("ffn_w_up", [DEPTH, D, 2 * DFF])
        ffn_w_down = self.din("ffn_w_down", [DEPTH, DFF, D])
        ln1_g = self.din("ln1_g", [DEPTH, D])
        ln1_b = self.din("ln1_b", [DEPTH, D])
        ln2_g = self.din("ln2_g", [DEPTH, D])
        ln2_b = self.din("ln2_b", [DEPTH, D])
        convp = self.din("ffn_convp", [DEPTH, 128, 44, 4])
        ident_d = self.din("ident", [128, 128])
        self.c_perm = self.din("c_perm", [128, 128])
        self.c_cos = self.din("c_cos", [128, T])
        self.c_sin = self.din("c_sin", [128, T])
        self.c_blockmask = self.din("c_blockmask", [128, 256])
        self.c_notown = self.din("c_notown", [128, 256])
        self.c_esel = self.din("c_esel", [8, 1024], BF16)
        self.c_cm = self.din("c_cm", [128, 512], BF16)
        self.c_retm = self.din("c_retm", [4, 128, 5 * 512], BF16)
        self.lrup_d = self.din("lrup", [DEPTH, 128, 2, 8])
        self.lru_w_r = self.din("lru_w_r", [DEPTH, NH, HD, HD])
        self.lru_w_i = self.din("lru_w_i", [DEPTH, NH, HD, HD])
        self.retp_d = self.din("retp", [DEPTH, 128, 2, 2])
        self.c_ms4 = self.din("c_ms4", [128, 512])
        self.c_mst = self.din("c_mst", [128, 128])
        self.c_blkones = self.din("c_blkones", [128, 128])
        self.rwkvp_d = self.din("rwkvp", [DEPTH, 128, 2, 8])
        self.mulr_d = self.din("mulr", [DEPTH, 128, 1])
        self.rwkv_w_up = self.din("rwkv_w_up", [DEPTH, 32, GW])
        self.rwkv_a_up = self.din("rwkv_a_up", [DEPTH, 32, GW])
        self.rwkv_g_up = self.din("rwkv_g_up", [DEPTH, 64, GW])
        self.rwkv_gn_g = self.din("rwkv_gn_g", [DEPTH, GW])
        self.rwkv_gn_b = self.din("rwkv_gn_b", [DEPTH, GW])
        out = self.dout("out", [NS, T, D], nsub=NS * 16)
        xmid = self.dscr("xmid", [NS, T, D], nsub=NS * 16)
        if 'yT' in self.debug:
            self.dbg_yT = self.dout("dbg_yT", [128, 8, T], BF16)
        xl = self.dscr("xl", [NS, T, D], nsub=NS * 16)
        self.w_in = w_in

        self.xT = P.sb([128, 8, T], BF16, "xT", nsub=4)
        yT_addr = (self.nc.sbuf_base + 31) // 32 * 32
        self.yT = P.sb([128, 8, T], BF16, "yT", nsub=64)
        self.rtpoolR = self.nc.alloc_sbuf_tensor_at("rtpoolR", [128, 24 * 128], F32R, offset=yT_addr + 2 * 4096)
        self.rtpoolF = self.nc.alloc_sbuf_tensor_at("rtpoolF", [128, 24 * 128], F32, offset=yT_addr + 5 * 4096)
        self.ident = P.sb([128, 128], F32, "ident")
        P.dma('sp', self.ident[:], ident_d[:])
        self.lnp = P.sb([128, 2, D], F32, "lnp", nsub=2)
        self.perm = P.sb([128, 128], F32, "perm")
        P.dma('sp', self.perm[:], self.c_perm[:])
        self.esel = P.sb([8, 1024], BF16, "esel")
        P.dma('sp', self.esel[:], self.c_esel[:])
        self.cm = P.sb([128, 512], BF16, "cm")
        P.dma('sp', self.cm[:], self.c_cm[:])
        self.ones_bf = P.sb([128, 64], BF16, "ones_bf")
        P.memset('pool', self.ones_bf[:], 1.0)
        self.ones64 = P.sb([128, 64], F32, "ones64")
        P.memset('pool', self.ones64[:], 1.0 / 64.0)
        self.lrup = P.sb([128, 2, 8], F32, "lrup")
        self.retp = P.sb([128, 2, 2], F32, "retp")
        self.kmean = P.sb([128, 8], F32, "kmean")
        self.pT = [P.sb([128, 512], BF16, f"pT{i}") for i in range(3)]
        self.small = P.sb([128, 8], F32, "small")
        self.wst_rr = 0
        self.rr = 0
        self.ms4 = P.sb([128, 512], F32, "ms4")
        P.dma('sp', self.ms4[:], self.c_ms4[:])
        self.mst = P.sb([128, 128], F32, "mst")
        P.dma('sp', self.mst[:], self.c_mst[:])
        self.blkones = P.sb([128, 128], F32, "blkones")
        P.dma('sp', self.blkones[:], self.c_blkones[:])
        self.rwkvp = P.sb([128, 2, 8], F32, "rwkvp")
        self.mulr = P.sb([128, 1], F32, "mulr")
        self.lrw = P.sb([128, GW], F32, "lrw")
        self.gnb = P.sb([128, 2, GW], F32, "gnb")
        self.gc = P.sb([128, 64], F32, "gc")
        self.rsm = [P.sb([128, 16], F32, f"rsm{i}") for i in range(2)]
        self.convp = P.sb([128, 44, 4], F32, "convp")
        self.halo = P.sb([128, 44, 2], F32, "halo")
        self.xtok = [P.sb([128, D], F32, f"xtok{i}") for i in range(3)]
        self.stat = [P.sb([128, 16], F32, f"stat{i}") for i in range(3)]
        self.S = P.sb([128, SCOLS], F32, "S", nsub=SCOLS // 512)
        self.wst = [P.sb([128, 8, 128], BF16, f"wst{i}") for i in range(6)]
        self.usb = [P.sb([128, 514], F32, f"usb{i}") for i in range(4)]
        self.ysb = [P.sb([128, 512], F32, f"ysb{i}") for i in range(4)]
        self.psA = [P.ps([128, 1024], F32, f"psA{i}", nsub=2) for i in range(1)]
        self.psB = [P.ps([128, 512], F32, f"psB{i}") for i in range(6)]
        self.psb_rr = 0

        for l in range(L):
            pass
        self.sbuf_free = self.nc.sbuf_bytes_remaining
        for s in range(NS):
            for l in range(L):
                self.layer(l, s, x, xmid, xl, out, w_out, ffn_w_up, ffn_w_down,
                           (ln1_g, ln1_b, ln2_g, ln2_b), convp)
        P.finalize()

    def sv(self, c0, n, dt=F32):
        ap = self.S.t[:, c0:c0 + n]
        if dt == BF16:
            ap = ap.bitcast(BF16)
        return self.S.v(ap, sub=list(range(c0 // 512, (c0 + n - 1) // 512 + 1)))

    def wbig_v(self, fc, c0=0, n=1024):
        base = fc * 512
        ap = self.S.t[:, base:base + 512].bitcast(BF16)[:, c0:c0 + n]
        return self.S.v(ap, sub=[base // 512])

    def actT_v(self, fc, c0=0, n=512):
        base = 11264 + fc * 256
        ap = self.S.t[:, base:base + 256].bitcast(BF16)[:, c0:c0 + n]
        return self.S.v(ap, sub=[base // 512])

    def next_psB(self):
        t = self.psB[self.psb_rr % len(self.psB)]
        self.psb_rr += 1
        return t

    def tok_to_xT(self, src_tl, i):
        P = self.P
        for g in range(2):
            ps = self.next_psB()
            for k in range(4):
                kk = g * 4 + k
                P.transpose(ps.v(ps.t[:, k * 128:(k + 1) * 128]),
                            src_tl.v(src_tl.t[:, kk * 128:(kk + 1) * 128]), self.ident[:])
            dst = self.xT.v(self.xT.t[:, g * 4:(g + 1) * 4, i * 128:(i + 1) * 128], sub=i // 4)
            srcv = ps.v(ps.t[:, :].rearrange("p (k t) -> p k t", k=4))
            P.copy('act' if g == 0 else 'dve', dst, srcv)

    def layer_norm_tile(self, ps_tl, res_tl, gidx, out_tl, stat):
        P = self.P
        s = res_tl
        P.stt('dve', s[:], res_tl[:], ALPHA, ps_tl[:], ALU.mult, ALU.add)
        st = stat
        for h in range(2):
            P.op('dve', lambda e, h=h: e.bn_stats(out=st.t[:, h * 6:(h + 1) * 6], in_=s.t[:, h * 512:(h + 1) * 512]),
                 reads=[s.buf], writes=[st.buf])
        P.op('dve', lambda e: e.bn_aggr(out=st.t[:, 12:14], in_=st.t[:, 0:12]), reads=[st.buf], writes=[st.buf])
        P.ts('dve', st.v(st.t[:, 14:15]), st.v(st.t[:, 13:14]), LN_EPS, None, ALU.add)
        P.act(st.v(st.t[:, 14:15]), st.v(st.t[:, 14:15]), AF.Sqrt)
        P.op('dve', lambda e: e.reciprocal(out=st.t[:, 14:15], in_=st.t[:, 14:15]), reads=[st.buf], writes=[st.buf])
        P.stt('dve', st.v(st.t[:, 15:16]), st.v(st.t[:, 12:13]), -1.0, st.v(st.t[:, 14:15]), ALU.mult, ALU.mult)
        P.act(s[:], s[:], AF.Identity, bias=st.v(st.t[:, 15:16]), scale=st.v(st.t[:, 14:15]))
        P.tt('dve', s[:], s[:], self.lnp.v(self.lnp.t[:, gidx, :], sub=gidx), ALU.mult)
        P.tt('dve', out_tl[:], s[:], self.lnp.v(self.lnp.t[:, gidx + 1, :], sub=gidx + 1), ALU.add)

    def layer(self, l, s, x, xmid, xl, out, w_out, ffn_w_up, ffn_w_down, lns, convp_d):
        P = self.P
        L = self.depth
        xin = x if l == 0 else xl
        last = (l == L - 1)
        xout = out if last else xl
        P.dma('sp', self.convp[:], V(convp_d.t[l], [convp_d.buf]))

        if l == 0:
            for i in range(16):
                xt = self.xtok[i % 2]
                P.dma('sp', xt[:], V(xin.t[s, i * 128:(i + 1) * 128, :], [xin.buf]))
                self.tok_to_xT(xt, i)

        self.mixers_phase(l, s)
        if 'yT' in self.debug and l == 0 and s == 0:
            P.dma('sp', self.dbg_yT[:], self.yT.all())

        for j, t in enumerate(lns[0:2]):
            P.dma('sp', self.lnp.v(self.lnp.t[:, j, :], sub=j), V(t.t[l, :].partition_broadcast(128), [t.buf]))
        for k in range(8):
            P.dma('sp', self.stg(k, False), V(w_out.t[l, k * 128:(k + 1) * 128, :], [w_out.buf]))
            P.copy('act', self.wbig_v(k), self.stg(k, False))
        def load_res2(i):
            xt = self.xtok[i % 3]
            src_sub = s * 16 + i
            P.dma('sp', xt[:], V(xin.t[s, i * 128:(i + 1) * 128, :], [xin.subs[src_sub]] if xin.subs else [xin.buf]))

        pend_st2 = []
        load_res2(0)
        for i in range(16):
            ps = self.psA[0]
            for hf in range(2):
                for k in range(8):
                    P.mm(ps.v(ps.t[:, hf * 512:(hf + 1) * 512]),
                         self.yT.v(self.yT.t[:, k, i * 128:(i + 1) * 128], sub=CH(k)),
                         self.wbig_v(k, hf * 512, 512),
                         start=(k == 0), stop=(k == 7))
            if i >= 2:
                self.tok_to_xT(self.xtok[(i - 2) % 3], i - 2)
            if i + 1 < 16:
                load_res2(i + 1)
            if i < 14:
                fcp = 8 + i
                P.dma('sp', self.stg(fcp, False), V(ffn_w_down.t[l, fcp * 128:(fcp + 1) * 128, :], [ffn_w_down.buf]))
                P.copy('act', self.wbig_v(fcp), self.stg(fcp, False))
            xt = self.xtok[i % 3]
            src_sub = s * 16 + i
            self.layer_norm_tile(ps, xt, 0, xt, self.stat[i % 3])
            if i >= 14:
                pend_st2.append((V(xmid.t[s, i * 128:(i + 1) * 128, :], [xmid.subs[src_sub]]), xt))
            else:
                P.dma('sp', V(xmid.t[s, i * 128:(i + 1) * 128, :], [xmid.subs[src_sub]]), xt[:])
        self.tok_to_xT(self.xtok[14 % 3], 14)
        self.tok_to_xT(self.xtok[15 % 3], 15)

        for j, t in enumerate(lns[2:4]):
            P.dma('sp', self.lnp.v(self.lnp.t[:, j, :], sub=j), V(t.t[l, :].partition_broadcast(128), [t.buf]))
        for fc in range(8):
            P.dma('sp', self.stg(fc, False), V(ffn_w_down.t[l, fc * 128:(fc + 1) * 128, :], [ffn_w_down.buf]))
            P.copy('act', self.wbig_v(fc), self.stg(fc, False))
        P.memset('pool', self.halo[:], 0.0)
        wsrc = ffn_w_up.t[l].rearrange("(k p) c -> p k c", p=128)
        seq = [(g, fc) for g in range(4) for fc in range(22)]
        wts = {}

        def issue(idx):
            g, fc = seq[idx]
            for half in range(2):
                j = 2 * idx + half
                wt = self.wst[j % 6]
                c0 = half * DFF + fc * 128
                P.dma('sp', self.stg(j), V(wsrc[:, :, c0:c0 + 128], [ffn_w_up.buf]))
                P.copy('dve' if half == 0 else 'act', wt[:], self.stg(j))
                wts[(idx, half)] = wt

        pend_xt = []
        pend_st = []

        def flush_xt():
            while pend_st:
                dst, xt_p = pend_st.pop(0)
                P.dma('sp', dst, xt_p[:])
            while pend_xt:
                self.tok_to_xT(*pend_xt.pop(0))

        def load_res3(i):
            xt = self.xtok[i % 3]
            P.dma('sp', xt[:], V(xmid.t[s, i * 128:(i + 1) * 128, :], [xmid.subs[s * 16 + i]]))

        issue(0)
        issue(1)
        while pend_st2:
            dst, xt_p = pend_st2.pop(0)
            P.dma('sp', dst, xt_p[:])
        for g in range(4):
            tok = slice(g * 512, (g + 1) * 512)
            for fc in range(22):
                idx = g * 22 + fc
                if idx + 2 < len(seq):
                    issue(idx + 2)
                if fc == 4:
                    flush_xt()
                if fc == 10:
                    load_res3(4 * g)
                ys = []
                for half in range(2):
                    ps = self.next_psB()
                    for k in range(8):
                        wt = wts[(idx, half)]
                        P.mm(ps[:], wt.v(wt.t[:, k, :]),
                             self.xT.v(self.xT.t[:, k, tok], sub=g), start=(k == 0), stop=(k == 7))
                    ch = half * 22 + fc
                    ub = self.usb[(fc * 2 + half) % 4]
                    yb = self.ysb[(fc * 2 + half) % 4]
                    cp = self.convp
                    P.copy('act', ub.v(ub.t[:, 2:514]), ps[:])
                    P.copy('pool', ub.v(ub.t[:, 0:2]), self.halo.v(self.halo.t[:, ch, :]))
                    P.copy('pool', self.halo.v(self.halo.t[:, ch, :]), ub.v(ub.t[:, 512:514]))
                    P.act(yb[:], ps[:], AF.Identity, bias=cp.v(cp.t[:, ch, 3:4]), scale=cp.v(cp.t[:, ch, 2:3]))
                    P.stt('dve', yb[:], ub.v(ub.t[:, 1:513]), cp.v(cp.t[:, ch, 1:2]), yb[:], ALU.mult, ALU.add)
                    P.stt('dve', yb[:], ub.v(ub.t[:, 0:512]), cp.v(cp.t[:, ch, 0:1]), yb[:], ALU.mult, ALU.add)
                    ys.append(yb)
                P.act(ys[0][:], ys[0][:], AF.Gelu_apprx_tanh)
                P.tt('pool', self.actT_v(fc), ys[0][:], ys[1][:], ALU.mult)
            for ii in range(4):
                i = g * 4 + ii
                ps = self.psA[0]
                for hf in range(2):
                    for fc in range(22):
                        P.mm(ps.v(ps.t[:, hf * 512:(hf + 1) * 512]),
                             self.actT_v(fc, ii * 128, 128),
                             self.wbig_v(fc, hf * 512, 512),
                             start=(fc == 0), stop=(fc == 21))
                if ii >= 2:
                    self.tok_to_xT(self.xtok[(i - 2) % 3], i - 2)
                if ii < 3:
                    load_res3(i + 1)
                xt = self.xtok[i % 3]
                sub = s * 16 + i
                self.layer_norm_tile(ps, xt, 0, xt, self.stat[i % 3])
                if ii >= 2:
                    pend_xt.append((xt, i))
                    pend_st.append((V(xout.t[s, i * 128:(i + 1) * 128, :], [xout.subs[sub]]), xt))
                else:
                    P.dma('sp', V(xout.t[s, i * 128:(i + 1) * 128, :], [xout.subs[sub]]), xt[:])
        flush_xt()

    COS_C0 = 0
    SIN_C0 = 2048
    A0 = 4096
    def R(self, i):
        return self.A0 + i * 2048

    def svb(self, c0, off, n):
        lo = c0 + off // 2
        hi = c0 + (off + n + 1) // 2
        ap = self.S.t[:, lo:hi].bitcast(BF16)[:, (off - 2 * (lo - c0)):(off - 2 * (lo - c0)) + n]
        return self.S.v(ap, sub=list(range(lo // 512, (hi - 1) // 512 + 1)))

    def negmT_v(self, off, n):
        c0 = self.R(7)
        ap = self.S.t[0:8, c0:c0 + 2048].bitcast(BF16)[:, off:off + n]
        return self.S.v(ap, sub=list(range((c0 + off // 2) // 512, (c0 + (off + n - 1) // 2) // 512 + 1)))

    def stg(self, j, three_d=True):
        c0 = 16896 + (j % 3) * 1024
        ap = self.S.t[:, c0:c0 + 1024]
        if three_d:
            ap = ap.rearrange("p (k c) -> p k c", k=8)
        return self.S.v(ap, sub=[c0 // 512, c0 // 512 + 1])

    def load_w(self, l, c0, n=128):
        P = self.P
        st = self.xtok[self.wst_rr % 2]
        wt = self.wst[self.wst_rr % 6]
        self.wst_rr += 1
        src = self.w_in.t[l].rearrange("(k p) c -> p k c", p=128)
        stv = st.v(st.t[:, :].rearrange("p (k c) -> p k c", k=8))
        P.dma('sp', stv, V(src[:, :, c0:c0 + 128], [self.w_in.buf]))
        P.copy('pool', wt[:], stv)
        return wt

    def proj_fm(self, l, c0, dst_c0, evac='act', wt=None):
        P = self.P
        if wt is None:
            wt = self.load_w(l, c0, 128)
        for tt in range(4):
            ps = self.next_psB()
            for k in range(8):
                P.mm(ps[:], wt.v(wt.t[:, k, 0:128]), self.xT.v(self.xT.t[:, k, tt * 512:(tt + 1) * 512], sub=tt),
                     start=(k == 0), stop=(k == 7))
            P.copy(evac, self.sv(dst_c0 + tt * 512, 512), ps[:])

    def proj_tm(self, l, c0, dst_c0):
        P = self.P
        wt = self.load_w(l, c0, 128)
        for g in range(4):
            ps = self.next_psB()
            for j in range(4):
                i = g * 4 + j
                for k in range(8):
                    P.mm(ps.v(ps.t[:, j * 128:(j + 1) * 128]),
                         self.xT.v(self.xT.t[:, k, i * 128:(i + 1) * 128], sub=i // 4),
                         wt.v(wt.t[:, k, 0:128]), start=(k == 0), stop=(k == 7))
            P.copy('act', self.svb(dst_c0, g * 512, 512), ps[:])

    def rope(self, src_c0, dst32_c0, dstb_c0):
        P = self.P
        for tt in range(4):
            ps = self.next_psB()
            src = self.sv(src_c0 + tt * 512, 512)
            P.mm(ps[:], self.perm[:], src)
            tmp = self.ysb[self.rr % 4]
            self.rr += 1
            d32 = self.sv(dst32_c0 + tt * 512, 512)
            P.tt('pool', tmp[:], src, self.sv(self.COS_C0 + tt * 512, 512), ALU.mult)
            P.tt('dve', d32, ps[:], self.sv(self.SIN_C0 + tt * 512, 512), ALU.mult)
            P.tt('dve', d32, d32, tmp[:], ALU.add)
            P.copy('act', self.svb(dstb_c0, tt * 512, 512), d32)

    def mixers_phase(self, l, s):
        P = self.P
        if 'zero' in self.mixers or not self.mixers:
            for k in range(8):
                P.memset('pool', self.yT.v(self.yT.t[:, k, :], sub=CH(k)), 0.0)
        if not self.mixers:
            return
        if 'rwkv' in self.mixers:
            self.rwkv(l)
        P.dma('sp', self.lrup[:], V(self.lrup_d.t[l], [self.lrup_d.buf]))
        P.dma('sp', self.retp[:], V(self.retp_d.t[l], [self.retp_d.buf]))
        if 'lru' in self.mixers:
            self.lru_both(l)
        P.dma('sp', self.sv(self.COS_C0, 2048), self.c_cos[:])
        P.dma('sp', self.sv(self.SIN_C0, 2048), self.c_sin[:])
        if 'moba' in self.mixers:
            for c in range(2):
                self.attn_chunk(l, c, 'moba')
        if 'ret' in self.mixers:
            for c in range(2):
                self.attn_chunk(l, c, 'ret')

    def lru_both(self, l):
        P = self.P
        pp = self.lrup
        sm = self.small
        CS = (0, 1)
        base = lambda c, i: c * 10240 + i * 2048
        XB = [base(c, 0) for c in CS]
        GB = [base(c, 1) for c in CS]
        XC = [base(c, 2) for c in CS]
        GR = [base(c, 3) for c in CS]
        GI = [base(c, 4) for c in CS]
        TMP = XB
        par = lambda c, j: pp.v(pp.t[:, c, j:j + 1])
        for c in CS:
            self.proj_fm(l, 896 + c * 128, XB[c])
        for c in CS:
            self.proj_fm(l, 1152 + c * 128, GB[c])
        wt = self.xtok[2]
        wblk = {}
        P.memset('pool', wt.v(wt.t[:, 0:512]), 0.0)
        for c in CS:
            for j, wsrc in enumerate((self.lru_w_r, self.lru_w_i)):
                col = (c * 2 + j) * 128
                wblk[(c, j)] = wt.v(wt.t[:, col:col + 128])
                for h2 in range(2):
                    P.dma('sp', wt.v(wt.t[h2 * 64:(h2 + 1) * 64, col + h2 * 64:col + (h2 + 1) * 64]),
                          V(wsrc.t[l, 2 * c + h2], [wsrc.buf]))
        for c in CS:
            P.act(self.sv(XC[c], T), self.sv(XB[c], T), AF.Identity, bias=par(c, 4), scale=par(c, 3))
        for sh in (1, 2, 3):
            for c in CS:
                P.stt('dve', self.sv(XC[c] + sh, T - sh), self.sv(XB[c], T - sh), par(c, 3 - sh),
                      self.sv(XC[c] + sh, T - sh), ALU.mult, ALU.add)
        smv = lambda c, j: sm.v(sm.t[:, 3 * c + j:3 * c + j + 1])
        for c in CS:
            P.act(smv(c, 0), par(c, 7), AF.Exp, scale=-1.0)
        for c in CS:
            P.ts('dve', smv(c, 0), smv(c, 0), 1.0, None, ALU.add)
        for c in CS:
            P.act(smv(c, 1), smv(c, 0), AF.Ln)
        for c in CS:
            P.ts('dve', smv(c, 2), smv(c, 1), -8.0, None, ALU.mult)
        for tt in range(4):
            for c in CS:
                for j, (dst, bj) in enumerate(((GR, 5), (GI, 6))):
                    ps = self.next_psB()
                    P.mm(ps[:], wblk[(c, j)], self.sv(XC[c] + tt * 512, 512))
                    P.act(self.sv(dst[c] + tt * 512, 512), ps[:], AF.Sigmoid, bias=par(c, bj))
        for c in CS:
            P.act(self.sv(GR[c], T), self.sv(GR[c], T), AF.Exp, scale=smv(c, 2))
        for c in CS:
            P.act(self.sv(TMP[c], T), self.sv(GR[c], T), AF.Square)
        for c in CS:
            P.ts('dve', self.sv(TMP[c], T), self.sv(TMP[c], T), -1.0, 1.0, ALU.mult, ALU.add)
        for c in CS:
            P.act(self.sv(TMP[c], T), self.sv(TMP[c], T), AF.Sqrt)
        for c in CS:
            P.tt('pool', self.sv(GI[c], T), self.sv(GI[c], T), self.sv(XC[c], T), ALU.mult)
        for c in CS:
            P.tt('dve', self.sv(GI[c], T), self.sv(GI[c], T), self.sv(TMP[c], T), ALU.mult)
        for c in CS:
            a_ap, u_ap, h_ap = self.sv(GR[c], T), self.sv(GI[c], T), self.sv(XC[c], T)
            P.op('dve', lambda e, a_ap=a_ap, u_ap=u_ap, h_ap=h_ap: e.tensor_tensor_scan(
                out=h_ap.ap, data0=a_ap.ap, data1=u_ap.ap, initial=0.0, op0=ALU.mult, op1=ALU.add),
                reads=_bufs(a_ap, u_ap), writes=_bufs(h_ap))
        for c in CS:
            P.act(self.sv(GB[c], T), self.sv(GB[c], T), AF.Gelu_apprx_tanh)
        for c in CS:
            P.tt('dve', self.yT.v(self.yT.t[:, 2 + c, :], sub=CH(2 + c)), self.sv(GB[c], T), self.sv(XC[c], T), ALU.mult)

    def attn_chunk(self, l, c, kind):
        P = self.P
        moba = (kind == 'moba')
        base = 1408 if moba else 2176
        ychunk = (4 if moba else 6) + c
        Q32, K32, QR32, KR32 = [self.R(i) for i in range(4)]
        QRB = self.R(4)
        KRB = self.R(4) + 1024
        VB = self.R(5)
        G32 = self.R(6)
        MSK = self.R(7)
        self.proj_fm(l, base + c * 128, Q32)
        self.proj_fm(l, base + 256 + c * 128, K32)
        self.rope(Q32, QR32, QRB)
        self.rope(K32, KR32, KRB)
        self.proj_tm(l, base + 512 + c * 128, VB)
        vb = lambda kt, h2: self.svb(VB, kt * 128 + h2 * 64, 64)
        if moba:
            kv = self.sv(KR32, T)
            km = self.kmean
            P.op('dve', lambda e: e.tensor_reduce(out=km.t[:, :], in_=kv.ap.rearrange("p (n k) -> p n k", k=256),
                                                  axis=AX.X, op=ALU.add), reads=_bufs(kv), writes=[km.buf])
            P.ts('dve', km[:], km[:], 1.0 / 256.0, None, ALU.mult)
            psgs = [self.next_psB(), self.next_psB()]
            for h2 in range(2):
                rows = slice(h2 * 64, (h2 + 1) * 64)
                psg = psgs[h2]
                for qt in range(16):
                    qsl = self.S.v(self.S.t[rows, QR32 + qt * 128:QR32 + (qt + 1) * 128],
                                   sub=[(QR32 + qt * 128) // 512])
                    P.mm(psg.v(psg.t[:, qt * 8:(qt + 1) * 8]), qsl, km.v(km.t[rows, :]))
            def s_tile(c0):
                tl = Tl(self.S.t[:, c0:c0 + 256], f"S@{c0}")
                tl.buf = self.S.subs[c0 // 512]
                return tl
            g = s_tile(self.R(6))
            bmv = self.sv(self.R(6) + 1536, 256)
            nov = self.sv(self.R(6) + 1792, 256)
            P.dma('sp', bmv, self.c_blockmask[:])
            P.dma('sp', nov, self.c_notown[:])
            for h2 in range(2):
                P.tt('dve', g.v(g.t[:, h2 * 128:(h2 + 1) * 128]), psgs[h2].v(psgs[h2].t[:, 0:128]),
                     self.sv(self.R(6) + 1536 + h2 * 128, 128), ALU.add)
            m8 = s_tile(self.R(6) + 512)
            for idx in range(32):
                P.op('dve', lambda e, idx=idx: e.max(out=m8.t[:, idx * 8:(idx + 1) * 8], in_=g.t[:, idx * 8:(idx + 1) * 8]),
                     reads=[g.buf], writes=[m8.buf])
            thr = m8.t[:, :].rearrange("p (i k) -> p i k", k=8)[:, :, 2:3].to_broadcast([128, 32, 8])
            g3 = g.t[:, :].rearrange("p (i k) -> p i k", k=8)
            nm = s_tile(self.R(6) + 1024)
            nm3 = nm.t[:, :].rearrange("p (i k) -> p i k", k=8)
            P.op('dve', lambda e: e.tensor_tensor(out=nm3, in0=g3, in1=thr, op=ALU.is_lt),
                 reads=[g.buf, m8.buf], writes=[nm.buf])
            P.tt('dve', nm[:], nm[:], nov, ALU.mult)
            for gq in range(8):
                ps = self.next_psB()
                for j in range(4):
                    idx = gq * 4 + j
                    P.transpose(ps.v(ps.t[0:8, j * 128:(j + 1) * 128]), nm.v(nm.t[:, idx * 8:(idx + 1) * 8]),
                                self.ident[:])
                P.copy('act', self.negmT_v(gq * 512, 512), ps.v(ps.t[0:8, :]))
        else:
            self.proj_fm(l, 2944 + c * 128, G32)
            P.act(self.sv(G32, T), self.sv(G32, T), AF.Silu)
        num_ps = self.psA[0]
        deferred = []
        MSKh = [MSK, Q32]
        gams = [1.0 - 2.0 ** (-5.0 - (2 * c + h2)) for h2 in range(2)]
        if not moba:
            for h2 in range(2):
                P.dma('sp', self.svb(MSKh[h2], 0, 2560), V(self.c_retm.t[2 * c + h2], [self.c_retm.buf]))
        ptbufs = [(pt.t, pt.buf) for pt in self.pT] + [(ub.t[:, 0:256].bitcast(BF16), ub.buf) for ub in self.usb]
        for qc in range(4):
            nkt = 4 * qc + 4

            def scores(h2, kt):
                rows = slice(h2 * 64, (h2 + 1) * 64)
                n = kt // 2
                c_lo = 0
                if moba and n * 256 > qc * 512:
                    c_lo = 256
                w = 512 - c_lo
                q0 = qc * 512 + c_lo
                ps = self.next_psB()
                ksl = self.S.v(self.S.t[rows, KRB + kt * 64:KRB + (kt + 1) * 64].bitcast(BF16),
                               sub=[(KRB + kt * 64) // 512])
                qsl = self.S.v(self.S.t[rows, QRB + q0 // 2:QRB + (q0 + w) // 2].bitcast(BF16),
                               sub=list(range((QRB + q0 // 2) // 512, (QRB + (q0 + w) // 2 - 1) // 512 + 1)))
                pap, pbuf = ptbufs[self.rr % len(ptbufs)]
                self.rr += 1
                ptv = lambda a, bb: V(pap[:, a:bb], [pbuf])
                if moba:
                    need_mask = (qc >= 2) and (n != 2 * qc + 1)
                    P.mm(ps.v(ps.t[:, 0:w]), ksl, qsl, start=True, stop=not need_mask)
                    if need_mask:
                        P.mm(ps.v(ps.t[:, 0:w]), self.esel.v(self.esel.t[:, n * 128:(n + 1) * 128]),
                             self.negmT_v(h2 * T + q0, w), start=False, stop=True)
                    P.act(ptv(0, w), ps.v(ps.t[:, 0:w]), AF.Exp, scale=0.125)
                    if n == 2 * qc or n == 2 * qc + 1:
                        P.tt('pool', ptv(0, 256), ptv(0, 256),
                             self.cm.v(self.cm.t[:, (kt % 2) * 256:(kt % 2 + 1) * 256]), ALU.mult)
                else:
                    P.mm(ps[:], ksl, qsl)
                    r = kt - 4 * qc
                    if r >= 0:
                        P.tt('dve', ptv(0, 512), ps[:], self.svb(MSKh[h2], r * 512, 512), ALU.mult)
                    else:
                        P.stt('dve', ptv(0, 512), ps[:], float(gams[h2] ** (qc * 512 - kt * 128)),
                              self.svb(MSKh[h2], 4 * 512, 512), ALU.mult, ALU.mult)
                return (h2, kt, ptv, c_lo, w)

            def pv(st):
                h2, kt, ptv, c_lo, w = st
                rows = slice(h2 * 64, (h2 + 1) * 64)
                P.mm(num_ps.v(num_ps.t[rows, c_lo:512]), vb(kt, h2), ptv(0, w),
                     start=(kt == 0), stop=(kt == nkt - 1))
                if moba:
                    P.mm(num_ps.v(num_ps.t[rows, 512 + c_lo:1024]), self.ones_bf[:], ptv(0, w),
                         start=(kt == 0), stop=(kt == nkt - 1))

            pend = []
            for kt in range(nkt):
                for h2 in range(2):
                    pend.append(scores(h2, kt))
                while len(pend) > 4:
                    pv(pend.pop(0))
            while pend:
                pv(pend.pop(0))

            for h2 in range(2):
                rows = slice(h2 * 64, (h2 + 1) * 64)
                ydst = self.yT.v(self.yT.t[rows, ychunk, qc * 512:(qc + 1) * 512], sub=CH(ychunk))
                numv = num_ps.v(num_ps.t[rows, 0:512])
                bi = qc % 2
                t0 = self.ysb[2 * bi]
                t1 = self.ysb[2 * bi + 1]
                t0v = t0.v(t0.t[rows, :])
                t1v = t1.v(t1.t[rows, :])
                if moba:
                    P.op('dve', lambda e, t0=t0, rows=rows: e.reciprocal(out=t0.t[rows, :], in_=num_ps.t[rows, 512:1024]),
                         reads=num_ps.subs, writes=[t0.buf])
                    P.tt('dve', ydst, numv, t0v, ALU.mult)
                else:
                    P.copy('act', t0v, numv)
                    P.act(t1v, numv, AF.Square)

                    def headnorm(rows=rows, t0=t0, t1=t1, t0v=t0v, t1v=t1v, ydst=ydst, qc=qc):
                        rp = self.retp
                        ps_m = self.next_psB()
                        ps_q = self.next_psB()
                        o64 = self.ones64.v(self.ones64.t[rows, :])
                        P.mm(ps_m.v(ps_m.t[rows, :]), o64, t0v)
                        P.mm(ps_q.v(ps_q.t[rows, :]), o64, t1v)
                        pm = ps_m.v(ps_m.t[rows, :])
                        pq = ps_q.v(ps_q.t[rows, :])
                        P.act(t1v, pm, AF.Square)
                        P.tt('dve', t1v, pq, t1v, ALU.subtract)
                        P.ts('dve', t1v, t1v, 1e-5, None, ALU.add)
                        P.act(t1v, t1v, AF.Sqrt)
                        P.op('dve', lambda e, t1=t1, rows=rows: e.reciprocal(out=t1.t[rows, :], in_=t1.t[rows, :]),
                             reads=[t1.buf], writes=[t1.buf])
                        P.tt('dve', t0v, t0v, pm, ALU.subtract)
                        P.tt('dve', t0v, t0v, t1v, ALU.mult)
                        P.ts('dve', t0v, t0v, rp.v(rp.t[rows, c, 0:1]), rp.v(rp.t[rows, c, 1:2]), ALU.mult, ALU.add)
                        gsl = self.S.v(self.S.t[rows, G32 + qc * 512:G32 + (qc + 1) * 512], sub=[(G32 + qc * 512) // 512])
                        P.tt('dve', ydst, t0v, gsl, ALU.mult)

                    deferred.append(headnorm)
            while len(deferred) > 2:
                deferred.pop(0)()
        while deferred:
            deferred.pop(0)()

    def rwkv(self, l):
        P = self.P
        S = self.S
        Zc = lambda i: i * 2048

        def Z(i, a=0, n=T):
            return self.sv(Zc(i) + a, n)

        def Zr(i, rows, a, n):
            c0 = Zc(i) + a
            return S.v(S.t[rows, c0:c0 + n], sub=list(range(c0 // 512, (c0 + n - 1) // 512 + 1)))

        def Z3(i):
            return S.t[:, Zc(i):Zc(i) + T].rearrange("p (c k) -> p c k", k=64)

        slot = {'r': 16, 'f': 40}

        class RT:
            def __init__(s2, kind='f'):
                i = slot[kind]
                slot[kind] += 1
                if kind == 'r':
                    assert i < 40
                    s2.tr = self.rtpoolR[:, (i - 16) * 128:(i - 15) * 128]
                    s2.t = s2.tr.bitcast(F32)
                else:
                    assert i < 64
                    s2.t = self.rtpoolF[:, (i - 40) * 128:(i - 39) * 128]
                    s2.tr = None
                s2.b = self.yT.subs[i]

            def v(s2, rows=slice(0, 128), cols=slice(0, 128)):
                return V(s2.t[rows, cols], [s2.b])

            def vr(s2, rows=slice(0, 128), cols=slice(0, 128)):
                return V(s2.t[rows, cols], [s2.b])

        pp = self.rwkvp
        P.dma('sp', pp[:], V(self.rwkvp_d.t[l], [self.rwkvp_d.buf]))
        P.dma('sp', self.mulr[:], V(self.mulr_d.t[l], [self.mulr_d.buf]))
        P.dma('sp', self.lrw.v(self.lrw.t[0:32, :]), V(self.rwkv_w_up.t[l], [self.rwkv_w_up.buf]))
        P.dma('sp', self.lrw.v(self.lrw.t[32:64, :]), V(self.rwkv_a_up.t[l], [self.rwkv_a_up.buf]))
        P.dma('sp', self.lrw.v(self.lrw.t[64:128, :]), V(self.rwkv_g_up.t[l], [self.rwkv_g_up.buf]))
        P.dma('sp', self.gnb.v(self.gnb.t[:, 0, :]), V(self.rwkv_gn_g.t[l, :].partition_broadcast(128), [self.rwkv_gn_g.buf]))
        P.dma('sp', self.gnb.v(self.gnb.t[:, 1, :]), V(self.rwkv_gn_b.t[l, :].partition_broadcast(128), [self.rwkv_gn_b.buf]))

        def shiftmix(i, mu, tmp):
            P.tt('dve', Z(tmp, 1, T - 1), Z(i, 0, T - 1), Z(i, 1, T - 1), ALU.subtract)
            P.ts('dve', Z(tmp, 0, 1), Z(i, 0, 1), -1.0, None, ALU.mult)
            P.stt('dve', Z(i), Z(tmp), mu, Z(i), ALU.mult, ALU.add)

        LR, RR, KK, VV, LW, LL, AA, KAP, EE, RKR = range(10)
        self.proj_fm(l, 768, Zc(LR))
        shiftmix(LR, self.mulr[:], EE)
        P.act(Zr(LR, slice(0, 32), 0, T), Zr(LR, slice(0, 32), 0, T), AF.Tanh)
        P.act(Zr(LR, slice(64, 128), 0, T), Zr(LR, slice(64, 128), 0, T), AF.Sigmoid)

        for c in range(2):
            par = lambda j, c=c: pp.v(pp.t[:, c, j:j + 1])
            self.proj_fm(l, c * 128, Zc(RR))
            self.proj_fm(l, 256 + c * 128, Zc(KK))
            self.proj_fm(l, 512 + c * 128, Zc(VV))
            shiftmix(RR, par(0), EE)
            shiftmix(KK, par(1), EE)
            shiftmix(VV, par(2), EE)
            for tt in range(4):
                psw = self.next_psB()
                P.mm(psw[:], self.lrw.v(self.lrw.t[0:32, c * 128:(c + 1) * 128]), Zr(LR, slice(0, 32), tt * 512, 512))
                P.act(Z(LW, tt * 512, 512), psw[:], AF.Sigmoid, bias=par(3))
                psa = self.next_psB()
                P.mm(psa[:], self.lrw.v(self.lrw.t[32:64, c * 128:(c + 1) * 128]), Zr(LR, slice(32, 64), tt * 512, 512))
                P.act(Z(AA, tt * 512, 512), psa[:], AF.Sigmoid, bias=par(4))
            P.act(Z(LW), Z(LW), AF.Copy, scale=-0.6065306597126334)
            P.ts('dve', Z(KAP), Z(KK), par(5), None, ALU.mult)
            P.act(Z(EE), Z(KAP), AF.Square)
            for tt in range(4):
                ps = self.next_psB()
                P.mm(ps[:], self.blkones[:], Z(EE, tt * 512, 512))
                P.act(Z(EE, tt * 512, 512), ps[:], AF.Sqrt)
            P.ts('dve', Z(EE), Z(EE), 1e-12, None, ALU.max)
            ee = Z(EE)
            P.op('dve', lambda e, ee=ee: e.reciprocal(out=ee.ap, in_=ee.ap), reads=_bufs(ee), writes=_bufs(ee))
            P.tt('dve', Z(KAP), Z(KAP), Z(EE), ALU.mult)
            sm = self.rsm[c]
            P.ts('dve', sm.v(sm.t[:, 0:1]), par(6), -1.0, 1.0, ALU.mult, ALU.add)
            P.act(Z(EE), Z(AA), AF.Identity, bias=sm.v(sm.t[:, 0:1]), scale=par(6))
            P.tt('dve', Z(KK), Z(KK), Z(EE), ALU.mult)
            P.tt('dve', Z(AA), Z(KAP), Z(AA), ALU.mult)
            P.stt('dve', Z(RKR), Z(RR), par(7), Z(KK), ALU.mult, ALU.mult)
            P.memset('pool', Z(EE), 1.0)
            e3 = S.v(Z3(EE)[:, :, 0:1], sub=list(range(Zc(EE) // 512, Zc(EE) // 512 + 4)))
            P.memset('pool', e3, 0.0)
            d0, d1, lo = Z(EE), Z(LW), Z(LL)
            P.op('dve', lambda e, d0=d0, d1=d1, lo=lo: e.tensor_tensor_scan(out=lo.ap, data0=d0.ap, data1=d1.ap, initial=0.0,
                                                                           op0=ALU.mult, op1=ALU.add),
                 reads=_bufs(d0, d1), writes=_bufs(lo))
            P.tt('dve', Z(LW), Z(LL), Z(LW), ALU.subtract)
            gc = self.gc
            lend = S.v(Z3(LL)[:, :, 63:64], sub=list(range(Zc(LL) // 512, Zc(LL) // 512 + 4)))
            P.act(gc.v(gc.t[:, 0:32].rearrange("p (c k) -> p c k", k=1)), lend, AF.Exp)
            P.ts('dve', gc.v(gc.t[:, 32:64]), gc.v(gc.t[:, 0:32]), -1.0, None, ALU.mult)
            P.act(Z(EE), Z(LL), AF.Exp)
            P.tt('dve', Z(RR), Z(RR), Z(EE), ALU.mult)
            P.act(Z(EE), Z(LW), AF.Exp)
            P.tt('dve', Z(KAP), Z(KAP), Z(EE), ALU.mult)
            P.act(Z(EE), Z(LL), AF.Exp, scale=-1.0)
            P.tt('dve', Z(AA), Z(AA), Z(EE), ALU.mult)
            P.tt('dve', Z(KK), Z(KK), Z(EE), ALU.mult)
            allsub = lambda i: list(range(Zc(i) // 512, Zc(i) // 512 + 4))
            ngc_b = gc.t[:, 32:64].rearrange("p (c k) -> p c k", k=1).to_broadcast([128, 32, 64])
            gc_b = gc.t[:, 0:32].rearrange("p (c k) -> p c k", k=1).to_broadcast([128, 32, 64])
            P.tt('dve', S.v(Z3(LW), sub=allsub(LW)), S.v(Z3(AA), sub=allsub(AA)), V(ngc_b, [gc.buf]), ALU.mult)
            P.tt('dve', S.v(Z3(EE), sub=allsub(EE)), S.v(Z3(KK), sub=allsub(KK)), V(gc_b, [gc.buf]), ALU.mult)
            NB, KB = LW, EE

            slot['f'] = 40
            H = [[RT(), RT()] for _ in range(2)]
            for h2 in range(2):
                rows = slice(h2 * 64, (h2 + 1) * 64)
                P.memset('pool', H[h2][0].v(rows, slice(0, 64)), 0.0)
            pending_out = []

            def flush_out():
                while pending_out:
                    yn_p, t0_p = pending_out.pop(0)
                    pso = self.next_psB()
                    P.transpose(pso.v(pso.t[:, 0:128]), yn_p.v(), self.ident[:])
                    P.copy('act', self.yT.v(self.yT.t[:, c, t0_p:t0_p + 128], sub=CH(c)), pso.v(pso.t[:, 0:128]))

            for cp in range(16):
                t0 = cp * 128
                slot['r'] = 16
                slot['f'] = 44
                pst = self.next_psB()
                for j, reg in enumerate((KAP, NB, KB, VV)):
                    P.transpose(pst.v(pst.t[:, j * 128:(j + 1) * 128], sub=j), Z(reg, t0, 128), self.ident[:])
                TMa, TMb = RT(), RT()
                TMc, TMd = RT(), RT('r')
                tms = (TMa, TMb, TMc, TMd)
                for j in range(4):
                    P.copy('act' if j % 2 == 0 else 'dve', tms[j].vr() if j == 3 else tms[j].v(),
                           pst.v(pst.t[:, j * 128:(j + 1) * 128], sub=j))
                KAPt, NBt, KBt, Vt = tms
                st = []
                for h2 in range(2):
                    rows = slice(h2 * 64, (h2 + 1) * 64)
                    d = {'rows': rows, 'h2': h2}
                    d.update(psg=self.next_psB(), psa=self.next_psB(),
                             bh=Zr(AA, rows, t0, 128), kh=Zr(KK, rows, t0, 128),
                             kap=Zr(KAP, rows, t0, 128), rh=Zr(RR, rows, t0, 128))
                    st.append(d)
                for j, (la, ra) in enumerate((('bh', 'kap'), ('bh', 'rh'), ('kh', 'kap'), ('kh', 'rh'))):
                    for d in st:
                        psg = d['psg']
                        P.mm(psg.v(psg.t[:, j * 128:(j + 1) * 128], sub=j), d[la], d[ra])
                for d in st:
                    psa = d['psa']
                    P.mm(psa.v(psa.t[:, 0:128], sub=0), d['kap'], d['bh'])
                flush_out()
                for d in st:
                    psg, psa = d['psg'], d['psa']
                    G4 = [RT('r') for _ in range(4)]
                    for j in range(4):
                        P.tt('dve', G4[j].vr(), psg.v(psg.t[:, j * 128:(j + 1) * 128], sub=j),
                             self.ms4.v(self.ms4.t[:, j * 128:(j + 1) * 128]), ALU.mult)
                    Asb = RT('r')
                    P.tt('dve', Asb.vr(), psa.v(psa.t[:, 0:128], sub=0), self.mst[:], ALU.mult)
                    NTt, NAb, BT, ArT = G4
                    W = RT('r')
                    d.update(W=W, Np=NTt, Ap=Asb, NAb=NAb, ArT=ArT, BT=BT,
                             Wr=[RT('r'), W], Nr=[RT('r'), RT('r')], Ar=[RT('r'), RT('r')])
                for d in st:
                    hs = d['rows']
                    psa = d['psa']
                    P.mm(psa.v(psa.t[:, 128:192], sub=1), d['BT'].vr(), Vt.vr(cols=hs))
                for d in st:
                    hs = d['rows']
                    psa = d['psa']
                    W = d['W']
                    P.copy('act', W.vr(cols=slice(64, 128)), psa.v(psa.t[:, 128:192], sub=1))
                    P.copy('dve', W.vr(cols=slice(0, 64)), KAPt.v(cols=hs))
                for lvl in range(6):
                    for d in st:
                        ps = self.next_psB()
                        d['ps'] = ps
                        P.mm(ps.v(ps.t[:, 0:128], sub=0), d['Np'].vr(), d['W'].vr())
                        if lvl < 5:
                            P.mm(ps.v(ps.t[:, 128:256], sub=1), d['Ap'].vr(), d['Np'].vr())
                        if lvl < 4:
                            P.mm(ps.v(ps.t[:, 256:384], sub=2), d['Np'].vr(), d['Ap'].vr())
                    for d in st:
                        ps = d['ps']
                        Wn = d['Wr'][lvl % 2]
                        P.tt('dve', Wn.vr(), d['W'].v(), ps.v(ps.t[:, 0:128], sub=0),
                             ALU.subtract if lvl == 0 else ALU.add)
                        d['W'] = Wn
                        if lvl < 5:
                            Nn = d['Nr'][lvl % 2]
                            P.copy('act', Nn.vr(), ps.v(ps.t[:, 128:256], sub=1))
                        if lvl < 4:
                            An = d['Ar'][lvl % 2]
                            P.copy('act', An.vr(), ps.v(ps.t[:, 256:384], sub=2))
                            d['Ap'] = An
                        if lvl < 5:
                            d['Np'] = Nn
                ytm = RT()
                for d in st:
                    rows = d['rows']
                    hs = rows
                    W = d['W']
                    ps = self.next_psB()
                    d['ps'] = ps
                    P.mm(ps.v(ps.t[:, 0:64], sub=0), d['ArT'].vr(), Vt.vr(cols=hs), start=True, stop=False)
                    P.mm(ps.v(ps.t[:, 0:64], sub=0), d['NAb'].vr(), W.vr(cols=slice(64, 128)), start=False, stop=True)
                    P.mm(ps.v(ps.t[rows, 128:256], sub=1), W.v(cols=slice(0, 64)), d['NAb'].v())
                    d['ps2'] = []
                    for q in range(2):
                        qs = slice(q * 64, (q + 1) * 64)
                        ps2 = self.next_psB()
                        d['ps2'].append(ps2)
                        P.mm(ps2.v(ps2.t[rows, 0:64], sub=0), W.v(qs, slice(0, 64)), NBt.v(qs, hs))
                        P.mm(ps2.v(ps2.t[rows, 128:192], sub=1), KBt.v(qs, hs), Vt.v(qs, hs), start=True, stop=False)
                        P.mm(ps2.v(ps2.t[rows, 128:192], sub=1), NBt.v(qs, hs), W.v(qs, slice(64, 128)), start=False, stop=True)
                for d in st:
                    rows = d['rows']
                    ps = d['ps']
                    Y0 = RT()
                    P.copy('act', Y0.v(cols=slice(0, 64)), ps.v(ps.t[:, 0:64], sub=0))
                    RH = RT()
                    P.tt('dve', RH.v(rows), Zr(RR, rows, t0, 128), ps.v(ps.t[rows, 128:256], sub=1), ALU.add)
                    d.update(Y0=Y0, RH=RH, GT=[], H0c=[])
                    for q in range(2):
                        ch = 2 * cp + q
                        ps2 = d['ps2'][q]
                        GT = RT()
                        P.stt('dve', GT.v(rows, slice(0, 64)), self.ident.v(self.ident.t[rows, rows]),
                              self.gc.v(self.gc.t[rows, ch:ch + 1]), ps2.v(ps2.t[rows, 0:64], sub=0), ALU.mult, ALU.add)
                        H0c = RT()
                        P.copy('act', H0c.v(rows, slice(0, 64)), ps2.v(ps2.t[rows, 128:192], sub=1))
                        d['GT'].append(GT)
                        d['H0c'].append(H0c)
                for q in range(2):
                    qs = slice(q * 64, (q + 1) * 64)
                    ch = 2 * cp + q
                    for d in st:
                        rows = d['rows']
                        hs = rows
                        h2 = d['h2']
                        ps3 = self.next_psB()
                        d['ps3'] = ps3
                        Hc = H[h2][ch % 2]
                        P.mm(ps3.v(ps3.t[qs, h2 * 64:(h2 + 1) * 64], sub=0),
                             d['RH'].v(rows, qs), Hc.v(rows, slice(0, 64)))
                        P.mm(ps3.v(ps3.t[rows, 128:192], sub=1), d['GT'][q].v(rows, slice(0, 64)), Hc.v(rows, slice(0, 64)))
                    for d in st:
                        rows = d['rows']
                        hs = rows
                        h2 = d['h2']
                        ps3 = d['ps3']
                        Hn = H[h2][(ch + 1) % 2]
                        P.tt('dve', Hn.v(rows, slice(0, 64)), d['H0c'][q].v(rows, slice(0, 64)),
                             ps3.v(ps3.t[rows, 128:192], sub=1), ALU.add)
                        P.tt('dve', ytm.v(qs, hs), d['Y0'].v(qs, slice(0, 64)),
                             ps3.v(ps3.t[qs, h2 * 64:(h2 + 1) * 64], sub=0), ALU.add)
                sm = self.rsm[cp % 2]
                psx = self.next_psB()
                psy = self.next_psB()
                for h2 in range(2):
                    rows = slice(h2 * 64, (h2 + 1) * 64)
                    hs = rows
                    P.op('dve', lambda e, ytm=ytm, hs=hs, h2=h2, sm=sm: e.bn_stats(out=sm.t[:, h2 * 6:(h2 + 1) * 6], in_=ytm.t[:, hs]),
                         reads=[ytm.b], writes=[sm.buf])
                    P.op('dve', lambda e, h2=h2, sm=sm: e.bn_aggr(out=sm.t[:, 12 + 2 * h2:14 + 2 * h2], in_=sm.t[:, h2 * 6:(h2 + 1) * 6]),
                         reads=[sm.buf], writes=[sm.buf])
                    pb = psx if h2 == 0 else psy
                    P.mm(pb.v(pb.t[:, 0:1]), Zr(RKR, rows, t0, 128), self.ones64.v(self.ones64.t[rows, 0:1]))
                var2 = sm.t[:, 12:16].rearrange("p (h k) -> p h k", k=2)[:, :, 1:2]
                rs2 = sm.t[:, 8:10].rearrange("p (h k) -> p h k", k=1)
                P.op('dve', lambda e, var2=var2, rs2=rs2: e.tensor_scalar(out=rs2, in0=var2, scalar1=64e-5, scalar2=None, op0=ALU.add),
                     reads=[sm.buf], writes=[sm.buf])
                P.act(sm.v(sm.t[:, 8:10]), sm.v(sm.t[:, 8:10]), AF.Sqrt)
                P.op('dve', lambda e, sm=sm: e.reciprocal(out=sm.t[:, 8:10], in_=sm.t[:, 8:10]), reads=[sm.buf], writes=[sm.buf])
                P.ts('dve', sm.v(sm.t[:, 10:11]), psx.v(psx.t[:, 0:1]), 64.0, None, ALU.mult)
                P.ts('dve', sm.v(sm.t[:, 11:12]), psy.v(psy.t[:, 0:1]), 64.0, None, ALU.mult)
                yn = RT()
                for h2 in range(2):
                    hs = slice(h2 * 64, (h2 + 1) * 64)
                    P.ts('dve', yn.v(cols=hs), ytm.v(cols=hs), sm.v(sm.t[:, 12 + 2 * h2:13 + 2 * h2]),
                         sm.v(sm.t[:, 8 + h2:9 + h2]), ALU.subtract, ALU.mult)
                gsl = slice(c * 128, (c + 1) * 128)
                P.tt('pool', yn.v(), yn.v(), self.gnb.v(self.gnb.t[:, 0, gsl]), ALU.mult)
                P.tt('pool', yn.v(), yn.v(), self.gnb.v(self.gnb.t[:, 1, gsl]), ALU.add)
                for h2 in range(2):
                    hs = slice(h2 * 64, (h2 + 1) * 64)
                    P.stt('dve', yn.v(cols=hs), Vt.v(cols=hs), sm.v(sm.t[:, 10 + h2:11 + h2]), yn.v(cols=hs), ALU.mult, ALU.add)
                P.mm(psx.v(psx.t[:, 128:256], sub=1), Zr(LR, slice(64, 128), t0, 128),
                     self.lrw.v(self.lrw.t[64:128, c * 128:(c + 1) * 128]))
                P.tt('dve', yn.v(), yn.v(), psx.v(psx.t[:, 128:256], sub=1), ALU.mult)
                pending_out.append((yn, t0))
            flush_out()


_CACHE = {}


def host_consts():
    c = {}
    c["ident"] = np.eye(128, dtype=np.float32)
    perm = np.zeros((128, 128), np.float32)
    for m in range(128):
        d = m % 64
        if d < 32:
            perm[m + 32, m] = -1.0
        else:
            perm[m - 32, m] = 1.0
    c["c_perm"] = perm
    inv = (np.float32(10000.0) ** (-np.arange(0, 64, 2, dtype=np.float32) / np.float32(64))).astype(np.float32)
    ang = (np.arange(T, dtype=np.float32)[:, None] * inv[None, :]).astype(np.float32)
    cos = np.cos(ang.astype(np.float64)).astype(np.float32).T
    sin = np.sin(ang.astype(np.float64)).astype(np.float32).T
    c["c_cos"] = np.ascontiguousarray(np.tile(cos, (4, 1)))
    c["c_sin"] = np.ascontiguousarray(np.tile(sin, (4, 1)))
    bm = np.zeros((128, 32, 8), np.float32)
    no = np.zeros((128, 32, 8), np.float32)
    for idx in range(32):
        qt = idx % 16
        blk = qt // 2
        for n in range(8):
            bm[:, idx, n] = 0.0 if n < blk else -1e30
            no[:, idx, n] = 0.0 if n == blk else -30000.0
    c["c_blockmask"] = bm.reshape(128, 256)
    c["c_notown"] = no.reshape(128, 256)
    es = np.zeros((8, 8, 128), np.float32)
    for n in range(8):
        es[n, n, :] = 1.0
    c["c_esel"] = np.ascontiguousarray(es.transpose(1, 0, 2).reshape(8, 1024)).astype(ml_dtypes.bfloat16)
    p = np.arange(128)[:, None]
    cc = np.arange(256)[None, :]
    cm = np.stack([(r * 128 + p <= cc).astype(np.float32) for r in range(2)], axis=1)
    c["c_cm"] = np.ascontiguousarray(cm.reshape(128, 512)).astype(ml_dtypes.bfloat16)
    retm = np.zeros((4, 128, 5, 512), np.float64)
    col = np.arange(512)[None, :].astype(np.float64)
    pr = np.arange(128)[:, None].astype(np.float64)
    for h in range(4):
        gam = 1.0 - 2.0 ** (-5.0 - h)
        for r in range(4):
            dd = col - pr - 128.0 * r
            retm[h, :, r, :] = np.where(dd >= 0, 0.125 * gam ** np.maximum(dd, 0.0), 0.0)
        retm[h, :, 4, :] = 0.125 * gam ** (col - pr)
    c["c_retm"] = np.ascontiguousarray(retm.reshape(4, 128, 2560).astype(np.float32)).astype(ml_dtypes.bfloat16)
    si = np.arange(128)[:, None]
    ti = np.arange(128)[None, :]
    same = (si // 64) == (ti // 64)
    ms = (same & (si < ti)).astype(np.float32)
    mi = (same & (si <= ti)).astype(np.float32)
    c["c_ms4"] = np.ascontiguousarray(np.concatenate([ms, -mi, ms, mi], axis=1))
    c["c_mst"] = np.ascontiguousarray(ms.T)
    c["c_blkones"] = same.astype(np.float32)
    return c


def relayout_params(inp):
    o = {}
    cw = np.asarray(inp["ffn_conv_w"], np.float32)
    cb = np.asarray(inp["ffn_conv_b"], np.float32)
    cat = np.concatenate([cw, cb[:, None, :]], axis=1)
    o["ffn_convp"] = np.ascontiguousarray(cat.reshape(DEPTH, 4, 44, 128).transpose(0, 3, 2, 1))
    f = lambda k: np.asarray(inp[k], np.float32)
    lr = np.concatenate([f("lru_conv_w"), f("lru_conv_b")[:, None], f("lru_b_r")[:, None], f("lru_b_i")[:, None],
                         f("lru_lambda")[:, None]], axis=1)
    o["lrup"] = np.ascontiguousarray(lr.reshape(DEPTH, 8, 2, 128).transpose(0, 3, 2, 1))
    mu = f("tshift_mu")
    z = np.zeros_like(f("rwkv_w0"))
    rk = f("rwkv_r_k").reshape(DEPTH, 256)
    rw = np.stack([mu[:, 0:256], mu[:, 256:512], mu[:, 512:768], f("rwkv_w0"), f("rwkv_a0"), f("rwkv_k_k"),
                   f("rwkv_k_a"), rk], axis=1)
    o["rwkvp"] = np.ascontiguousarray(rw.reshape(DEPTH, 8, 2, 128).transpose(0, 3, 2, 1))
    o["mulr"] = np.ascontiguousarray(mu[:, 768:896].reshape(DEPTH, 128, 1))
    rp = np.stack([f("ret_gn_g"), f("ret_gn_b")], axis=1)
    o["retp"] = np.ascontiguousarray(rp.reshape(DEPTH, 2, 2, 128).transpose(0, 3, 2, 1))
    return o


def kernel(**inputs):
    if "b" not in _CACHE:
        _CACHE["b"] = Builder()
    b = _CACHE["b"]
    consts = host_consts()
    rel = relayout_params(inputs)
    x = np.ascontiguousarray(np.asarray(inputs["x"], np.float32))
    shared = {}
    for name in b.dram:
        if name in ("x", "out", "xmid", "xl"):
            continue
        if name in consts:
            shared[name] = consts[name]
        elif name in rel:
            shared[name] = rel[name]
        else:
            shared[name] = np.ascontiguousarray(np.asarray(inputs[name], np.float32))
    in_maps = []
    for c in range(NCORES):
        m = dict(shared)
        m["x"] = x[2 * c:2 * c + 2]
        in_maps.append(m)
    res = run_bass_kernel_spmd(b.nc, in_maps, core_ids=list(range(NCORES)))
    outs = [np.asarray(r["out"]) for r in res.results]
    return np.concatenate(outs, axis=0).astype(np.float32)
```

```python
import numpy as np
import ml_dtypes
import concourse.bass as bass
import concourse.mybir as mybir
from concourse.bass_utils import run_bass_kernel_spmd
from contextlib import ExitStack

F32 = mybir.dt.float32
BF16 = mybir.dt.bfloat16
F32R = mybir.dt.float32r
AF = mybir.ActivationFunctionType
ALU = mybir.AluOpType
AX = mybir.AxisListType

D = 1024
T = 2048
NH = 4
HD = 64
GW = 256
DFF = 2816
INC = 3200
DEPTH = 2
NCORES = 8
ALPHA = (2.0 * DEPTH) ** 0.25
LN_EPS = 1e-5
SCOLS = 20480


def CH(k):
    return list(range(8 * k, 8 * k + 8))

ENGS = ['pe', 'act', 'dve', 'pool', 'sp']


class Buf:
    __slots__ = ('name', 'last_w', 'readers', 'excl')

    def __init__(self, name=''):
        self.name = name
        self.last_w = None
        self.readers = {}
        self.excl = False


class Ins:
    __slots__ = ('eng', 'fn', 'deps', 'is_dma', 'slot', 'val', 'needed')

    def __init__(self, eng, fn):
        self.eng = eng
        self.fn = fn
        self.deps = []
        self.is_dma = False
        self.slot = -1
        self.val = 0
        self.needed = False


class V:
    __slots__ = ('ap', 'bufs')

    def __init__(self, ap, bufs):
        self.ap = ap
        self.bufs = bufs


class Tl:
    def __init__(self, t, name='', nsub=0, hier=False):
        self.t = t
        self.buf = Buf(name)
        self.subs = [Buf(f"{name}.{i}") for i in range(nsub)]
        self.hier = hier

    def __getitem__(self, key):
        return V(self.t[key], self.subs if self.hier else [self.buf])

    def v(self, ap, sub=None):
        if sub is None:
            return V(ap, self.subs if self.hier else [self.buf])
        if not self.subs:
            return V(ap, [self.buf])
        if isinstance(sub, int):
            return V(ap, [self.subs[sub]])
        return V(ap, [self.subs[i] for i in sub])

    def all(self):
        return V(self.t[:], [self.buf] + self.subs)


def _bufs(*vs):
    out = []
    for v in vs:
        if isinstance(v, V):
            out.extend(v.bufs)
    return out


def _ap(v):
    return v.ap if isinstance(v, V) else v


class Prog:
    def __init__(self, nc, n_dma_sems=24):
        self.nc = nc
        self.streams = {e: [] for e in ENGS}
        self.n_dma = n_dma_sems
        self.dma_rr = 0
        self.dma_rr_sw = 0
        self.dma_last = [None] * n_dma_sems
        self.dma_order = []
        self.es = ExitStack()
        self.nbuf = 0

    def sb(self, shape, dtype, name=None, nsub=0):
        self.nbuf += 1
        name = "s_" + (name or f"sb{self.nbuf}")
        t = self.es.enter_context(self.nc.sbuf_tensor(name, list(shape), dtype))
        return Tl(t, name, nsub)

    def ps(self, shape, dtype, name=None, nsub=0):
        self.nbuf += 1
        name = "p_" + (name or f"ps{self.nbuf}")
        t = self.es.enter_context(self.nc.psum_tensor(name, list(shape), dtype))
        tl = Tl(t, name, nsub, hier=(nsub > 0))
        tl.buf.excl = True
        for b in tl.subs:
            b.excl = True
        return tl

    def op(self, eng, fn, reads=(), writes=()):
        ins = Ins(eng, fn)
        deps = []
        if any(b.excl for b in reads):
            writes = list(writes) + [b for b in reads if b.excl]
            reads = [b for b in reads if not b.excl]
        for b in reads:
            if b.last_w is not None:
                deps.append(b.last_w)
        for b in writes:
            if b.last_w is not None:
                deps.append(b.last_w)
            deps.extend(b.readers.values())
        seen = set()
        out = []
        for d in deps:
            if d is ins or id(d) in seen:
                continue
            seen.add(id(d))
            if eng == 'pe' and d.eng == 'pe' and not d.is_dma:
                continue
            out.append(d)
        ins.deps = out
        for b in writes:
            b.last_w = ins
            b.readers = {}
        for b in reads:
            if b.last_w is ins:
                continue
            b.readers[eng] = ins
        self.streams[eng].append(ins)
        return ins

    def dma(self, eng, out, in_, **kw):
        reads = _bufs(in_)
        writes = _bufs(out)
        oap, iap = _ap(out), _ap(in_)
        ins = self.op(eng, lambda e: e.dma_start(out=oap, in_=iap, **kw), reads, writes)
        ins.is_dma = True
        for b in reads:
            if b.readers.get(eng) is ins:
                del b.readers[eng]
            b.readers[('dma', id(ins))] = ins
        half = self.n_dma // 2
        if eng == 'pool':
            slot = half + self.dma_rr_sw % (self.n_dma - half)
            self.dma_rr_sw += 1
        else:
            slot = self.dma_rr % half
            self.dma_rr += 1
        prev = self.dma_last[slot]
        if prev is not None and prev not in ins.deps:
            ins.deps.append(prev)
        ins.slot = slot
        self.dma_last[slot] = ins
        self.dma_order.append(ins)
        return ins

    def mm(self, out, lhsT, rhs, start=True, stop=True):
        o, l, r = out.ap, lhsT.ap, rhs.ap
        return self.op('pe', lambda e: e.matmul(o, lhsT=l, rhs=r, start=start, stop=stop),
                       reads=_bufs(lhsT, rhs), writes=_bufs(out))

    def transpose(self, out, in_, ident):
        o, i, d = out.ap, in_.ap, ident.ap
        return self.op('pe', lambda e: e.transpose(o, i, d), reads=_bufs(in_, ident), writes=_bufs(out))

    def act(self, out, in_, func, bias=None, scale=None, accum=None):
        o, i = out.ap, in_.ap
        kw = {}
        if bias is not None:
            kw['bias'] = _ap(bias)
        if scale is not None:
            kw['scale'] = _ap(scale)
        if accum is not None:
            kw['accum_out'] = accum.ap
        return self.op('act', lambda e: e.activation(out=o, in_=i, func=func, **kw),
                       reads=_bufs(in_, bias, scale), writes=_bufs(out, accum))

    def copy(self, eng, out, in_):
        o, i = out.ap, in_.ap
        if eng == 'act':
            return self.op('act', lambda e: e.activation(out=o, in_=i, func=AF.Copy),
                           reads=_bufs(in_), writes=_bufs(out))
        return self.op(eng, lambda e: e.tensor_copy(out=o, in_=i), reads=_bufs(in_), writes=_bufs(out))

    def tt(self, eng, out, a, b, op):
        o, x, y = out.ap, a.ap, b.ap
        return self.op(eng, lambda e: e.tensor_tensor(out=o, in0=x, in1=y, op=op),
                       reads=_bufs(a, b), writes=_bufs(out))

    def ts(self, eng, out, a, s1, s2=None, op0=ALU.mult, op1=None, accum=None):
        o, x = out.ap, a.ap
        s1a, s2a = _ap(s1), _ap(s2)
        kw = {}
        if op1 is not None:
            kw['op1'] = op1
        if accum is not None:
            kw['accum_out'] = accum.ap
        return self.op(eng, lambda e: e.tensor_scalar(out=o, in0=x, scalar1=s1a, scalar2=s2a, op0=op0, **kw),
                       reads=_bufs(a, s1, s2), writes=_bufs(out, accum))

    def stt(self, eng, out, a, scalar, b, op0, op1):
        o, x, y = out.ap, a.ap, b.ap
        sa = _ap(scalar)
        return self.op(eng, lambda e: e.scalar_tensor_tensor(out=o, in0=x, scalar=sa, in1=y, op0=op0, op1=op1),
                       reads=_bufs(a, b, scalar), writes=_bufs(out))

    def memset(self, eng, out, val):
        o = out.ap
        return self.op(eng, lambda e: e.memset(o, val), reads=(), writes=_bufs(out))

    def finalize(self):
        nc = self.nc
        for e in ENGS:
            for ins in self.streams[e]:
                for d in ins.deps:
                    d.needed = True
        esem = {}
        for e in ['pe', 'act', 'dve', 'pool']:
            esem[e] = self.es.enter_context(nc.semaphore(f"sem_{e}"))
            c = 0
            for ins in self.streams[e]:
                if not ins.is_dma and ins.needed:
                    c += 1
                    ins.val = c
        dsem = [self.es.enter_context(nc.semaphore(f"sem_dma{i}")) for i in range(self.n_dma)]
        cnt = [0] * self.n_dma
        for ins in self.dma_order:
            cnt[ins.slot] += 16
            ins.val = cnt[ins.slot]
        final_waits = [(dsem[i], cnt[i]) for i in range(self.n_dma) if cnt[i] > 0]

        def emit(e, eng):
            waited = {}
            for ins in self.streams[e]:
                for d in ins.deps:
                    if d.is_dma:
                        key, sem, val = ('d', d.slot), dsem[d.slot], d.val
                    else:
                        key, sem, val = d.eng, esem[d.eng], d.val
                    if waited.get(key, 0) < val:
                        eng.wait_ge(sem, val)
                        waited[key] = val
                r = ins.fn(eng)
                if ins.is_dma:
                    r.then_inc(dsem[ins.slot], 16)
                elif ins.needed:
                    r.then_inc(esem[e], 1)
            if e == 'sp':
                for sem, val in final_waits:
                    eng.wait_ge(sem, val)

        with nc.Block() as block:
            @block.sync
            def _(eng):
                emit('sp', eng)

            @block.tensor
            def _(eng):
                emit('pe', eng)

            @block.scalar
            def _(eng):
                emit('act', eng)

            @block.vector
            def _(eng):
                emit('dve', eng)

            @block.gpsimd
            def _(eng):
                emit('pool', eng)
        self.es.close()


class Builder:
    def __init__(self, nseq=2, depth=DEPTH, mixers=('rwkv', 'lru', 'moba', 'ret'), debug=()):
        self.nseq = nseq
        self.depth = depth
        self.mixers = mixers
        self.debug = debug
        self.nc = bass.Bass("TRN2", target_bir_lowering=False)
        self.P = Prog(self.nc)
        self.dram = {}
        self.build()

    def din(self, name, shape, dtype=F32):
        t = self.nc.dram_tensor(name, list(shape), dtype, kind="ExternalInput")
        tl = Tl(t.ap(), name)
        self.dram[name] = tl
        return tl

    def dout(self, name, shape, dtype=F32, nsub=0):
        t = self.nc.dram_tensor(name, list(shape), dtype, kind="ExternalOutput")
        tl = Tl(t.ap(), name, nsub)
        self.dram[name] = tl
        return tl

    def dscr(self, name, shape, dtype=F32, nsub=0):
        t = self.nc.dram_tensor(name, list(shape), dtype, kind="Internal")
        tl = Tl(t.ap(), name, nsub)
        self.dram[name] = tl
        return tl

    def build(self):
        P = self.P
        L = self.depth
        NS = self.nseq
        x = self.din("x", [NS, T, D])
        w_in = self.din("w_in", [DEPTH, D, INC])
        w_out = self.din("w_out", [DEPTH, D, D])
        ffn_w_up = self.din("ffn_w_up", [DEPTH, D, 2 * DFF])
        ffn_w_down = self.din("ffn_w_down", [DEPTH, DFF, D])
        ln1_g = self.din("ln1_g", [DEPTH, D])
        ln1_b = self.din("ln1_b", [DEPTH, D])
        ln2_g = self.din("ln2_g", [DEPTH, D])
        ln2_b = self.din("ln2_b", [DEPTH, D])
        convp = self.din("ffn_convp", [DEPTH, 128, 44, 4])
        ident_d = self.din("ident", [128, 128])
        self.c_perm = self.din("c_perm", [128, 128])
        self.c_cos = self.din("c_cos", [128, T])
        self.c_sin = self.din("c_sin", [128, T])
        self.c_blockmask = self.din("c_blockmask", [128, 256])
        self.c_notown = self.din("c_notown", [128, 256])
        self.c_esel = self.din("c_esel", [8, 1024], BF16)
        self.c_cm = self.din("c_cm", [128, 512], BF16)
        self.c_retm = self.din("c_retm", [4, 128, 5 * 512], BF16)
        self.lrup_d = self.din("lrup", [DEPTH, 128, 2, 8])
        self.lru_w_r = self.din("lru_w_r", [DEPTH, NH, HD, HD])
        self.lru_w_i = self.din("lru_w_i", [DEPTH, NH, HD, HD])
        self.retp_d = self.din("retp", [DEPTH, 128, 2, 2])
        self.c_ms4 = self.din("c_ms4", [128, 512])
        self.c_mst = self.din("c_mst", [128, 128])
        self.c_blkones = self.din("c_blkones", [128, 128])
        self.rwkvp_d = self.din("rwkvp", [DEPTH, 128, 2, 8])
        self.mulr_d = self.din("mulr", [DEPTH, 128, 1])
        self.rwkv_w_up = self.din("rwkv_w_up", [DEPTH, 32, GW])
        self.rwkv_a_up = self.din("rwkv_a_up", [DEPTH, 32, GW])
        self.rwkv_g_up = self.din("rwkv_g_up", [DEPTH, 64, GW])
        self.rwkv_gn_g = self.din("rwkv_gn_g", [DEPTH, GW])
        self.rwkv_gn_b = self.din("rwkv_gn_b", [DEPTH, GW])
        out = self.dout("out", [NS, T, D], nsub=NS * 16)
        xmid = self.dscr("xmid", [NS, T, D], nsub=NS * 16)
        if 'yT' in self.debug:
            self.dbg_yT = self.dout("dbg_yT", [128, 8, T], BF16)
        xl = self.dscr("xl", [NS, T, D], nsub=NS * 16)
        self.w_in = w_in

        self.xT = P.sb([128, 8, T], BF16, "xT", nsub=4)
        yT_addr = (self.nc.sbuf_base + 31) // 32 * 32
        self.yT = P.sb([128, 8, T], BF16, "yT", nsub=64)
        self.rtpoolR = self.nc.alloc_sbuf_tensor_at("rtpoolR", [128, 24 * 128], F32R, offset=yT_addr + 2 * 4096)
        self.rtpoolF = self.nc.alloc_sbuf_tensor_at("rtpoolF", [128, 24 * 128], F32, offset=yT_addr + 5 * 4096)
        self.ident = P.sb([128, 128], F32, "ident")
        P.dma('sp', self.ident[:], ident_d[:])
        self.lnp = P.sb([128, 2, D], F32, "lnp", nsub=2)
        self.perm = P.sb([128, 128], F32, "perm")
        P.dma('sp', self.perm[:], self.c_perm[:])
        self.esel = P.sb([8, 1024], BF16, "esel")
        P.dma('sp', self.esel[:], self.c_esel[:])
        self.cm = P.sb([128, 512], BF16, "cm")
        P.dma('sp', self.cm[:], self.c_cm[:])
        self.ones_bf = P.sb([128, 64], BF16, "ones_bf")
        P.memset('pool', self.ones_bf[:], 1.0)
        self.ones64 = P.sb([128, 64], F32, "ones64")
        P.memset('pool', self.ones64[:], 1.0 / 64.0)
        self.lrup = P.sb([128, 2, 8], F32, "lrup")
        self.retp = P.sb([128, 2, 2], F32, "retp")
        self.kmean = P.sb([128, 8], F32, "kmean")
        self.pT = [P.sb([128, 512], BF16, f"pT{i}") for i in range(3)]
        self.small = P.sb([128, 8], F32, "small")
        self.wst_rr = 0
        self.rr = 0
        self.ms4 = P.sb([128, 512], F32, "ms4")
        P.dma('sp', self.ms4[:], self.c_ms4[:])
        self.mst = P.sb([128, 128], F32, "mst")
        P.dma('sp', self.mst[:], self.c_mst[:])
        self.blkones = P.sb([128, 128], F32, "blkones")
        P.dma('sp', self.blkones[:], self.c_blkones[:])
        self.rwkvp = P.sb([128, 2, 8], F32, "rwkvp")
        self.mulr = P.sb([128, 1], F32, "mulr")
        self.lrw = P.sb([128, GW], F32, "lrw")
        self.gnb = P.sb([128, 2, GW], F32, "gnb")
        self.gc = P.sb([128, 64], F32, "gc")
        self.rsm = [P.sb([128, 16], F32, f"rsm{i}") for i in range(2)]
        self.convp = P.sb([128, 44, 4], F32, "convp")
        self.halo = P.sb([128, 44, 2], F32, "halo")
        self.xtok = [P.sb([128, D], F32, f"xtok{i}") for i in range(3)]
        self.stat = [P.sb([128, 16], F32, f"stat{i}") for i in range(3)]
        self.S = P.sb([128, SCOLS], F32, "S", nsub=SCOLS // 512)
        self.wst = [P.sb([128, 8, 128], BF16, f"wst{i}") for i in range(6)]
        self.usb = [P.sb([128, 514], F32, f"usb{i}") for i in range(4)]
        self.ysb = [P.sb([128, 512], F32, f"ysb{i}") for i in range(4)]
        self.psA = [P.ps([128, 1024], F32, f"psA{i}", nsub=2) for i in range(1)]
        self.psB = [P.ps([128, 512], F32, f"psB{i}") for i in range(6)]
        self.psb_rr = 0

        for l in range(L):
            pass
        self.sbuf_free = self.nc.sbuf_bytes_remaining
        for s in range(NS):
            for l in range(L):
                self.layer(l, s, x, xmid, xl, out, w_out, ffn_w_up, ffn_w_down,
                           (ln1_g, ln1_b, ln2_g, ln2_b), convp)
        P.finalize()

    def sv(self, c0, n, dt=F32):
        ap = self.S.t[:, c0:c0 + n]
        if dt == BF16:
            ap = ap.bitcast(BF16)
        return self.S.v(ap, sub=list(range(c0 // 512, (c0 + n - 1) // 512 + 1)))

    def wbig_v(self, fc, c0=0, n=1024):
        base = fc * 512
        ap = self.S.t[:, base:base + 512].bitcast(BF16)[:, c0:c0 + n]
        return self.S.v(ap, sub=[base // 512])

    def actT_v(self, fc, c0=0, n=512):
        base = 11264 + fc * 256
        ap = self.S.t[:, base:base + 256].bitcast(BF16)[:, c0:c0 + n]
        return self.S.v(ap, sub=[base // 512])

    def next_psB(self):
        t = self.psB[self.psb_rr % len(self.psB)]
        self.psb_rr += 1
        return t

    def tok_to_xT(self, src_tl, i):
        P = self.P
        for g in range(2):
            ps = self.next_psB()
            for k in range(4):
                kk = g * 4 + k
                P.transpose(ps.v(ps.t[:, k * 128:(k + 1) * 128]),
                            src_tl.v(src_tl.t[:, kk * 128:(kk + 1) * 128]), self.ident[:])
            dst = self.xT.v(self.xT.t[:, g * 4:(g + 1) * 4, i * 128:(i + 1) * 128], sub=i // 4)
            srcv = ps.v(ps.t[:, :].rearrange("p (k t) -> p k t", k=4))
            P.copy('act' if g == 0 else 'dve', dst, srcv)

    def layer_norm_tile(self, ps_tl, res_tl, gidx, out_tl, stat):
        P = self.P
        s = res_tl
        P.stt('dve', s[:], res_tl[:], ALPHA, ps_tl[:], ALU.mult, ALU.add)
        st = stat
        for h in range(2):
            P.op('dve', lambda e, h=h: e.bn_stats(out=st.t[:, h * 6:(h + 1) * 6], in_=s.t[:, h * 512:(h + 1) * 512]),
                 reads=[s.buf], writes=[st.buf])
        P.op('dve', lambda e: e.bn_aggr(out=st.t[:, 12:14], in_=st.t[:, 0:12]), reads=[st.buf], writes=[st.buf])
        P.ts('dve', st.v(st.t[:, 14:15]), st.v(st.t[:, 13:14]), LN_EPS, None, ALU.add)
        P.act(st.v(st.t[:, 14:15]), st.v(st.t[:, 14:15]), AF.Sqrt)
        P.op('dve', lambda e: e.reciprocal(out=st.t[:, 14:15], in_=st.t[:, 14:15]), reads=[st.buf], writes=[st.buf])
        P.stt('dve', st.v(st.t[:, 15:16]), st.v(st.t[:, 12:13]), -1.0, st.v(st.t[:, 14:15]), ALU.mult, ALU.mult)
        P.act(s[:], s[:], AF.Identity, bias=st.v(st.t[:, 15:16]), scale=st.v(st.t[:, 14:15]))
        P.tt('dve', s[:], s[:], self.lnp.v(self.lnp.t[:, gidx, :], sub=gidx), ALU.mult)
        P.tt('dve', out_tl[:], s[:], self.lnp.v(self.lnp.t[:, gidx + 1, :], sub=gidx + 1), ALU.add)

    def layer(self, l, s, x, xmid, xl, out, w_out, ffn_w_up, ffn_w_down, lns, convp_d):
        P = self.P
        L = self.depth
        xin = x if l == 0 else xl
        last = (l == L - 1)
        xout = out if last else xl
        P.dma('sp', self.convp[:], V(convp_d.t[l], [convp_d.buf]))

        if l == 0:
            for i in range(16):
                xt = self.xtok[i % 2]
                P.dma('sp', xt[:], V(xin.t[s, i * 128:(i + 1) * 128, :], [xin.buf]))
                self.tok_to_xT(xt, i)

        self.mixers_phase(l, s)
        if 'yT' in self.debug and l == 0 and s == 0:
            P.dma('sp', self.dbg_yT[:], self.yT.all())

        for j, t in enumerate(lns[0:2]):
            P.dma('sp', self.lnp.v(self.lnp.t[:, j, :], sub=j), V(t.t[l, :].partition_broadcast(128), [t.buf]))
        for k in range(8):
            P.dma('sp', self.stg(k, False), V(w_out.t[l, k * 128:(k + 1) * 128, :], [w_out.buf]))
            P.copy('act', self.wbig_v(k), self.stg(k, False))
        def load_res2(i):
            xt = self.xtok[i % 3]
            src_sub = s * 16 + i
            P.dma('sp', xt[:], V(xin.t[s, i * 128:(i + 1) * 128, :], [xin.subs[src_sub]] if xin.subs else [xin.buf]))

        pend_st2 = []
        load_res2(0)
        for i in range(16):
            ps = self.psA[0]
            for hf in range(2):
                for k in range(8):
                    P.mm(ps.v(ps.t[:, hf * 512:(hf + 1) * 512]),
                         self.yT.v(self.yT.t[:, k, i * 128:(i + 1) * 128], sub=CH(k)),
                         self.wbig_v(k, hf * 512, 512),
                         start=(k == 0), stop=(k == 7))
            if i >= 2:
                self.tok_to_xT(self.xtok[(i - 2) % 3], i - 2)
            if i + 1 < 16:
                load_res2(i + 1)
            if i < 14:
                fcp = 8 + i
                P.dma('sp', self.stg(fcp, False), V(ffn_w_down.t[l, fcp * 128:(fcp + 1) * 128, :], [ffn_w_down.buf]))
                P.copy('act', self.wbig_v(fcp), self.stg(fcp, False))
            xt = self.xtok[i % 3]
            src_sub = s * 16 + i
            self.layer_norm_tile(ps, xt, 0, xt, self.stat[i % 3])
            if i >= 14:
                pend_st2.append((V(xmid.t[s, i * 128:(i + 1) * 128, :], [xmid.subs[src_sub]]), xt))
            else:
                P.dma('sp', V(xmid.t[s, i * 128:(i + 1) * 128, :], [xmid.subs[src_sub]]), xt[:])
        self.tok_to_xT(self.xtok[14 % 3], 14)
        self.tok_to_xT(self.xtok[15 % 3], 15)

        for j, t in enumerate(lns[2:4]):
            P.dma('sp', self.lnp.v(self.lnp.t[:, j, :], sub=j), V(t.t[l, :].partition_broadcast(128), [t.buf]))
        for fc in range(8):
            P.dma('sp', self.stg(fc, False), V(ffn_w_down.t[l, fc * 128:(fc + 1) * 128, :], [ffn_w_down.buf]))
            P.copy('act', self.wbig_v(fc), self.stg(fc, False))
        P.memset('pool', self.halo[:], 0.0)
        wsrc = ffn_w_up.t[l].rearrange("(k p) c -> p k c", p=128)
        seq = [(g, fc) for g in range(4) for fc in range(22)]
        wts = {}

        def issue(idx):
            g, fc = seq[idx]
            for half in range(2):
                j = 2 * idx + half
                wt = self.wst[j % 6]
                c0 = half * DFF + fc * 128
                P.dma('sp', self.stg(j), V(wsrc[:, :, c0:c0 + 128], [ffn_w_up.buf]))
                P.copy('dve' if half == 0 else 'act', wt[:], self.stg(j))
                wts[(idx, half)] = wt

        pend_xt = []
        pend_st = []

        def flush_xt():
            while pend_st:
                dst, xt_p = pend_st.pop(0)
                P.dma('sp', dst, xt_p[:])
            while pend_xt:
                self.tok_to_xT(*pend_xt.pop(0))

        def load_res3(i):
            xt = self.xtok[i % 3]
            P.dma('sp', xt[:], V(xmid.t[s, i * 128:(i + 1) * 128, :], [xmid.subs[s * 16 + i]]))

        issue(0)
        issue(1)
        while pend_st2:
            dst, xt_p = pend_st2.pop(0)
            P.dma('sp', dst, xt_p[:])
        for g in range(4):
            tok = slice(g * 512, (g + 1) * 512)
            for fc in range(22):
                idx = g * 22 + fc
                if idx + 2 < len(seq):
                    issue(idx + 2)
                if fc == 4:
                    flush_xt()
                if fc == 10:
                    load_res3(4 * g)
                ys = []
                for half in range(2):
                    ps = self.next_psB()
                    for k in range(8):
                        wt = wts[(idx, half)]
                        P.mm(ps[:], wt.v(wt.t[:, k, :]),
                             self.xT.v(self.xT.t[:, k, tok], sub=g), start=(k == 0), stop=(k == 7))
                    ch = half * 22 + fc
                    ub = self.usb[(fc * 2 + half) % 4]
                    yb = self.ysb[(fc * 2 + half) % 4]
                    cp = self.convp
                    P.copy('act', ub.v(ub.t[:, 2:514]), ps[:])
                    P.copy('pool', ub.v(ub.t[:, 0:2]), self.halo.v(self.halo.t[:, ch, :]))
                    P.copy('pool', self.halo.v(self.halo.t[:, ch, :]), ub.v(ub.t[:, 512:514]))
                    P.act(yb[:], ps[:], AF.Identity, bias=cp.v(cp.t[:, ch, 3:4]), scale=cp.v(cp.t[:, ch, 2:3]))
                    P.stt('dve', yb[:], ub.v(ub.t[:, 1:513]), cp.v(cp.t[:, ch, 1:2]), yb[:], ALU.mult, ALU.add)
                    P.stt('dve', yb[:], ub.v(ub.t[:, 0:512]), cp.v(cp.t[:, ch, 0:1]), yb[:], ALU.mult, ALU.add)
                    ys.append(yb)
                P.act(ys[0][:], ys[0][:], AF.Gelu_apprx_tanh)
                P.tt('pool', self.actT_v(fc), ys[0][:], ys[1][:], ALU.mult)
            for ii in range(4):
                i = g * 4 + ii
                ps = self.psA[0]
                for hf in range(2):
                    for fc in range(22):
                        P.mm(ps.v(ps.t[:, hf * 512:(hf + 1) * 512]),
                             self.actT_v(fc, ii * 128, 128),
                             self.wbig_v(fc, hf * 512, 512),
                             start=(fc == 0), stop=(fc == 21))
                if ii >= 2:
                    self.tok_to_xT(self.xtok[(i - 2) % 3], i - 2)
                if ii < 3:
                    load_res3(i + 1)
                xt = self.xtok[i % 3]
                sub = s * 16 + i
                self.layer_norm_tile(ps, xt, 0, xt, self.stat[i % 3])
                if ii >= 2:
                    pend_xt.append((xt, i))
                    pend_st.append((V(xout.t[s, i * 128:(i + 1) * 128, :], [xout.subs[sub]]), xt))
                else:
                    P.dma('sp', V(xout.t[s, i * 128:(i + 1) * 128, :], [xout.subs[sub]]), xt[:])
        flush_xt()

    COS_C0 = 0
    SIN_C0 = 2048
    A0 = 4096
    def R(self, i):
        return self.A0 + i * 2048

    def svb(self, c0, off, n):
        lo = c0 + off // 2
        hi = c0 + (off + n + 1) // 2
        ap = self.S.t[:, lo:hi].bitcast(BF16)[:, (off - 2 * (lo - c0)):(off - 2 * (lo - c0)) + n]
        return self.S.v(ap, sub=list(range(lo // 512, (hi - 1) // 512 + 1)))

    def negmT_v(self, off, n):
        c0 = self.R(7)
        ap = self.S.t[0:8, c0:c0 + 2048].bitcast(BF16)[:, off:off + n]
        return self.S.v(ap, sub=list(range((c0 + off // 2) // 512, (c0 + (off + n - 1) // 2) // 512 + 1)))

    def stg(self, j, three_d=True):
        c0 = 16896 + (j % 3) * 1024
        ap = self.S.t[:, c0:c0 + 1024]
        if three_d:
            ap = ap.rearrange("p (k c) -> p k c", k=8)
        return self.S.v(ap, sub=[c0 // 512, c0 // 512 + 1])

    def load_w(self, l, c0, n=128):
        P = self.P
        st = self.xtok[self.wst_rr % 2]
        wt = self.wst[self.wst_rr % 6]
        self.wst_rr += 1
        src = self.w_in.t[l].rearrange("(k p) c -> p k c", p=128)
        stv = st.v(st.t[:, :].rearrange("p (k c) -> p k c", k=8))
        P.dma('sp', stv, V(src[:, :, c0:c0 + 128], [self.w_in.buf]))
        P.copy('pool', wt[:], stv)
        return wt

    def proj_fm(self, l, c0, dst_c0, evac='act', wt=None):
        P = self.P
        if wt is None:
            wt = self.load_w(l, c0, 128)
        for tt in range(4):
            ps = self.next_psB()
            for k in range(8):
                P.mm(ps[:], wt.v(wt.t[:, k, 0:128]), self.xT.v(self.xT.t[:, k, tt * 512:(tt + 1) * 512], sub=tt),
                     start=(k == 0), stop=(k == 7))
            P.copy(evac, self.sv(dst_c0 + tt * 512, 512), ps[:])

    def proj_tm(self, l, c0, dst_c0):
        P = self.P
        wt = self.load_w(l, c0, 128)
        for g in range(4):
            ps = self.next_psB()
            for j in range(4):
                i = g * 4 + j
                for k in range(8):
                    P.mm(ps.v(ps.t[:, j * 128:(j + 1) * 128]),
                         self.xT.v(self.xT.t[:, k, i * 128:(i + 1) * 128], sub=i // 4),
                         wt.v(wt.t[:, k, 0:128]), start=(k == 0), stop=(k == 7))
            P.copy('act', self.svb(dst_c0, g * 512, 512), ps[:])

    def rope(self, src_c0, dst32_c0, dstb_c0):
        P = self.P
        for tt in range(4):
            ps = self.next_psB()
            src = self.sv(src_c0 + tt * 512, 512)
            P.mm(ps[:], self.perm[:], src)
            tmp = self.ysb[self.rr % 4]
            self.rr += 1
            d32 = self.sv(dst32_c0 + tt * 512, 512)
            P.tt('pool', tmp[:], src, self.sv(self.COS_C0 + tt * 512, 512), ALU.mult)
            P.tt('dve', d32, ps[:], self.sv(self.SIN_C0 + tt * 512, 512), ALU.mult)
            P.tt('dve', d32, d32, tmp[:], ALU.add)
            P.copy('act', self.svb(dstb_c0, tt * 512, 512), d32)

    def mixers_phase(self, l, s):
        P = self.P
        if 'zero' in self.mixers or not self.mixers:
            for k in range(8):
                P.memset('pool', self.yT.v(self.yT.t[:, k, :], sub=CH(k)), 0.0)
        if not self.mixers:
            return
        if 'rwkv' in self.mixers:
            self.rwkv(l)
        P.dma('sp', self.lrup[:], V(self.lrup_d.t[l], [self.lrup_d.buf]))
        P.dma('sp', self.retp[:], V(self.retp_d.t[l], [self.retp_d.buf]))
        if 'lru' in self.mixers:
            self.lru_both(l)
        P.dma('sp', self.sv(self.COS_C0, 2048), self.c_cos[:])
        P.dma('sp', self.sv(self.SIN_C0, 2048), self.c_sin[:])
        if 'moba' in self.mixers:
            for c in range(2):
                self.attn_chunk(l, c, 'moba')
        if 'ret' in self.mixers:
            for c in range(2):
                self.attn_chunk(l, c, 'ret')

    def lru_both(self, l):
        P = self.P
        pp = self.lrup
        sm = self.small
        CS = (0, 1)
        base = lambda c, i: c * 10240 + i * 2048
        XB = [base(c, 0) for c in CS]
        GB = [base(c, 1) for c in CS]
        XC = [base(c, 2) for c in CS]
        GR = [base(c, 3) for c in CS]
        GI = [base(c, 4) for c in CS]
        TMP = XB
        par = lambda c, j: pp.v(pp.t[:, c, j:j + 1])
        for c in CS:
            self.proj_fm(l, 896 + c * 128, XB[c])
        for c in CS:
            self.proj_fm(l, 1152 + c * 128, GB[c])
        wt = self.xtok[2]
        wblk = {}
        P.memset('pool', wt.v(wt.t[:, 0:512]), 0.0)
        for c in CS:
            for j, wsrc in enumerate((self.lru_w_r, self.lru_w_i)):
                col = (c * 2 + j) * 128
                wblk[(c, j)] = wt.v(wt.t[:, col:col + 128])
                for h2 in range(2):
                    P.dma('sp', wt.v(wt.t[h2 * 64:(h2 + 1) * 64, col + h2 * 64:col + (h2 + 1) * 64]),
                          V(wsrc.t[l, 2 * c + h2], [wsrc.buf]))
        for c in CS:
            P.act(self.sv(XC[c], T), self.sv(XB[c], T), AF.Identity, bias=par(c, 4), scale=par(c, 3))
        for sh in (1, 2, 3):
            for c in CS:
                P.stt('dve', self.sv(XC[c] + sh, T - sh), self.sv(XB[c], T - sh), par(c, 3 - sh),
                      self.sv(XC[c] + sh, T - sh), ALU.mult, ALU.add)
        smv = lambda c, j: sm.v(sm.t[:, 3 * c + j:3 * c + j + 1])
        for c in CS:
            P.act(smv(c, 0), par(c, 7), AF.Exp, scale=-1.0)
        for c in CS:
            P.ts('dve', smv(c, 0), smv(c, 0), 1.0, None, ALU.add)
        for c in CS:
            P.act(smv(c, 1), smv(c, 0), AF.Ln)
        for c in CS:
            P.ts('dve', smv(c, 2), smv(c, 1), -8.0, None, ALU.mult)
        for tt in range(4):
            for c in CS:
                for j, (dst, bj) in enumerate(((GR, 5), (GI, 6))):
                    ps = self.next_psB()
                    P.mm(ps[:], wblk[(c, j)], self.sv(XC[c] + tt * 512, 512))
                    P.act(self.sv(dst[c] + tt * 512, 512), ps[:], AF.Sigmoid, bias=par(c, bj))
        for c in CS:
            P.act(self.sv(GR[c], T), self.sv(GR[c], T), AF.Exp, scale=smv(c, 2))
        for c in CS:
            P.act(self.sv(TMP[c], T), self.sv(GR[c], T), AF.Square)
        for c in CS:
            P.ts('dve', self.sv(TMP[c], T), self.sv(TMP[c], T), -1.0, 1.0, ALU.mult, ALU.add)
        for c in CS:
            P.act(self.sv(TMP[c], T), self.sv(TMP[c], T), AF.Sqrt)
        for c in CS:
            P.tt('pool', self.sv(GI[c], T), self.sv(GI[c], T), self.sv(XC[c], T), ALU.mult)
        for c in CS:
            P.tt('dve', self.sv(GI[c], T), self.sv(GI[c], T), self.sv(TMP[c], T), ALU.mult)
        for c in CS:
            a_ap, u_ap, h_ap = self.sv(GR[c], T), self.sv(GI[c], T), self.sv(XC[c], T)
            P.op('dve', lambda e, a_ap=a_ap, u_ap=u_ap, h_ap=h_ap: e.tensor_tensor_scan(
                out=h_ap.ap, data0=a_ap.ap, data1=u_ap.ap, initial=0.0, op0=ALU.mult, op1=ALU.add),
                reads=_bufs(a_ap, u_ap), writes=_bufs(h_ap))
        for c in CS:
            P.act(self.sv(GB[c], T), self.sv(GB[c], T), AF.Gelu_apprx_tanh)
        for c in CS:
            P.tt('dve', self.yT.v(self.yT.t[:, 2 + c, :], sub=CH(2 + c)), self.sv(GB[c], T), self.sv(XC[c], T), ALU.mult)

    def attn_chunk(self, l, c, kind):
        P = self.P
        moba = (kind == 'moba')
        base = 1408 if moba else 2176
        ychunk = (4 if moba else 6) + c
        Q32, K32, QR32, KR32 = [self.R(i) for i in range(4)]
        QRB = self.R(4)
        KRB = self.R(4) + 1024
        VB = self.R(5)
        G32 = self.R(6)
        MSK = self.R(7)
        self.proj_fm(l, base + c * 128, Q32)
        self.proj_fm(l, base + 256 + c * 128, K32)
        self.rope(Q32, QR32, QRB)
        self.rope(K32, KR32, KRB)
        self.proj_tm(l, base + 512 + c * 128, VB)
        vb = lambda kt, h2: self.svb(VB, kt * 128 + h2 * 64, 64)
        if moba:
            kv = self.sv(KR32, T)
            km = self.kmean
            P.op('dve', lambda e: e.tensor_reduce(out=km.t[:, :], in_=kv.ap.rearrange("p (n k) -> p n k", k=256),
                                                  axis=AX.X, op=ALU.add), reads=_bufs(kv), writes=[km.buf])
            P.ts('dve', km[:], km[:], 1.0 / 256.0, None, ALU.mult)
            psgs = [self.next_psB(), self.next_psB()]
            for h2 in range(2):
                rows = slice(h2 * 64, (h2 + 1) * 64)
                psg = psgs[h2]
                for qt in range(16):
                    qsl = self.S.v(self.S.t[rows, QR32 + qt * 128:QR32 + (qt + 1) * 128],
                                   sub=[(QR32 + qt * 128) // 512])
                    P.mm(psg.v(psg.t[:, qt * 8:(qt + 1) * 8]), qsl, km.v(km.t[rows, :]))
            def s_tile(c0):
                tl = Tl(self.S.t[:, c0:c0 + 256], f"S@{c0}")
                tl.buf = self.S.subs[c0 // 512]
                return tl
            g = s_tile(self.R(6))
            bmv = self.sv(self.R(6) + 1536, 256)
            nov = self.sv(self.R(6) + 1792, 256)
            P.dma('sp', bmv, self.c_blockmask[:])
            P.dma('sp', nov, self.c_notown[:])
            for h2 in range(2):
                P.tt('dve', g.v(g.t[:, h2 * 128:(h2 + 1) * 128]), psgs[h2].v(psgs[h2].t[:, 0:128]),
                     self.sv(self.R(6) + 1536 + h2 * 128, 128), ALU.add)
            m8 = s_tile(self.R(6) + 512)
            for idx in range(32):
                P.op('dve', lambda e, idx=idx: e.max(out=m8.t[:, idx * 8:(idx + 1) * 8], in_=g.t[:, idx * 8:(idx + 1) * 8]),
                     reads=[g.buf], writes=[m8.buf])
            thr = m8.t[:, :].rearrange("p (i k) -> p i k", k=8)[:, :, 2:3].to_broadcast([128, 32, 8])
            g3 = g.t[:, :].rearrange("p (i k) -> p i k", k=8)
            nm = s_tile(self.R(6) + 1024)
            nm3 = nm.t[:, :].rearrange("p (i k) -> p i k", k=8)
            P.op('dve', lambda e: e.tensor_tensor(out=nm3, in0=g3, in1=thr, op=ALU.is_lt),
                 reads=[g.buf, m8.buf], writes=[nm.buf])
            P.tt('dve', nm[:], nm[:], nov, ALU.mult)
            for gq in range(8):
                ps = self.next_psB()
                for j in range(4):
                    idx = gq * 4 + j
                    P.transpose(ps.v(ps.t[0:8, j * 128:(j + 1) * 128]), nm.v(nm.t[:, idx * 8:(idx + 1) * 8]),
                                self.ident[:])
                P.copy('act', self.negmT_v(gq * 512, 512), ps.v(ps.t[0:8, :]))
        else:
            self.proj_fm(l, 2944 + c * 128, G32)
            P.act(self.sv(G32, T), self.sv(G32, T), AF.Silu)
        num_ps = self.psA[0]
        deferred = []
        MSKh = [MSK, Q32]
        gams = [1.0 - 2.0 ** (-5.0 - (2 * c + h2)) for h2 in range(2)]
        if not moba:
            for h2 in range(2):
                P.dma('sp', self.svb(MSKh[h2], 0, 2560), V(self.c_retm.t[2 * c + h2], [self.c_retm.buf]))
        ptbufs = [(pt.t, pt.buf) for pt in self.pT] + [(ub.t[:, 0:256].bitcast(BF16), ub.buf) for ub in self.usb]
        for qc in range(4):
            nkt = 4 * qc + 4

            def scores(h2, kt):
                rows = slice(h2 * 64, (h2 + 1) * 64)
                n = kt // 2
                c_lo = 0
                if moba and n * 256 > qc * 512:
                    c_lo = 256
                w = 512 - c_lo
                q0 = qc * 512 + c_lo
                ps = self.next_psB()
                ksl = self.S.v(self.S.t[rows, KRB + kt * 64:KRB + (kt + 1) * 64].bitcast(BF16),
                               sub=[(KRB + kt * 64) // 512])
                qsl = self.S.v(self.S.t[rows, QRB + q0 // 2:QRB + (q0 + w) // 2].bitcast(BF16),
                               sub=list(range((QRB + q0 // 2) // 512, (QRB + (q0 + w) // 2 - 1) // 512 + 1)))
                pap, pbuf = ptbufs[self.rr % len(ptbufs)]
                self.rr += 1
                ptv = lambda a, bb: V(pap[:, a:bb], [pbuf])
                if moba:
                    need_mask = (qc >= 2) and (n != 2 * qc + 1)
                    P.mm(ps.v(ps.t[:, 0:w]), ksl, qsl, start=True, stop=not need_mask)
                    if need_mask:
                        P.mm(ps.v(ps.t[:, 0:w]), self.esel.v(self.esel.t[:, n * 128:(n + 1) * 128]),
                             self.negmT_v(h2 * T + q0, w), start=False, stop=True)
                    P.act(ptv(0, w), ps.v(ps.t[:, 0:w]), AF.Exp, scale=0.125)
                    if n == 2 * qc or n == 2 * qc + 1:
                        P.tt('pool', ptv(0, 256), ptv(0, 256),
                             self.cm.v(self.cm.t[:, (kt % 2) * 256:(kt % 2 + 1) * 256]), ALU.mult)
                else:
                    P.mm(ps[:], ksl, qsl)
                    r = kt - 4 * qc
                    if r >= 0:
                        P.tt('dve', ptv(0, 512), ps[:], self.svb(MSKh[h2], r * 512, 512), ALU.mult)
                    else:
                        P.stt('dve', ptv(0, 512), ps[:], float(gams[h2] ** (qc * 512 - kt * 128)),
                              self.svb(MSKh[h2], 4 * 512, 512), ALU.mult, ALU.mult)
                return (h2, kt, ptv, c_lo, w)

            def pv(st):
                h2, kt, ptv, c_lo, w = st
                rows = slice(h2 * 64, (h2 + 1) * 64)
                P.mm(num_ps.v(num_ps.t[rows, c_lo:512]), vb(kt, h2), ptv(0, w),
                     start=(kt == 0), stop=(kt == nkt - 1))
                if moba:
                    P.mm(num_ps.v(num_ps.t[rows, 512 + c_lo:1024]), self.ones_bf[:], ptv(0, w),
                         start=(kt == 0), stop=(kt == nkt - 1))

            pend = []
            for kt in range(nkt):
                for h2 in range(2):
                    pend.append(scores(h2, kt))
                while len(pend) > 4:
                    pv(pend.pop(0))
            while pend:
                pv(pend.pop(0))

            for h2 in range(2):
                rows = slice(h2 * 64, (h2 + 1) * 64)
                ydst = self.yT.v(self.yT.t[rows, ychunk, qc * 512:(qc + 1) * 512], sub=CH(ychunk))
                numv = num_ps.v(num_ps.t[rows, 0:512])
                bi = qc % 2
                t0 = self.ysb[2 * bi]
                t1 = self.ysb[2 * bi + 1]
                t0v = t0.v(t0.t[rows, :])
                t1v = t1.v(t1.t[rows, :])
                if moba:
                    P.op('dve', lambda e, t0=t0, rows=rows: e.reciprocal(out=t0.t[rows, :], in_=num_ps.t[rows, 512:1024]),
                         reads=num_ps.subs, writes=[t0.buf])
                    P.tt('dve', ydst, numv, t0v, ALU.mult)
                else:
                    P.copy('act', t0v, numv)
                    P.act(t1v, numv, AF.Square)

                    def headnorm(rows=rows, t0=t0, t1=t1, t0v=t0v, t1v=t1v, ydst=ydst, qc=qc):
                        rp = self.retp
                        ps_m = self.next_psB()
                        ps_q = self.next_psB()
                        o64 = self.ones64.v(self.ones64.t[rows, :])
                        P.mm(ps_m.v(ps_m.t[rows, :]), o64, t0v)
                        P.mm(ps_q.v(ps_q.t[rows, :]), o64, t1v)
                        pm = ps_m.v(ps_m.t[rows, :])
                        pq = ps_q.v(ps_q.t[rows, :])
                        P.act(t1v, pm, AF.Square)
                        P.tt('dve', t1v, pq, t1v, ALU.subtract)
                        P.ts('dve', t1v, t1v, 1e-5, None, ALU.add)
                        P.act(t1v, t1v, AF.Sqrt)
                        P.op('dve', lambda e, t1=t1, rows=rows: e.reciprocal(out=t1.t[rows, :], in_=t1.t[rows, :]),
                             reads=[t1.buf], writes=[t1.buf])
                        P.tt('dve', t0v, t0v, pm, ALU.subtract)
                        P.tt('dve', t0v, t0v, t1v, ALU.mult)
                        P.ts('dve', t0v, t0v, rp.v(rp.t[rows, c, 0:1]), rp.v(rp.t[rows, c, 1:2]), ALU.mult, ALU.add)
                        gsl = self.S.v(self.S.t[rows, G32 + qc * 512:G32 + (qc + 1) * 512], sub=[(G32 + qc * 512) // 512])
                        P.tt('dve', ydst, t0v, gsl, ALU.mult)

                    deferred.append(headnorm)
            while len(deferred) > 2:
                deferred.pop(0)()
        while deferred:
            deferred.pop(0)()

    def rwkv(self, l):
        P = self.P
        S = self.S
        Zc = lambda i: i * 2048

        def Z(i, a=0, n=T):
            return self.sv(Zc(i) + a, n)

        def Zr(i, rows, a, n):
            c0 = Zc(i) + a
            return S.v(S.t[rows, c0:c0 + n], sub=list(range(c0 // 512, (c0 + n - 1) // 512 + 1)))

        def Z3(i):
            return S.t[:, Zc(i):Zc(i) + T].rearrange("p (c k) -> p c k", k=64)

        slot = {'r': 16, 'f': 40}

        class RT:
            def __init__(s2, kind='f'):
                i = slot[kind]
                slot[kind] += 1
                if kind == 'r':
                    assert i < 40
                    s2.tr = self.rtpoolR[:, (i - 16) * 128:(i - 15) * 128]
                    s2.t = s2.tr.bitcast(F32)
                else:
                    assert i < 64
                    s2.t = self.rtpoolF[:, (i - 40) * 128:(i - 39) * 128]
                    s2.tr = None
                s2.b = self.yT.subs[i]

            def v(s2, rows=slice(0, 128), cols=slice(0, 128)):
                return V(s2.t[rows, cols], [s2.b])

            def vr(s2, rows=slice(0, 128), cols=slice(0, 128)):
                return V(s2.t[rows, cols], [s2.b])

        pp = self.rwkvp
        P.dma('sp', pp[:], V(self.rwkvp_d.t[l], [self.rwkvp_d.buf]))
        P.dma('sp', self.mulr[:], V(self.mulr_d.t[l], [self.mulr_d.buf]))
        P.dma('sp', self.lrw.v(self.lrw.t[0:32, :]), V(self.rwkv_w_up.t[l], [self.rwkv_w_up.buf]))
        P.dma('sp', self.lrw.v(self.lrw.t[32:64, :]), V(self.rwkv_a_up.t[l], [self.rwkv_a_up.buf]))
        P.dma('sp', self.lrw.v(self.lrw.t[64:128, :]), V(self.rwkv_g_up.t[l], [self.rwkv_g_up.buf]))
        P.dma('sp', self.gnb.v(self.gnb.t[:, 0, :]), V(self.rwkv_gn_g.t[l, :].partition_broadcast(128), [self.rwkv_gn_g.buf]))
        P.dma('sp', self.gnb.v(self.gnb.t[:, 1, :]), V(self.rwkv_gn_b.t[l, :].partition_broadcast(128), [self.rwkv_gn_b.buf]))

        def shiftmix(i, mu, tmp):
            P.tt('dve', Z(tmp, 1, T - 1), Z(i, 0, T - 1), Z(i, 1, T - 1), ALU.subtract)
            P.ts('dve', Z(tmp, 0, 1), Z(i, 0, 1), -1.0, None, ALU.mult)
            P.stt('dve', Z(i), Z(tmp), mu, Z(i), ALU.mult, ALU.add)

        LR, RR, KK, VV, LW, LL, AA, KAP, EE, RKR = range(10)
        self.proj_fm(l, 768, Zc(LR))
        shiftmix(LR, self.mulr[:], EE)
        P.act(Zr(LR, slice(0, 32), 0, T), Zr(LR, slice(0, 32), 0, T), AF.Tanh)
        P.act(Zr(LR, slice(64, 128), 0, T), Zr(LR, slice(64, 128), 0, T), AF.Sigmoid)

        for c in range(2):
            par = lambda j, c=c: pp.v(pp.t[:, c, j:j + 1])
            self.proj_fm(l, c * 128, Zc(RR))
            self.proj_fm(l, 256 + c * 128, Zc(KK))
            self.proj_fm(l, 512 + c * 128, Zc(VV))
            shiftmix(RR, par(0), EE)
            shiftmix(KK, par(1), EE)
            shiftmix(VV, par(2), EE)
            for tt in range(4):
                psw = self.next_psB()
                P.mm(psw[:], self.lrw.v(self.lrw.t[0:32, c * 128:(c + 1) * 128]), Zr(LR, slice(0, 32), tt * 512, 512))
                P.act(Z(LW, tt * 512, 512), psw[:], AF.Sigmoid, bias=par(3))
                psa = self.next_psB()
                P.mm(psa[:], self.lrw.v(self.lrw.t[32:64, c * 128:(c + 1) * 128]), Zr(LR, slice(32, 64), tt * 512, 512))
                P.act(Z(AA, tt * 512, 512), psa[:], AF.Sigmoid, bias=par(4))
            P.act(Z(LW), Z(LW), AF.Copy, scale=-0.6065306597126334)
            P.ts('dve', Z(KAP), Z(KK), par(5), None, ALU.mult)
            P.act(Z(EE), Z(KAP), AF.Square)
            for tt in range(4):
                ps = self.next_psB()
                P.mm(ps[:], self.blkones[:], Z(EE, tt * 512, 512))
                P.act(Z(EE, tt * 512, 512), ps[:], AF.Sqrt)
            P.ts('dve', Z(EE), Z(EE), 1e-12, None, ALU.max)
            ee = Z(EE)
            P.op('dve', lambda e, ee=ee: e.reciprocal(out=ee.ap, in_=ee.ap), reads=_bufs(ee), writes=_bufs(ee))
            P.tt('dve', Z(KAP), Z(KAP), Z(EE), ALU.mult)
            sm = self.rsm[c]
            P.ts('dve', sm.v(sm.t[:, 0:1]), par(6), -1.0, 1.0, ALU.mult, ALU.add)
            P.act(Z(EE), Z(AA), AF.Identity, bias=sm.v(sm.t[:, 0:1]), scale=par(6))
            P.tt('dve', Z(KK), Z(KK), Z(EE), ALU.mult)
            P.tt('dve', Z(AA), Z(KAP), Z(AA), ALU.mult)
            P.stt('dve', Z(RKR), Z(RR), par(7), Z(KK), ALU.mult, ALU.mult)
            P.memset('pool', Z(EE), 1.0)
            e3 = S.v(Z3(EE)[:, :, 0:1], sub=list(range(Zc(EE) // 512, Zc(EE) // 512 + 4)))
            P.memset('pool', e3, 0.0)
            d0, d1, lo = Z(EE), Z(LW), Z(LL)
            P.op('dve', lambda e, d0=d0, d1=d1, lo=lo: e.tensor_tensor_scan(out=lo.ap, data0=d0.ap, data1=d1.ap, initial=0.0,
                                                                           op0=ALU.mult, op1=ALU.add),
                 reads=_bufs(d0, d1), writes=_bufs(lo))
            P.tt('dve', Z(LW), Z(LL), Z(LW), ALU.subtract)
            gc = self.gc
            lend = S.v(Z3(LL)[:, :, 63:64], sub=list(range(Zc(LL) // 512, Zc(LL) // 512 + 4)))
            P.act(gc.v(gc.t[:, 0:32].rearrange("p (c k) -> p c k", k=1)), lend, AF.Exp)
            P.ts('dve', gc.v(gc.t[:, 32:64]), gc.v(gc.t[:, 0:32]), -1.0, None, ALU.mult)
            P.act(Z(EE), Z(LL), AF.Exp)
            P.tt('dve', Z(RR), Z(RR), Z(EE), ALU.mult)
            P.act(Z(EE), Z(LW), AF.Exp)
            P.tt('dve', Z(KAP), Z(KAP), Z(EE), ALU.mult)
            P.act(Z(EE), Z(LL), AF.Exp, scale=-1.0)
            P.tt('dve', Z(AA), Z(AA), Z(EE), ALU.mult)
            P.tt('dve', Z(KK), Z(KK), Z(EE), ALU.mult)
            allsub = lambda i: list(range(Zc(i) // 512, Zc(i) // 512 + 4))
            ngc_b = gc.t[:, 32:64].rearrange("p (c k) -> p c k", k=1).to_broadcast([128, 32, 64])
            gc_b = gc.t[:, 0:32].rearrange("p (c k) -> p c k", k=1).to_broadcast([128, 32, 64])
            P.tt('dve', S.v(Z3(LW), sub=allsub(LW)), S.v(Z3(AA), sub=allsub(AA)), V(ngc_b, [gc.buf]), ALU.mult)
            P.tt('dve', S.v(Z3(EE), sub=allsub(EE)), S.v(Z3(KK), sub=allsub(KK)), V(gc_b, [gc.buf]), ALU.mult)
            NB, KB = LW, EE

            slot['f'] = 40
            H = [[RT(), RT()] for _ in range(2)]
            for h2 in range(2):
                rows = slice(h2 * 64, (h2 + 1) * 64)
                P.memset('pool', H[h2][0].v(rows, slice(0, 64)), 0.0)
            pending_out = []

            def flush_out():
                while pending_out:
                    yn_p, t0_p = pending_out.pop(0)
                    pso = self.next_psB()
                    P.transpose(pso.v(pso.t[:, 0:128]), yn_p.v(), self.ident[:])
                    P.copy('act', self.yT.v(self.yT.t[:, c, t0_p:t0_p + 128], sub=CH(c)), pso.v(pso.t[:, 0:128]))

            for cp in range(16):
                t0 = cp * 128
                slot['r'] = 16
                slot['f'] = 44
                pst = self.next_psB()
                for j, reg in enumerate((KAP, NB, KB, VV)):
                    P.transpose(pst.v(pst.t[:, j * 128:(j + 1) * 128], sub=j), Z(reg, t0, 128), self.ident[:])
                TMa, TMb = RT(), RT()
                TMc, TMd = RT(), RT('r')
                tms = (TMa, TMb, TMc, TMd)
                for j in range(4):
                    P.copy('act' if j % 2 == 0 else 'dve', tms[j].vr() if j == 3 else tms[j].v(),
                           pst.v(pst.t[:, j * 128:(j + 1) * 128], sub=j))
                KAPt, NBt, KBt, Vt = tms
                st = []
                for h2 in range(2):
                    rows = slice(h2 * 64, (h2 + 1) * 64)
                    d = {'rows': rows, 'h2': h2}
                    d.update(psg=self.next_psB(), psa=self.next_psB(),
                             bh=Zr(AA, rows, t0, 128), kh=Zr(KK, rows, t0, 128),
                             kap=Zr(KAP, rows, t0, 128), rh=Zr(RR, rows, t0, 128))
                    st.append(d)
                for j, (la, ra) in enumerate((('bh', 'kap'), ('bh', 'rh'), ('kh', 'kap'), ('kh', 'rh'))):
                    for d in st:
                        psg = d['psg']
                        P.mm(psg.v(psg.t[:, j * 128:(j + 1) * 128], sub=j), d[la], d[ra])
                for d in st:
                    psa = d['psa']
                    P.mm(psa.v(psa.t[:, 0:128], sub=0), d['kap'], d['bh'])
                flush_out()
                for d in st:
                    psg, psa = d['psg'], d['psa']
                    G4 = [RT('r') for _ in range(4)]
                    for j in range(4):
                        P.tt('dve', G4[j].vr(), psg.v(psg.t[:, j * 128:(j + 1) * 128], sub=j),
                             self.ms4.v(self.ms4.t[:, j * 128:(j + 1) * 128]), ALU.mult)
                    Asb = RT('r')
                    P.tt('dve', Asb.vr(), psa.v(psa.t[:, 0:128], sub=0), self.mst[:], ALU.mult)
                    NTt, NAb, BT, ArT = G4
                    W = RT('r')
                    d.update(W=W, Np=NTt, Ap=Asb, NAb=NAb, ArT=ArT, BT=BT,
                             Wr=[RT('r'), W], Nr=[RT('r'), RT('r')], Ar=[RT('r'), RT('r')])
                for d in st:
                    hs = d['rows']
                    psa = d['psa']
                    P.mm(psa.v(psa.t[:, 128:192], sub=1), d['BT'].vr(), Vt.vr(cols=hs))
                for d in st:
                    hs = d['rows']
                    psa = d['psa']
                    W = d['W']
                    P.copy('act', W.vr(cols=slice(64, 128)), psa.v(psa.t[:, 128:192], sub=1))
                    P.copy('dve', W.vr(cols=slice(0, 64)), KAPt.v(cols=hs))
                for lvl in range(6):
                    for d in st:
                        ps = self.next_psB()
                        d['ps'] = ps
                        P.mm(ps.v(ps.t[:, 0:128], sub=0), d['Np'].vr(), d['W'].vr())
                        if lvl < 5:
                            P.mm(ps.v(ps.t[:, 128:256], sub=1), d['Ap'].vr(), d['Np'].vr())
                        if lvl < 4:
                            P.mm(ps.v(ps.t[:, 256:384], sub=2), d['Np'].vr(), d['Ap'].vr())
                    for d in st:
                        ps = d['ps']
                        Wn = d['Wr'][lvl % 2]
                        P.tt('dve', Wn.vr(), d['W'].v(), ps.v(ps.t[:, 0:128], sub=0),
                             ALU.subtract if lvl == 0 else ALU.add)
                        d['W'] = Wn
                        if lvl < 5:
                            Nn = d['Nr'][lvl % 2]
                            P.copy('act', Nn.vr(), ps.v(ps.t[:, 128:256], sub=1))
                        if lvl < 4:
                            An = d['Ar'][lvl % 2]
                            P.copy('act', An.vr(), ps.v(ps.t[:, 256:384], sub=2))
                            d['Ap'] = An
                        if lvl < 5:
                            d['Np'] = Nn
                ytm = RT()
                for d in st:
                    rows = d['rows']
                    hs = rows
                    W = d['W']
                    ps = self.next_psB()
                    d['ps'] = ps
                    P.mm(ps.v(ps.t[:, 0:64], sub=0), d['ArT'].vr(), Vt.vr(cols=hs), start=True, stop=False)
                    P.mm(ps.v(ps.t[:, 0:64], sub=0), d['NAb'].vr(), W.vr(cols=slice(64, 128)), start=False, stop=True)
                    P.mm(ps.v(ps.t[rows, 128:256], sub=1), W.v(cols=slice(0, 64)), d['NAb'].v())
                    d['ps2'] = []
                    for q in range(2):
                        qs = slice(q * 64, (q + 1) * 64)
                        ps2 = self.next_psB()
                        d['ps2'].append(ps2)
                        P.mm(ps2.v(ps2.t[rows, 0:64], sub=0), W.v(qs, slice(0, 64)), NBt.v(qs, hs))
                        P.mm(ps2.v(ps2.t[rows, 128:192], sub=1), KBt.v(qs, hs), Vt.v(qs, hs), start=True, stop=False)
                        P.mm(ps2.v(ps2.t[rows, 128:192], sub=1), NBt.v(qs, hs), W.v(qs, slice(64, 128)), start=False, stop=True)
                for d in st:
                    rows = d['rows']
                    ps = d['ps']
                    Y0 = RT()
                    P.copy('act', Y0.v(cols=slice(0, 64)), ps.v(ps.t[:, 0:64], sub=0))
                    RH = RT()
                    P.tt('dve', RH.v(rows), Zr(RR, rows, t0, 128), ps.v(ps.t[rows, 128:256], sub=1), ALU.add)
                    d.update(Y0=Y0, RH=RH, GT=[], H0c=[])
                    for q in range(2):
                        ch = 2 * cp + q
                        ps2 = d['ps2'][q]
                        GT = RT()
                        P.stt('dve', GT.v(rows, slice(0, 64)), self.ident.v(self.ident.t[rows, rows]),
                              self.gc.v(self.gc.t[rows, ch:ch + 1]), ps2.v(ps2.t[rows, 0:64], sub=0), ALU.mult, ALU.add)
                        H0c = RT()
                        P.copy('act', H0c.v(rows, slice(0, 64)), ps2.v(ps2.t[rows, 128:192], sub=1))
                        d['GT'].append(GT)
                        d['H0c'].append(H0c)
                for q in range(2):
                    qs = slice(q * 64, (q + 1) * 64)
                    ch = 2 * cp + q
                    for d in st:
                        rows = d['rows']
                        hs = rows
                        h2 = d['h2']
                        ps3 = self.next_psB()
                        d['ps3'] = ps3
                        Hc = H[h2][ch % 2]
                        P.mm(ps3.v(ps3.t[qs, h2 * 64:(h2 + 1) * 64], sub=0),
                             d['RH'].v(rows, qs), Hc.v(rows, slice(0, 64)))
                        P.mm(ps3.v(ps3.t[rows, 128:192], sub=1), d['GT'][q].v(rows, slice(0, 64)), Hc.v(rows, slice(0, 64)))
                    for d in st:
                        rows = d['rows']
                        hs = rows
                        h2 = d['h2']
                        ps3 = d['ps3']
                        Hn = H[h2][(ch + 1) % 2]
                        P.tt('dve', Hn.v(rows, slice(0, 64)), d['H0c'][q].v(rows, slice(0, 64)),
                             ps3.v(ps3.t[rows, 128:192], sub=1), ALU.add)
                        P.tt('dve', ytm.v(qs, hs), d['Y0'].v(qs, slice(0, 64)),
                             ps3.v(ps3.t[qs, h2 * 64:(h2 + 1) * 64], sub=0), ALU.add)
                sm = self.rsm[cp % 2]
                psx = self.next_psB()
                psy = self.next_psB()
                for h2 in range(2):
                    rows = slice(h2 * 64, (h2 + 1) * 64)
                    hs = rows
                    P.op('dve', lambda e, ytm=ytm, hs=hs, h2=h2, sm=sm: e.bn_stats(out=sm.t[:, h2 * 6:(h2 + 1) * 6], in_=ytm.t[:, hs]),
                         reads=[ytm.b], writes=[sm.buf])
                    P.op('dve', lambda e, h2=h2, sm=sm: e.bn_aggr(out=sm.t[:, 12 + 2 * h2:14 + 2 * h2], in_=sm.t[:, h2 * 6:(h2 + 1) * 6]),
                         reads=[sm.buf], writes=[sm.buf])
                    pb = psx if h2 == 0 else psy
                    P.mm(pb.v(pb.t[:, 0:1]), Zr(RKR, rows, t0, 128), self.ones64.v(self.ones64.t[rows, 0:1]))
                var2 = sm.t[:, 12:16].rearrange("p (h k) -> p h k", k=2)[:, :, 1:2]
                rs2 = sm.t[:, 8:10].rearrange("p (h k) -> p h k", k=1)
                P.op('dve', lambda e, var2=var2, rs2=rs2: e.tensor_scalar(out=rs2, in0=var2, scalar1=64e-5, scalar2=None, op0=ALU.add),
                     reads=[sm.buf], writes=[sm.buf])
                P.act(sm.v(sm.t[:, 8:10]), sm.v(sm.t[:, 8:10]), AF.Sqrt)
                P.op('dve', lambda e, sm=sm: e.reciprocal(out=sm.t[:, 8:10], in_=sm.t[:, 8:10]), reads=[sm.buf], writes=[sm.buf])
                P.ts('dve', sm.v(sm.t[:, 10:11]), psx.v(psx.t[:, 0:1]), 64.0, None, ALU.mult)
                P.ts('dve', sm.v(sm.t[:, 11:12]), psy.v(psy.t[:, 0:1]), 64.0, None, ALU.mult)
                yn = RT()
                for h2 in range(2):
                    hs = slice(h2 * 64, (h2 + 1) * 64)
                    P.ts('dve', yn.v(cols=hs), ytm.v(cols=hs), sm.v(sm.t[:, 12 + 2 * h2:13 + 2 * h2]),
                         sm.v(sm.t[:, 8 + h2:9 + h2]), ALU.subtract, ALU.mult)
                gsl = slice(c * 128, (c + 1) * 128)
                P.tt('dve', yn.v(), yn.v(), self.gnb.v(self.gnb.t[:, 0, gsl]), ALU.mult)
                P.tt('dve', yn.v(), yn.v(), self.gnb.v(self.gnb.t[:, 1, gsl]), ALU.add)
                for h2 in range(2):
                    hs = slice(h2 * 64, (h2 + 1) * 64)
                    P.stt('dve', yn.v(cols=hs), Vt.v(cols=hs), sm.v(sm.t[:, 10 + h2:11 + h2]), yn.v(cols=hs), ALU.mult, ALU.add)
                P.mm(psx.v(psx.t[:, 128:256], sub=1), Zr(LR, slice(64, 128), t0, 128),
                     self.lrw.v(self.lrw.t[64:128, c * 128:(c + 1) * 128]))
                P.tt('dve', yn.v(), yn.v(), psx.v(psx.t[:, 128:256], sub=1), ALU.mult)
                pending_out.append((yn, t0))
            flush_out()


_CACHE = {}


def host_consts():
    c = {}
    c["ident"] = np.eye(128, dtype=np.float32)
    perm = np.zeros((128, 128), np.float32)
    for m in range(128):
        d = m % 64
        if d < 32:
            perm[m + 32, m] = -1.0
        else:
            perm[m - 32, m] = 1.0
    c["c_perm"] = perm
    inv = (np.float32(10000.0) ** (-np.arange(0, 64, 2, dtype=np.float32) / np.float32(64))).astype(np.float32)
    ang = (np.arange(T, dtype=np.float32)[:, None] * inv[None, :]).astype(np.float32)
    cos = np.cos(ang.astype(np.float64)).astype(np.float32).T
    sin = np.sin(ang.astype(np.float64)).astype(np.float32).T
    c["c_cos"] = np.ascontiguousarray(np.tile(cos, (4, 1)))
    c["c_sin"] = np.ascontiguousarray(np.tile(sin, (4, 1)))
    bm = np.zeros((128, 32, 8), np.float32)
    no = np.zeros((128, 32, 8), np.float32)
    for idx in range(32):
        qt = idx % 16
        blk = qt // 2
        for n in range(8):
            bm[:, idx, n] = 0.0 if n < blk else -1e30
            no[:, idx, n] = 0.0 if n == blk else -30000.0
    c["c_blockmask"] = bm.reshape(128, 256)
    c["c_notown"] = no.reshape(128, 256)
    es = np.zeros((8, 8, 128), np.float32)
    for n in range(8):
        es[n, n, :] = 1.0
    c["c_esel"] = np.ascontiguousarray(es.transpose(1, 0, 2).reshape(8, 1024)).astype(ml_dtypes.bfloat16)
    p = np.arange(128)[:, None]
    cc = np.arange(256)[None, :]
    cm = np.stack([(r * 128 + p <= cc).astype(np.float32) for r in range(2)], axis=1)
    c["c_cm"] = np.ascontiguousarray(cm.reshape(128, 512)).astype(ml_dtypes.bfloat16)
    retm = np.zeros((4, 128, 5, 512), np.float64)
    col = np.arange(512)[None, :].astype(np.float64)
    pr = np.arange(128)[:, None].astype(np.float64)
    for h in range(4):
        gam = 1.0 - 2.0 ** (-5.0 - h)
        for r in range(4):
            dd = col - pr - 128.0 * r
            retm[h, :, r, :] = np.where(dd >= 0, 0.125 * gam ** np.maximum(dd, 0.0), 0.0)
        retm[h, :, 4, :] = 0.125 * gam ** (col - pr)
    c["c_retm"] = np.ascontiguousarray(retm.reshape(4, 128, 2560).astype(np.float32)).astype(ml_dtypes.bfloat16)
    si = np.arange(128)[:, None]
    ti = np.arange(128)[None, :]
    same = (si // 64) == (ti // 64)
    ms = (same & (si < ti)).astype(np.float32)
    mi = (same & (si <= ti)).astype(np.float32)
    c["c_ms4"] = np.ascontiguousarray(np.concatenate([ms, -mi, ms, mi], axis=1))
    c["c_mst"] = np.ascontiguousarray(ms.T)
    c["c_blkones"] = same.astype(np.float32)
    return c


def relayout_params(inp):
    o = {}
    cw = np.asarray(inp["ffn_conv_w"], np.float32)
    cb = np.asarray(inp["ffn_conv_b"], np.float32)
    cat = np.concatenate([cw, cb[:, None, :]], axis=1)
    o["ffn_convp"] = np.ascontiguousarray(cat.reshape(DEPTH, 4, 44, 128).transpose(0, 3, 2, 1))
    f = lambda k: np.asarray(inp[k], np.float32)
    lr = np.concatenate([f("lru_conv_w"), f("lru_conv_b")[:, None], f("lru_b_r")[:, None], f("lru_b_i")[:, None],
                         f("lru_lambda")[:, None]], axis=1)
    o["lrup"] = np.ascontiguousarray(lr.reshape(DEPTH, 8, 2, 128).transpose(0, 3, 2, 1))
    mu = f("tshift_mu")
    z = np.zeros_like(f("rwkv_w0"))
    rk = f("rwkv_r_k").reshape(DEPTH, 256)
    rw = np.stack([mu[:, 0:256], mu[:, 256:512], mu[:, 512:768], f("rwkv_w0"), f("rwkv_a0"), f("rwkv_k_k"),
                   f("rwkv_k_a"), rk], axis=1)
    o["rwkvp"] = np.ascontiguousarray(rw.reshape(DEPTH, 8, 2, 128).transpose(0, 3, 2, 1))
    o["mulr"] = np.ascontiguousarray(mu[:, 768:896].reshape(DEPTH, 128, 1))
    rp = np.stack([f("ret_gn_g"), f("ret_gn_b")], axis=1)
    o["retp"] = np.ascontiguousarray(rp.reshape(DEPTH, 2, 2, 128).transpose(0, 3, 2, 1))
    return o


def kernel(**inputs):
    if "b" not in _CACHE:
        _CACHE["b"] = Builder()
    b = _CACHE["b"]
    consts = host_consts()
    rel = relayout_params(inputs)
    x = np.ascontiguousarray(np.asarray(inputs["x"], np.float32))
    shared = {}
    for name in b.dram:
        if name in ("x", "out", "xmid", "xl"):
            continue
        if name in consts:
            shared[name] = consts[name]
        elif name in rel:
            shared[name] = rel[name]
        else:
            shared[name] = np.ascontiguousarray(np.asarray(inputs[name], np.float32))
    in_maps = []
    for c in range(NCORES):
        m = dict(shared)
        m["x"] = x[2 * c:2 * c + 2]
        in_maps.append(m)
    res = run_bass_kernel_spmd(b.nc, in_maps, core_ids=list(range(NCORES)))
    outs = [np.asarray(r["out"]) for r in res.results]
    return np.concatenate(outs, axis=0).astype(np.float32)
```

```python
import numpy as np
import ml_dtypes
import concourse.bass as bass
import concourse.mybir as mybir
from concourse.bass_utils import run_bass_kernel_spmd
from contextlib import ExitStack

F32 = mybir.dt.float32
BF16 = mybir.dt.bfloat16
F32R = mybir.dt.float32r
AF = mybir.ActivationFunctionType
ALU = mybir.AluOpType
AX = mybir.AxisListType

D = 1024
T = 2048
NH = 4
HD = 64
GW = 256
DFF = 2816
INC = 3200
DEPTH = 2
NCORES = 8
ALPHA = (2.0 * DEPTH) ** 0.25
LN_EPS = 1e-5
SCOLS = 20480


def CH(k):
    return list(range(8 * k, 8 * k + 8))

ENGS = ['pe', 'act', 'dve', 'pool', 'sp']


class Buf:
    __slots__ = ('name', 'last_w', 'readers', 'excl')

    def __init__(self, name=''):
        self.name = name
        self.last_w = None
        self.readers = {}
        self.excl = False


class Ins:
    __slots__ = ('eng', 'fn', 'deps', 'is_dma', 'slot', 'val', 'needed')

    def __init__(self, eng, fn):
        self.eng = eng
        self.fn = fn
        self.deps = []
        self.is_dma = False
        self.slot = -1
        self.val = 0
        self.needed = False


class V:
    __slots__ = ('ap', 'bufs')

    def __init__(self, ap, bufs):
        self.ap = ap
        self.bufs = bufs


class Tl:
    def __init__(self, t, name='', nsub=0, hier=False):
        self.t = t
        self.buf = Buf(name)
        self.subs = [Buf(f"{name}.{i}") for i in range(nsub)]
        self.hier = hier

    def __getitem__(self, key):
        return V(self.t[key], self.subs if self.hier else [self.buf])

    def v(self, ap, sub=None):
        if sub is None:
            return V(ap, self.subs if self.hier else [self.buf])
        if not self.subs:
            return V(ap, [self.buf])
        if isinstance(sub, int):
            return V(ap, [self.subs[sub]])
        return V(ap, [self.subs[i] for i in sub])

    def all(self):
        return V(self.t[:], [self.buf] + self.subs)


def _bufs(*vs):
    out = []
    for v in vs:
        if isinstance(v, V):
            out.extend(v.bufs)
    return out


def _ap(v):
    return v.ap if isinstance(v, V) else v


class Prog:
    def __init__(self, nc, n_dma_sems=24):
        self.nc = nc
        self.streams = {e: [] for e in ENGS}
        self.n_dma = n_dma_sems
        self.dma_rr = 0
        self.dma_rr_sw = 0
        self.dma_last = [None] * n_dma_sems
        self.dma_order = []
        self.es = ExitStack()
        self.nbuf = 0

    def sb(self, shape, dtype, name=None, nsub=0):
        self.nbuf += 1
        name = "s_" + (name or f"sb{self.nbuf}")
        t = self.es.enter_context(self.nc.sbuf_tensor(name, list(shape), dtype))
        return Tl(t, name, nsub)

    def ps(self, shape, dtype, name=None, nsub=0):
        self.nbuf += 1
        name = "p_" + (name or f"ps{self.nbuf}")
        t = self.es.enter_context(self.nc.psum_tensor(name, list(shape), dtype))
        tl = Tl(t, name, nsub, hier=(nsub > 0))
        tl.buf.excl = True
        for b in tl.subs:
            b.excl = True
        return tl

    def op(self, eng, fn, reads=(), writes=()):
        ins = Ins(eng, fn)
        deps = []
        if any(b.excl for b in reads):
            writes = list(writes) + [b for b in reads if b.excl]
            reads = [b for b in reads if not b.excl]
        for b in reads:
            if b.last_w is not None:
                deps.append(b.last_w)
        for b in writes:
            if b.last_w is not None:
                deps.append(b.last_w)
            deps.extend(b.readers.values())
        seen = set()
        out = []
        for d in deps:
            if d is ins or id(d) in seen:
                continue
            seen.add(id(d))
            if eng == 'pe' and d.eng == 'pe' and not d.is_dma:
                continue
            out.append(d)
        ins.deps = out
        for b in writes:
            b.last_w = ins
            b.readers = {}
        for b in reads:
            if b.last_w is ins:
                continue
            b.readers[eng] = ins
        self.streams[eng].append(ins)
        return ins

    def dma(self, eng, out, in_, **kw):
        reads = _bufs(in_)
        writes = _bufs(out)
        oap, iap = _ap(out), _ap(in_)
        ins = self.op(eng, lambda e: e.dma_start(out=oap, in_=iap, **kw), reads, writes)
        ins.is_dma = True
        for b in reads:
            if b.readers.get(eng) is ins:
                del b.readers[eng]
            b.readers[('dma', id(ins))] = ins
        half = self.n_dma // 2
        if eng == 'pool':
            slot = half + self.dma_rr_sw % (self.n_dma - half)
            self.dma_rr_sw += 1
        else:
            slot = self.dma_rr % half
            self.dma_rr += 1
        prev = self.dma_last[slot]
        if prev is not None and prev not in ins.deps:
            ins.deps.append(prev)
        ins.slot = slot
        self.dma_last[slot] = ins
        self.dma_order.append(ins)
        return ins

    def mm(self, out, lhsT, rhs, start=True, stop=True):
        o, l, r = out.ap, lhsT.ap, rhs.ap
        return self.op('pe', lambda e: e.matmul(o, lhsT=l, rhs=r, start=start, stop=stop),
                       reads=_bufs(lhsT, rhs), writes=_bufs(out))

    def transpose(self, out, in_, ident):
        o, i, d = out.ap, in_.ap, ident.ap
        return self.op('pe', lambda e: e.transpose(o, i, d), reads=_bufs(in_, ident), writes=_bufs(out))

    def act(self, out, in_, func, bias=None, scale=None, accum=None):
        o, i = out.ap, in_.ap
        kw = {}
        if bias is not None:
            kw['bias'] = _ap(bias)
        if scale is not None:
            kw['scale'] = _ap(scale)
        if accum is not None:
            kw['accum_out'] = accum.ap
        return self.op('act', lambda e: e.activation(out=o, in_=i, func=func, **kw),
                       reads=_bufs(in_, bias, scale), writes=_bufs(out, accum))

    def copy(self, eng, out, in_):
        o, i = out.ap, in_.ap
        if eng == 'act':
            return self.op('act', lambda e: e.activation(out=o, in_=i, func=AF.Copy),
                           reads=_bufs(in_), writes=_bufs(out))
        return self.op(eng, lambda e: e.tensor_copy(out=o, in_=i), reads=_bufs(in_), writes=_bufs(out))

    def tt(self, eng, out, a, b, op):
        o, x, y = out.ap, a.ap, b.ap
        return self.op(eng, lambda e: e.tensor_tensor(out=o, in0=x, in1=y, op=op),
                       reads=_bufs(a, b), writes=_bufs(out))

    def ts(self, eng, out, a, s1, s2=None, op0=ALU.mult, op1=None, accum=None):
        o, x = out.ap, a.ap
        s1a, s2a = _ap(s1), _ap(s2)
        kw = {}
        if op1 is not None:
            kw['op1'] = op1
        if accum is not None:
            kw['accum_out'] = accum.ap
        return self.op(eng, lambda e: e.tensor_scalar(out=o, in0=x, scalar1=s1a, scalar2=s2a, op0=op0, **kw),
                       reads=_bufs(a, s1, s2), writes=_bufs(out, accum))

    def stt(self, eng, out, a, scalar, b, op0, op1):
        o, x, y = out.ap, a.ap, b.ap
        sa = _ap(scalar)
        return self.op(eng, lambda e: e.scalar_tensor_tensor(out=o, in0=x, scalar=sa, in1=y, op0=op0, op1=op1),
                       reads=_bufs(a, b, scalar), writes=_bufs(out))

    def memset(self, eng, out, val):
        o = out.ap
        return self.op(eng, lambda e: e.memset(o, val), reads=(), writes=_bufs(out))

    def finalize(self):
        nc = self.nc
        for e in ENGS:
            for ins in self.streams[e]:
                for d in ins.deps:
                    d.needed = True
        esem = {}
        for e in ['pe', 'act', 'dve', 'pool']:
            esem[e] = self.es.enter_context(nc.semaphore(f"sem_{e}"))
            c = 0
            for ins in self.streams[e]:
                if not ins.is_dma and ins.needed:
                    c += 1
                    ins.val = c
        dsem = [self.es.enter_context(nc.semaphore(f"sem_dma{i}")) for i in range(self.n_dma)]
        cnt = [0] * self.n_dma
        for ins in self.dma_order:
            cnt[ins.slot] += 16
            ins.val = cnt[ins.slot]
        final_waits = [(dsem[i], cnt[i]) for i in range(self.n_dma) if cnt[i] > 0]

        def emit(e, eng):
            waited = {}
            for ins in self.streams[e]:
                for d in ins.deps:
                    if d.is_dma:
                        key, sem, val = ('d', d.slot), dsem[d.slot], d.val
                    else:
                        key, sem, val = d.eng, esem[d.eng], d.val
                    if waited.get(key, 0) < val:
                        eng.wait_ge(sem, val)
                        waited[key] = val
                r = ins.fn(eng)
                if ins.is_dma:
                    r.then_inc(dsem[ins.slot], 16)
                elif ins.needed:
                    r.then_inc(esem[e], 1)
            if e == 'sp':
                for sem, val in final_waits:
                    eng.wait_ge(sem, val)

        with nc.Block() as block:
            @block.sync
            def _(eng):
                emit('sp', eng)

            @block.tensor
            def _(eng):
                emit('pe', eng)

            @block.scalar
            def _(eng):
                emit('act', eng)

            @block.vector
            def _(eng):
                emit('dve', eng)

            @block.gpsimd
            def _(eng):
                emit('pool', eng)
        self.es.close()


class Builder:
    def __init__(self, nseq=2, depth=DEPTH, mixers=('rwkv', 'lru', 'moba', 'ret'), debug=()):
        self.nseq = nseq
        self.depth = depth
        self.mixers = mixers
        self.debug = debug
        self.nc = bass.Bass("TRN2", target_bir_lowering=False)
        self.P = Prog(self.nc)
        self.dram = {}
        self.build()

    def din(self, name, shape, dtype=F32):
        t = self.nc.dram_tensor(name, list(shape), dtype, kind="ExternalInput")
        tl = Tl(t.ap(), name)
        self.dram[name] = tl
        return tl

    def dout(self, name, shape, dtype=F32, nsub=0):
        t = self.nc.dram_tensor(name, list(shape), dtype, kind="ExternalOutput")
        tl = Tl(t.ap(), name, nsub)
        self.dram[name] = tl
        return tl

    def dscr(self, name, shape, dtype=F32, nsub=0):
        t = self.nc.dram_tensor(name, list(shape), dtype, kind="Internal")
        tl = Tl(t.ap(), name, nsub)
        self.dram[name] = tl
        return tl

    def build(self):
        P = self.P
        L = self.depth
        NS = self.nseq
        x = self.din("x", [NS, T, D])
        w_in = self.din("w_in", [DEPTH, D, INC])
        w_out = self.din("w_out", [DEPTH, D, D])
        ffn_w_up = self.din("ffn_w_up", [DEPTH, D, 2 * DFF])
        ffn_w_down = self.din("ffn_w_down", [DEPTH, DFF, D])
        ln1_g = self.din("ln1_g", [DEPTH, D])
        ln1_b = self.din("ln1_b", [DEPTH, D])
        ln2_g = self.din("ln2_g", [DEPTH, D])
        ln2_b = self.din("ln2_b", [DEPTH, D])
        convp = self.din("ffn_convp", [DEPTH, 128, 44, 4])
        ident_d = self.din("ident", [128, 128])
        self.c_perm = self.din("c_perm", [128, 128])
        self.c_cos = self.din("c_cos", [128, T])
        self.c_sin = self.din("c_sin", [128, T])
        self.c_blockmask = self.din("c_blockmask", [128, 256])
        self.c_notown = self.din("c_notown", [128, 256])
        self.c_esel = self.din("c_esel", [8, 1024], BF16)
        self.c_cm = self.din("c_cm", [128, 512], BF16)
        self.c_retm = self.din("c_retm", [4, 128, 5 * 512], BF16)
        self.lrup_d = self.din("lrup", [DEPTH, 128, 2, 8])
        self.lru_w_r = self.din("lru_w_r", [DEPTH, NH, HD, HD])
        self.lru_w_i = self.din("lru_w_i", [DEPTH, NH, HD, HD])
        self.retp_d = self.din("retp", [DEPTH, 128, 2, 2])
        self.c_ms4 = self.din("c_ms4", [128, 512])
        self.c_mst = self.din("c_mst", [128, 128])
        self.c_blkones = self.din("c_blkones", [128, 128])
        self.rwkvp_d = self.din("rwkvp", [DEPTH, 128, 2, 8])
        self.mulr_d = self.din("mulr", [DEPTH, 128, 1])
        self.rwkv_w_up = self.din("rwkv_w_up", [DEPTH, 32, GW])
        self.rwkv_a_up = self.din("rwkv_a_up", [DEPTH, 32, GW])
        self.rwkv_g_up = self.din("rwkv_g_up", [DEPTH, 64, GW])
        self.rwkv_gn_g = self.din("rwkv_gn_g", [DEPTH, GW])
        self.rwkv_gn_b = self.din("rwkv_gn_b", [DEPTH, GW])
        out = self.dout("out", [NS, T, D], nsub=NS * 16)
        xmid = self.dscr("xmid", [NS, T, D], nsub=NS * 16)
        if 'yT' in self.debug:
            self.dbg_yT = self.dout("dbg_yT", [128, 8, T], BF16)
        xl = self.dscr("xl", [NS, T, D], nsub=NS * 16)
        self.w_in = w_in

        self.xT = P.sb([128, 8, T], BF16, "xT", nsub=4)
        yT_addr = (self.nc.sbuf_base + 31) // 32 * 32
        self.yT = P.sb([128, 8, T], BF16, "yT", nsub=64)
        self.rtpoolR = self.nc.alloc_sbuf_tensor_at("rtpoolR", [128, 24 * 128], F32R, offset=yT_addr + 2 * 4096)
        self.rtpoolF = self.nc.alloc_sbuf_tensor_at("rtpoolF", [128, 24 * 128], F32, offset=yT_addr + 5 * 4096)
        self.ident = P.sb([128, 128], F32, "ident")
        P.dma('sp', self.ident[:], ident_d[:])
        self.lnp = P.sb([128, 2, D], F32, "lnp", nsub=2)
        self.perm = P.sb([128, 128], F32, "perm")
        P.dma('sp', self.perm[:], self.c_perm[:])
        self.esel = P.sb([8, 1024], BF16, "esel")
        P.dma('sp', self.esel[:], self.c_esel[:])
        self.cm = P.sb([128, 512], BF16, "cm")
        P.dma('sp', self.cm[:], self.c_cm[:])
        self.ones_bf = P.sb([128, 64], BF16, "ones_bf")
        P.memset('pool', self.ones_bf[:], 1.0)
        self.ones64 = P.sb([128, 64], F32, "ones64")
        P.memset('pool', self.ones64[:], 1.0 / 64.0)
        self.lrup = P.sb([128, 2, 8], F32, "lrup")
        self.retp = P.sb([128, 2, 2], F32, "retp")
        self.kmean = P.sb([128, 8], F32, "kmean")
        self.pT = [P.sb([128, 512], BF16, f"pT{i}") for i in range(3)]
        self.small = P.sb([128, 8], F32, "small")
        self.wst_rr = 0
        self.rr = 0
        self.ms4 = P.sb([128, 512], F32, "ms4")
        P.dma('sp', self.ms4[:], self.c_ms4[:])
        self.mst = P.sb([128, 128], F32, "mst")
        P.dma('sp', self.mst[:], self.c_mst[:])
        self.blkones = P.sb([128, 128], F32, "blkones")
        P.dma('sp', self.blkones[:], self.c_blkones[:])
        self.rwkvp = P.sb([128, 2, 8], F32, "rwkvp")
        self.mulr = P.sb([128, 1], F32, "mulr")
        self.lrw = P.sb([128, GW], F32, "lrw")
        self.gnb = P.sb([128, 2, GW], F32, "gnb")
        self.gc = P.sb([128, 64], F32, "gc")
        self.rsm = [P.sb([128, 16], F32, f"rsm{i}") for i in range(2)]
        self.convp = P.sb([128, 44, 4], F32, "convp")
        self.halo = P.sb([128, 44, 2], F32, "halo")
        self.xtok = [P.sb([128, D], F32, f"xtok{i}") for i in range(3)]
        self.stat = [P.sb([128, 16], F32, f"stat{i}") for i in range(3)]
        self.S = P.sb([128, SCOLS], F32, "S", nsub=SCOLS // 512)
        self.wst = [P.sb([128, 8, 128], BF16, f"wst{i}") for i in range(6)]
        self.usb = [P.sb([128, 514], F32, f"usb{i}") for i in range(4)]
        self.ysb = [P.sb([128, 512], F32, f"ysb{i}") for i in range(4)]
        self.psA = [P.ps([128, 1024], F32, f"psA{i}", nsub=2) for i in range(1)]
        self.psB = [P.ps([128, 512], F32, f"psB{i}") for i in range(6)]
        self.psb_rr = 0

        for l in range(L):
            pass
        self.sbuf_free = self.nc.sbuf_bytes_remaining
        for s in range(NS):
            for l in range(L):
                self.layer(l, s, x, xmid, xl, out, w_out, ffn_w_up, ffn_w_down,
                           (ln1_g, ln1_b, ln2_g, ln2_b), convp)
        P.finalize()

    def sv(self, c0, n, dt=F32):
        ap = self.S.t[:, c0:c0 + n]
        if dt == BF16:
            ap = ap.bitcast(BF16)
        return self.S.v(ap, sub=list(range(c0 // 512, (c0 + n - 1) // 512 + 1)))

    def wbig_v(self, fc, c0=0, n=1024):
        base = fc * 512
        ap = self.S.t[:, base:base + 512].bitcast(BF16)[:, c0:c0 + n]
        return self.S.v(ap, sub=[base // 512])

    def actT_v(self, fc, c0=0, n=512):
        base = 11264 + fc * 256
        ap = self.S.t[:, base:base + 256].bitcast(BF16)[:, c0:c0 + n]
        return self.S.v(ap, sub=[base // 512])

    def next_psB(self):
        t = self.psB[self.psb_rr % len(self.psB)]
        self.psb_rr += 1
        return t

    def tok_to_xT(self, src_tl, i):
        P = self.P
        for g in range(2):
            ps = self.next_psB()
            for k in range(4):
                kk = g * 4 + k
                P.transpose(ps.v(ps.t[:, k * 128:(k + 1) * 128]),
                            src_tl.v(src_tl.t[:, kk * 128:(kk + 1) * 128]), self.ident[:])
            dst = self.xT.v(self.xT.t[:, g * 4:(g + 1) * 4, i * 128:(i + 1) * 128], sub=i // 4)
            srcv = ps.v(ps.t[:, :].rearrange("p (k t) -> p k t", k=4))
            P.copy('act' if g == 0 else 'dve', dst, srcv)

    def layer_norm_tile(self, ps_tl, res_tl, gidx, out_tl, stat):
        P = self.P
        s = res_tl
        P.stt('dve', s[:], res_tl[:], ALPHA, ps_tl[:], ALU.mult, ALU.add)
        st = stat
        for h in range(2):
            P.op('dve', lambda e, h=h: e.bn_stats(out=st.t[:, h * 6:(h + 1) * 6], in_=s.t[:, h * 512:(h + 1) * 512]),
                 reads=[s.buf], writes=[st.buf])
        P.op('dve', lambda e: e.bn_aggr(out=st.t[:, 12:14], in_=st.t[:, 0:12]), reads=[st.buf], writes=[st.buf])
        P.ts('dve', st.v(st.t[:, 14:15]), st.v(st.t[:, 13:14]), LN_EPS, None, ALU.add)
        P.act(st.v(st.t[:, 14:15]), st.v(st.t[:, 14:15]), AF.Sqrt)
        P.op('dve', lambda e: e.reciprocal(out=st.t[:, 14:15], in_=st.t[:, 14:15]), reads=[st.buf], writes=[st.buf])
        P.stt('dve', st.v(st.t[:, 15:16]), st.v(st.t[:, 12:13]), -1.0, st.v(st.t[:, 14:15]), ALU.mult, ALU.mult)
        P.act(s[:], s[:], AF.Identity, bias=st.v(st.t[:, 15:16]), scale=st.v(st.t[:, 14:15]))
        P.tt('dve', s[:], s[:], self.lnp.v(self.lnp.t[:, gidx, :], sub=gidx), ALU.mult)
        P.tt('dve', out_tl[:], s[:], self.lnp.v(self.lnp.t[:, gidx + 1, :], sub=gidx + 1), ALU.add)

    def layer(self, l, s, x, xmid, xl, out, w_out, ffn_w_up, ffn_w_down, lns, convp_d):
        P = self.P
        L = self.depth
        xin = x if l == 0 else xl
        last = (l == L - 1)
        xout = out if last else xl
        P.dma('sp', self.convp[:], V(convp_d.t[l], [convp_d.buf]))

        if l == 0:
            for i in range(16):
                xt = self.xtok[i % 2]
                P.dma('sp', xt[:], V(xin.t[s, i * 128:(i + 1) * 128, :], [xin.buf]))
                self.tok_to_xT(xt, i)

        self.mixers_phase(l, s)
        if 'yT' in self.debug and l == 0 and s == 0:
            P.dma('sp', self.dbg_yT[:], self.yT.all())

        for j, t in enumerate(lns[0:2]):
            P.dma('sp', self.lnp.v(self.lnp.t[:, j, :], sub=j), V(t.t[l, :].partition_broadcast(128), [t.buf]))
        for k in range(8):
            P.dma('sp', self.stg(k, False), V(w_out.t[l, k * 128:(k + 1) * 128, :], [w_out.buf]))
            P.copy('act', self.wbig_v(k), self.stg(k, False))
        def load_res2(i):
            xt = self.xtok[i % 3]
            src_sub = s * 16 + i
            P.dma('sp', xt[:], V(xin.t[s, i * 128:(i + 1) * 128, :], [xin.subs[src_sub]] if xin.subs else [xin.buf]))

        pend_st2 = []
        load_res2(0)
        for i in range(16):
            ps = self.psA[0]
            for hf in range(2):
                for k in range(8):
                    P.mm(ps.v(ps.t[:, hf * 512:(hf + 1) * 512]),
                         self.yT.v(self.yT.t[:, k, i * 128:(i + 1) * 128], sub=CH(k)),
                         self.wbig_v(k, hf * 512, 512),
                         start=(k == 0), stop=(k == 7))
            if i >= 2:
                self.tok_to_xT(self.xtok[(i - 2) % 3], i - 2)
            if i + 1 < 16:
                load_res2(i + 1)
            if i < 14:
                fcp = 8 + i
                P.dma('sp', self.stg(fcp, False), V(ffn_w_down.t[l, fcp * 128:(fcp + 1) * 128, :], [ffn_w_down.buf]))
                P.copy('act', self.wbig_v(fcp), self.stg(fcp, False))
            xt = self.xtok[i % 3]
            src_sub = s * 16 + i
            self.layer_norm_tile(ps, xt, 0, xt, self.stat[i % 3])
            if i >= 14:
                pend_st2.append((V(xmid.t[s, i * 128:(i + 1) * 128, :], [xmid.subs[src_sub]]), xt))
            else:
                P.dma('sp', V(xmid.t[s, i * 128:(i + 1) * 128, :], [xmid.subs[src_sub]]), xt[:])
        self.tok_to_xT(self.xtok[14 % 3], 14)
        self.tok_to_xT(self.xtok[15 % 3], 15)

        for j, t in enumerate(lns[2:4]):
            P.dma('sp', self.lnp.v(self.lnp.t[:, j, :], sub=j), V(t.t[l, :].partition_broadcast(128), [t.buf]))
        for fc in range(8):
            P.dma('sp', self.stg(fc, False), V(ffn_w_down.t[l, fc * 128:(fc + 1) * 128, :], [ffn_w_down.buf]))
            P.copy('act', self.wbig_v(fc), self.stg(fc, False))
        P.memset('pool', self.halo[:], 0.0)
        wsrc = ffn_w_up.t[l].rearrange("(k p) c -> p k c", p=128)
        seq = [(g, fc) for g in range(4) for fc in range(22)]
        wts = {}

        def issue(idx):
            g, fc = seq[idx]
            for half in range(2):
                j = 2 * idx + half
                wt = self.wst[j % 6]
                c0 = half * DFF + fc * 128
                P.dma('sp', self.stg(j), V(wsrc[:, :, c0:c0 + 128], [ffn_w_up.buf]))
                P.copy('dve' if half == 0 else 'act', wt[:], self.stg(j))
                wts[(idx, half)] = wt

        pend_xt = []
        pend_st = []

        def flush_xt():
            while pend_st:
                dst, xt_p = pend_st.pop(0)
                P.dma('sp', dst, xt_p[:])
            while pend_xt:
                self.tok_to_xT(*pend_xt.pop(0))

        def load_res3(i):
            xt = self.xtok[i % 3]
            P.dma('sp', xt[:], V(xmid.t[s, i * 128:(i + 1) * 128, :], [xmid.subs[s * 16 + i]]))

        issue(0)
        issue(1)
        while pend_st2:
            dst, xt_p = pend_st2.pop(0)
            P.dma('sp', dst, xt_p[:])
        for g in range(4):
            tok = slice(g * 512, (g + 1) * 512)
            for fc in range(22):
                idx = g * 22 + fc
                if idx + 2 < len(seq):
                    issue(idx + 2)
                if fc == 4:
                    flush_xt()
                if fc == 10:
                    load_res3(4 * g)
                ys = []
                for half in range(2):
                    ps = self.next_psB()
                    for k in range(8):
                        wt = wts[(idx, half)]
                        P.mm(ps[:], wt.v(wt.t[:, k, :]),
                             self.xT.v(self.xT.t[:, k, tok], sub=g), start=(k == 0), stop=(k == 7))
                    ch = half * 22 + fc
                    ub = self.usb[(fc * 2 + half) % 4]
                    yb = self.ysb[(fc * 2 + half) % 4]
                    cp = self.convp
                    P.copy('act', ub.v(ub.t[:, 2:514]), ps[:])
                    P.copy('pool', ub.v(ub.t[:, 0:2]), self.halo.v(self.halo.t[:, ch, :]))
                    P.copy('pool', self.halo.v(self.halo.t[:, ch, :]), ub.v(ub.t[:, 512:514]))
                    P.act(yb[:], ps[:], AF.Identity, bias=cp.v(cp.t[:, ch, 3:4]), scale=cp.v(cp.t[:, ch, 2:3]))
                    P.stt('dve', yb[:], ub.v(ub.t[:, 1:513]), cp.v(cp.t[:, ch, 1:2]), yb[:], ALU.mult, ALU.add)
                    P.stt('dve', yb[:], ub.v(ub.t[:, 0:512]), cp.v(cp.t[:, ch, 0:1]), yb[:], ALU.mult, ALU.add)
                    ys.append(yb)
                P.act(ys[0][:], ys[0][:], AF.Gelu_apprx_tanh)
                P.tt('pool', self.actT_v(fc), ys[0][:], ys[1][:], ALU.mult)
            for ii in range(4):
                i = g * 4 + ii
                ps = self.psA[0]
                for hf in range(2):
                    for fc in range(22):
                        P.mm(ps.v(ps.t[:, hf * 512:(hf + 1) * 512]),
                             self.actT_v(fc, ii * 128, 128),
                             self.wbig_v(fc, hf * 512, 512),
                             start=(fc == 0), stop=(fc == 21))
                if ii >= 2:
                    self.tok_to_xT(self.xtok[(i - 2) % 3], i - 2)
                if ii < 3:
                    load_res3(i + 1)
                xt = self.xtok[i % 3]
                sub = s * 16 + i
                self.layer_norm_tile(ps, xt, 0, xt, self.stat[i % 3])
                if ii >= 2:
                    pend_xt.append((xt, i))
                    pend_st.append((V(xout.t[s, i * 128:(i + 1) * 128, :], [xout.subs[sub]]), xt))
                else:
                    P.dma('sp', V(xout.t[s, i * 128:(i + 1) * 128, :], [xout.subs[sub]]), xt[:])
        flush_xt()

    COS_C0 = 0
    SIN_C0 = 2048
    A0 = 4096
    def R(self, i):
        return self.A0 + i * 2048

    def svb(self, c0, off, n):
        lo = c0 + off // 2
        hi = c0 + (off + n + 1) // 2
        ap = self.S.t[:, lo:hi].bitcast(BF16)[:, (off - 2 * (lo - c0)):(off - 2 * (lo - c0)) + n]
        return self.S.v(ap, sub=list(range(lo // 512, (hi - 1) // 512 + 1)))

    def negmT_v(self, off, n):
        c0 = self.R(7)
        ap = self.S.t[0:8, c0:c0 + 2048].bitcast(BF16)[:, off:off + n]
        return self.S.v(ap, sub=list(range((c0 + off // 2) // 512, (c0 + (off + n - 1) // 2) // 512 + 1)))

    def stg(self, j, three_d=True):
        c0 = 16896 + (j % 3) * 1024
        ap = self.S.t[:, c0:c0 + 1024]
        if three_d:
            ap = ap.rearrange("p (k c) -> p k c", k=8)
        return self.S.v(ap, sub=[c0 // 512, c0 // 512 + 1])

    def load_w(self, l, c0, n=128):
        P = self.P
        st = self.xtok[self.wst_rr % 2]
        wt = self.wst[self.wst_rr % 6]
        self.wst_rr += 1
        src = self.w_in.t[l].rearrange("(k p) c -> p k c", p=128)
        stv = st.v(st.t[:, :].rearrange("p (k c) -> p k c", k=8))
        P.dma('sp', stv, V(src[:, :, c0:c0 + 128], [self.w_in.buf]))
        P.copy('pool', wt[:], stv)
        return wt

    def proj_fm(self, l, c0, dst_c0, evac='act', wt=None):
        P = self.P
        if wt is None:
            wt = self.load_w(l, c0, 128)
        for tt in range(4):
            ps = self.next_psB()
            for k in range(8):
                P.mm(ps[:], wt.v(wt.t[:, k, 0:128]), self.xT.v(self.xT.t[:, k, tt * 512:(tt + 1) * 512], sub=tt),
                     start=(k == 0), stop=(k == 7))
            P.copy(evac, self.sv(dst_c0 + tt * 512, 512), ps[:])

    def proj_tm(self, l, c0, dst_c0):
        P = self.P
        wt = self.load_w(l, c0, 128)
        for g in range(4):
            ps = self.next_psB()
            for j in range(4):
                i = g * 4 + j
                for k in range(8):
                    P.mm(ps.v(ps.t[:, j * 128:(j + 1) * 128]),
                         self.xT.v(self.xT.t[:, k, i * 128:(i + 1) * 128], sub=i // 4),
                         wt.v(wt.t[:, k, 0:128]), start=(k == 0), stop=(k == 7))
            P.copy('act', self.svb(dst_c0, g * 512, 512), ps[:])

    def rope(self, src_c0, dst32_c0, dstb_c0):
        P = self.P
        for tt in range(4):
            ps = self.next_psB()
            src = self.sv(src_c0 + tt * 512, 512)
            P.mm(ps[:], self.perm[:], src)
            tmp = self.ysb[self.rr % 4]
            self.rr += 1
            d32 = self.sv(dst32_c0 + tt * 512, 512)
            P.tt('pool', tmp[:], src, self.sv(self.COS_C0 + tt * 512, 512), ALU.mult)
            P.tt('dve', d32, ps[:], self.sv(self.SIN_C0 + tt * 512, 512), ALU.mult)
            P.tt('dve', d32, d32, tmp[:], ALU.add)
            P.copy('act', self.svb(dstb_c0, tt * 512, 512), d32)

    def mixers_phase(self, l, s):
        P = self.P
        if 'zero' in self.mixers or not self.mixers:
            for k in range(8):
                P.memset('pool', self.yT.v(self.yT.t[:, k, :], sub=CH(k)), 0.0)
        if not self.mixers:
            return
        if 'rwkv' in self.mixers:
            self.rwkv(l)
        P.dma('sp', self.lrup[:], V(self.lrup_d.t[l], [self.lrup_d.buf]))
        P.dma('sp', self.retp[:], V(self.retp_d.t[l], [self.retp_d.buf]))
        if 'lru' in self.mixers:
            self.lru_both(l)
        P.dma('sp', self.sv(self.COS_C0, 2048), self.c_cos[:])
        P.dma('sp', self.sv(self.SIN_C0, 2048), self.c_sin[:])
        if 'moba' in self.mixers:
            for c in range(2):
                self.attn_chunk(l, c, 'moba')
        if 'ret' in self.mixers:
            for c in range(2):
                self.attn_chunk(l, c, 'ret')

    def lru_both(self, l):
        P = self.P
        pp = self.lrup
        sm = self.small
        CS = (0, 1)
        base = lambda c, i: c * 10240 + i * 2048
        XB = [base(c, 0) for c in CS]
        GB = [base(c, 1) for c in CS]
        XC = [base(c, 2) for c in CS]
        GR = [base(c, 3) for c in CS]
        GI = [base(c, 4) for c in CS]
        TMP = XB
        par = lambda c, j: pp.v(pp.t[:, c, j:j + 1])
        for c in CS:
            self.proj_fm(l, 896 + c * 128, XB[c])
        for c in CS:
            self.proj_fm(l, 1152 + c * 128, GB[c])
        wt = self.xtok[2]
        wblk = {}
        P.memset('pool', wt.v(wt.t[:, 0:512]), 0.0)
        for c in CS:
            for j, wsrc in enumerate((self.lru_w_r, self.lru_w_i)):
                col = (c * 2 + j) * 128
                wblk[(c, j)] = wt.v(wt.t[:, col:col + 128])
                for h2 in range(2):
                    P.dma('sp', wt.v(wt.t[h2 * 64:(h2 + 1) * 64, col + h2 * 64:col + (h2 + 1) * 64]),
                          V(wsrc.t[l, 2 * c + h2], [wsrc.buf]))
        for c in CS:
            P.act(self.sv(XC[c], T), self.sv(XB[c], T), AF.Identity, bias=par(c, 4), scale=par(c, 3))
        for sh in (1, 2, 3):
            for c in CS:
                P.stt('dve', self.sv(XC[c] + sh, T - sh), self.sv(XB[c], T - sh), par(c, 3 - sh),
                      self.sv(XC[c] + sh, T - sh), ALU.mult, ALU.add)
        smv = lambda c, j: sm.v(sm.t[:, 3 * c + j:3 * c + j + 1])
        for c in CS:
            P.act(smv(c, 0), par(c, 7), AF.Exp, scale=-1.0)
        for c in CS:
            P.ts('dve', smv(c, 0), smv(c, 0), 1.0, None, ALU.add)
        for c in CS:
            P.act(smv(c, 1), smv(c, 0), AF.Ln)
        for c in CS:
            P.ts('dve', smv(c, 2), smv(c, 1), -8.0, None, ALU.mult)
        for tt in range(4):
            for c in CS:
                for j, (dst, bj) in enumerate(((GR, 5), (GI, 6))):
                    ps = self.next_psB()
                    P.mm(ps[:], wblk[(c, j)], self.sv(XC[c] + tt * 512, 512))
                    P.act(self.sv(dst[c] + tt * 512, 512), ps[:], AF.Sigmoid, bias=par(c, bj))
        for c in CS:
            P.act(self.sv(GR[c], T), self.sv(GR[c], T), AF.Exp, scale=smv(c, 2))
        for c in CS:
            P.act(self.sv(TMP[c], T), self.sv(GR[c], T), AF.Square)
        for c in CS:
            P.ts('dve', self.sv(TMP[c], T), self.sv(TMP[c], T), -1.0, 1.0, ALU.mult, ALU.add)
        for c in CS:
            P.act(self.sv(TMP[c], T), self.sv(TMP[c], T), AF.Sqrt)
        for c in CS:
            P.tt('pool', self.sv(GI[c], T), self.sv(GI[c], T), self.sv(XC[c], T), ALU.mult)
        for c in CS:
            P.tt('dve', self.sv(GI[c], T), self.sv(GI[c], T), self.sv(TMP[c], T), ALU.mult)
        for c in CS:
            a_ap, u_ap, h_ap = self.sv(GR[c], T), self.sv(GI[c], T), self.sv(XC[c], T)
            P.op('dve', lambda e, a_ap=a_ap, u_ap=u_ap, h_ap=h_ap: e.tensor_tensor_scan(
                out=h_ap.ap, data0=a_ap.ap, data1=u_ap.ap, initial=0.0, op0=ALU.mult, op1=ALU.add),
                reads=_bufs(a_ap, u_ap), writes=_bufs(h_ap))
        for c in CS:
            P.act(self.sv(GB[c], T), self.sv(GB[c], T), AF.Gelu_apprx_tanh)
        for c in CS:
            P.tt('dve', self.yT.v(self.yT.t[:, 2 + c, :], sub=CH(2 + c)), self.sv(GB[c], T), self.sv(XC[c], T), ALU.mult)

    def attn_chunk(self, l, c, kind):
        P = self.P
        moba = (kind == 'moba')
        base = 1408 if moba else 2176
        ychunk = (4 if moba else 6) + c
        Q32, K32, QR32, KR32 = [self.R(i) for i in range(4)]
        QRB = self.R(4)
        KRB = self.R(4) + 1024
        VB = self.R(5)
        G32 = self.R(6)
        MSK = self.R(7)
        self.proj_fm(l, base + c * 128, Q32)
        self.proj_fm(l, base + 256 + c * 128, K32)
        self.rope(Q32, QR32, QRB)
        self.rope(K32, KR32, KRB)
        self.proj_tm(l, base + 512 + c * 128, VB)
        vb = lambda kt, h2: self.svb(VB, kt * 128 + h2 * 64, 64)
        if moba:
            kv = self.sv(KR32, T)
            km = self.kmean
            P.op('dve', lambda e: e.tensor_reduce(out=km.t[:, :], in_=kv.ap.rearrange("p (n k) -> p n k", k=256),
                                                  axis=AX.X, op=ALU.add), reads=_bufs(kv), writes=[km.buf])
            P.ts('dve', km[:], km[:], 1.0 / 256.0, None, ALU.mult)
            psgs = [self.next_psB(), self.next_psB()]
            for h2 in range(2):
                rows = slice(h2 * 64, (h2 + 1) * 64)
                psg = psgs[h2]
                for qt in range(16):
                    qsl = self.S.v(self.S.t[rows, QR32 + qt * 128:QR32 + (qt + 1) * 128],
                                   sub=[(QR32 + qt * 128) // 512])
                    P.mm(psg.v(psg.t[:, qt * 8:(qt + 1) * 8]), qsl, km.v(km.t[rows, :]))
            def s_tile(c0):
                tl = Tl(self.S.t[:, c0:c0 + 256], f"S@{c0}")
                tl.buf = self.S.subs[c0 // 512]
                return tl
            g = s_tile(self.R(6))
            bmv = self.sv(self.R(6) + 1536, 256)
            nov = self.sv(self.R(6) + 1792, 256)
            P.dma('sp', bmv, self.c_blockmask[:])
            P.dma('sp', nov, self.c_notown[:])
            for h2 in range(2):
                P.tt('dve', g.v(g.t[:, h2 * 128:(h2 + 1) * 128]), psgs[h2].v(psgs[h2].t[:, 0:128]),
                     self.sv(self.R(6) + 1536 + h2 * 128, 128), ALU.add)
            m8 = s_tile(self.R(6) + 512)
            for idx in range(32):
                P.op('dve', lambda e, idx=idx: e.max(out=m8.t[:, idx * 8:(idx + 1) * 8], in_=g.t[:, idx * 8:(idx + 1) * 8]),
                     reads=[g.buf], writes=[m8.buf])
            thr = m8.t[:, :].rearrange("p (i k) -> p i k", k=8)[:, :, 2:3].to_broadcast([128, 32, 8])
            g3 = g.t[:, :].rearrange("p (i k) -> p i k", k=8)
            nm = s_tile(self.R(6) + 1024)
            nm3 = nm.t[:, :].rearrange("p (i k) -> p i k", k=8)
            P.op('dve', lambda e: e.tensor_tensor(out=nm3, in0=g3, in1=thr, op=ALU.is_lt),
                 reads=[g.buf, m8.buf], writes=[nm.buf])
            P.tt('dve', nm[:], nm[:], nov, ALU.mult)
            for gq in range(8):
                ps = self.next_psB()
                for j in range(4):
                    idx = gq * 4 + j
                    P.transpose(ps.v(ps.t[0:8, j * 128:(j + 1) * 128]), nm.v(nm.t[:, idx * 8:(idx + 1) * 8]),
                                self.ident[:])
                P.copy('act', self.negmT_v(gq * 512, 512), ps.v(ps.t[0:8, :]))
        else:
            self.proj_fm(l, 2944 + c * 128, G32)
            P.act(self.sv(G32, T), self.sv(G32, T), AF.Silu)
        num_ps = self.psA[0]
        deferred = []
        MSKh = [MSK, Q32]
        gams = [1.0 - 2.0 ** (-5.0 - (2 * c + h2)) for h2 in range(2)]
        if not moba:
            for h2 in range(2):
                P.dma('sp', self.svb(MSKh[h2], 0, 2560), V(self.c_retm.t[2 * c + h2], [self.c_retm.buf]))
        ptbufs = [(pt.t, pt.buf) for pt in self.pT] + [(ub.t[:, 0:256].bitcast(BF16), ub.buf) for ub in self.usb]
        for qc in range(4):
            nkt = 4 * qc + 4

            def scores(h2, kt):
                rows = slice(h2 * 64, (h2 + 1) * 64)
                n = kt // 2
                c_lo = 0
                if moba and n * 256 > qc * 512:
                    c_lo = 256
                w = 512 - c_lo
                q0 = qc * 512 + c_lo
                ps = self.next_psB()
                ksl = self.S.v(self.S.t[rows, KRB + kt * 64:KRB + (kt + 1) * 64].bitcast(BF16),
                               sub=[(KRB + kt * 64) // 512])
                qsl = self.S.v(self.S.t[rows, QRB + q0 // 2:QRB + (q0 + w) // 2].bitcast(BF16),
                               sub=list(range((QRB + q0 // 2) // 512, (QRB + (q0 + w) // 2 - 1) // 512 + 1)))
                pap, pbuf = ptbufs[self.rr % len(ptbufs)]
                self.rr += 1
                ptv = lambda a, bb: V(pap[:, a:bb], [pbuf])
                if moba:
                    need_mask = (qc >= 2) and (n != 2 * qc + 1)
                    P.mm(ps.v(ps.t[:, 0:w]), ksl, qsl, start=True, stop=not need_mask)
                    if need_mask:
                        P.mm(ps.v(ps.t[:, 0:w]), self.esel.v(self.esel.t[:, n * 128:(n + 1) * 128]),
                             self.negmT_v(h2 * T + q0, w), start=False, stop=True)
                    P.act(ptv(0, w), ps.v(ps.t[:, 0:w]), AF.Exp, scale=0.125)
                    if n == 2 * qc or n == 2 * qc + 1:
                        P.tt('dve', ptv(0, 256), ptv(0, 256),
                             self.cm.v(self.cm.t[:, (kt % 2) * 256:(kt % 2 + 1) * 256]), ALU.mult)
                else:
                    P.mm(ps[:], ksl, qsl)
                    r = kt - 4 * qc
                    if r >= 0:
                        P.tt('dve', ptv(0, 512), ps[:], self.svb(MSKh[h2], r * 512, 512), ALU.mult)
                    else:
                        P.stt('dve', ptv(0, 512), ps[:], float(gams[h2] ** (qc * 512 - kt * 128)),
                              self.svb(MSKh[h2], 4 * 512, 512), ALU.mult, ALU.mult)
                return (h2, kt, ptv, c_lo, w)

            def pv(st):
                h2, kt, ptv, c_lo, w = st
                rows = slice(h2 * 64, (h2 + 1) * 64)
                P.mm(num_ps.v(num_ps.t[rows, c_lo:512]), vb(kt, h2), ptv(0, w),
                     start=(kt == 0), stop=(kt == nkt - 1))
                if moba:
                    P.mm(num_ps.v(num_ps.t[rows, 512 + c_lo:1024]), self.ones_bf[:], ptv(0, w),
                         start=(kt == 0), stop=(kt == nkt - 1))

            pend = []
            for kt in range(nkt):
                for h2 in range(2):
                    pend.append(scores(h2, kt))
                while len(pend) > 4:
                    pv(pend.pop(0))
            while pend:
                pv(pend.pop(0))

            for h2 in range(2):
                rows = slice(h2 * 64, (h2 + 1) * 64)
                ydst = self.yT.v(self.yT.t[rows, ychunk, qc * 512:(qc + 1) * 512], sub=CH(ychunk))
                numv = num_ps.v(num_ps.t[rows, 0:512])
                bi = qc % 2
                t0 = self.ysb[2 * bi]
                t1 = self.ysb[2 * bi + 1]
                t0v = t0.v(t0.t[rows, :])
                t1v = t1.v(t1.t[rows, :])
                if moba:
                    P.op('dve', lambda e, t0=t0, rows=rows: e.reciprocal(out=t0.t[rows, :], in_=num_ps.t[rows, 512:1024]),
                         reads=num_ps.subs, writes=[t0.buf])
                    P.tt('dve', ydst, numv, t0v, ALU.mult)
                else:
                    P.copy('act', t0v, numv)
                    P.act(t1v, numv, AF.Square)

                    def headnorm(rows=rows, t0=t0, t1=t1, t0v=t0v, t1v=t1v, ydst=ydst, qc=qc):
                        rp = self.retp
                        ps_m = self.next_psB()
                        ps_q = self.next_psB()
                        o64 = self.ones64.v(self.ones64.t[rows, :])
                        P.mm(ps_m.v(ps_m.t[rows, :]), o64, t0v)
                        P.mm(ps_q.v(ps_q.t[rows, :]), o64, t1v)
                        pm = ps_m.v(ps_m.t[rows, :])
                        pq = ps_q.v(ps_q.t[rows, :])
                        P.act(t1v, pm, AF.Square)
                        P.tt('dve', t1v, pq, t1v, ALU.subtract)
                        P.ts('dve', t1v, t1v, 1e-5, None, ALU.add)
                        P.act(t1v, t1v, AF.Sqrt)
                        P.op('dve', lambda e, t1=t1, rows=rows: e.reciprocal(out=t1.t[rows, :], in_=t1.t[rows, :]),
                             reads=[t1.buf], writes=[t1.buf])
                        P.tt('dve', t0v, t0v, pm, ALU.subtract)
                        P.tt('dve', t0v, t0v, t1v, ALU.mult)
                        P.ts('dve', t0v, t0v, rp.v(rp.t[rows, c, 0:1]), rp.v(rp.t[rows, c, 1:2]), ALU.mult, ALU.add)
                        gsl = self.S.v(self.S.t[rows, G32 + qc * 512:G32 + (qc + 1) * 512], sub=[(G32 + qc * 512) // 512])
                        P.tt('dve', ydst, t0v, gsl, ALU.mult)

                    deferred.append(headnorm)
            while len(deferred) > 2:
                deferred.pop(0)()
        while deferred:
            deferred.pop(0)()

    def rwkv(self, l):
        P = self.P
        S = self.S
        Zc = lambda i: i * 2048

        def Z(i, a=0, n=T):
            return self.sv(Zc(i) + a, n)

        def Zr(i, rows, a, n):
            c0 = Zc(i) + a
            return S.v(S.t[rows, c0:c0 + n], sub=list(range(c0 // 512, (c0 + n - 1) // 512 + 1)))

        def Z3(i):
            return S.t[:, Zc(i):Zc(i) + T].rearrange("p (c k) -> p c k", k=64)

        slot = {'r': 16, 'f': 40}

        class RT:
            def __init__(s2, kind='f'):
                i = slot[kind]
                slot[kind] += 1
                if kind == 'r':
                    assert i < 40
                    s2.tr = self.rtpoolR[:, (i - 16) * 128:(i - 15) * 128]
                    s2.t = s2.tr.bitcast(F32)
                else:
                    assert i < 64
                    s2.t = self.rtpoolF[:, (i - 40) * 128:(i - 39) * 128]
                    s2.tr = None
                s2.b = self.yT.subs[i]

            def v(s2, rows=slice(0, 128), cols=slice(0, 128)):
                return V(s2.t[rows, cols], [s2.b])

            def vr(s2, rows=slice(0, 128), cols=slice(0, 128)):
                return V(s2.t[rows, cols], [s2.b])

        pp = self.rwkvp
        P.dma('sp', pp[:], V(self.rwkvp_d.t[l], [self.rwkvp_d.buf]))
        P.dma('sp', self.mulr[:], V(self.mulr_d.t[l], [self.mulr_d.buf]))
        P.dma('sp', self.lrw.v(self.lrw.t[0:32, :]), V(self.rwkv_w_up.t[l], [self.rwkv_w_up.buf]))
        P.dma('sp', self.lrw.v(self.lrw.t[32:64, :]), V(self.rwkv_a_up.t[l], [self.rwkv_a_up.buf]))
        P.dma('sp', self.lrw.v(self.lrw.t[64:128, :]), V(self.rwkv_g_up.t[l], [self.rwkv_g_up.buf]))
        P.dma('sp', self.gnb.v(self.gnb.t[:, 0, :]), V(self.rwkv_gn_g.t[l, :].partition_broadcast(128), [self.rwkv_gn_g.buf]))
        P.dma('sp', self.gnb.v(self.gnb.t[:, 1, :]), V(self.rwkv_gn_b.t[l, :].partition_broadcast(128), [self.rwkv_gn_b.buf]))

        def shiftmix(i, mu, tmp):
            P.tt('dve', Z(tmp, 1, T - 1), Z(i, 0, T - 1), Z(i, 1, T - 1), ALU.subtract)
            P.ts('dve', Z(tmp, 0, 1), Z(i, 0, 1), -1.0, None, ALU.mult)
            P.stt('dve', Z(i), Z(tmp), mu, Z(i), ALU.mult, ALU.add)

        LR, RR, KK, VV, LW, LL, AA, KAP, EE, RKR = range(10)
        self.proj_fm(l, 768, Zc(LR))
        shiftmix(LR, self.mulr[:], EE)
        P.act(Zr(LR, slice(0, 32), 0, T), Zr(LR, slice(0, 32), 0, T), AF.Tanh)
        P.act(Zr(LR, slice(64, 128), 0, T), Zr(LR, slice(64, 128), 0, T), AF.Sigmoid)

        for c in range(2):
            par = lambda j, c=c: pp.v(pp.t[:, c, j:j + 1])
            self.proj_fm(l, c * 128, Zc(RR))
            self.proj_fm(l, 256 + c * 128, Zc(KK))
            self.proj_fm(l, 512 + c * 128, Zc(VV))
            shiftmix(RR, par(0), EE)
            shiftmix(KK, par(1), EE)
            shiftmix(VV, par(2), EE)
            for tt in range(4):
                psw = self.next_psB()
                P.mm(psw[:], self.lrw.v(self.lrw.t[0:32, c * 128:(c + 1) * 128]), Zr(LR, slice(0, 32), tt * 512, 512))
                P.act(Z(LW, tt * 512, 512), psw[:], AF.Sigmoid, bias=par(3))
                psa = self.next_psB()
                P.mm(psa[:], self.lrw.v(self.lrw.t[32:64, c * 128:(c + 1) * 128]), Zr(LR, slice(32, 64), tt * 512, 512))
                P.act(Z(AA, tt * 512, 512), psa[:], AF.Sigmoid, bias=par(4))
            P.act(Z(LW), Z(LW), AF.Copy, scale=-0.6065306597126334)
            P.ts('dve', Z(KAP), Z(KK), par(5), None, ALU.mult)
            P.act(Z(EE), Z(KAP), AF.Square)
            for tt in range(4):
                ps = self.next_psB()
                P.mm(ps[:], self.blkones[:], Z(EE, tt * 512, 512))
                P.act(Z(EE, tt * 512, 512), ps[:], AF.Sqrt)
            P.ts('dve', Z(EE), Z(EE), 1e-12, None, ALU.max)
            ee = Z(EE)
            P.op('dve', lambda e, ee=ee: e.reciprocal(out=ee.ap, in_=ee.ap), reads=_bufs(ee), writes=_bufs(ee))
            P.tt('dve', Z(KAP), Z(KAP), Z(EE), ALU.mult)
            sm = self.rsm[c]
            P.ts('dve', sm.v(sm.t[:, 0:1]), par(6), -1.0, 1.0, ALU.mult, ALU.add)
            P.act(Z(EE), Z(AA), AF.Identity, bias=sm.v(sm.t[:, 0:1]), scale=par(6))
            P.tt('dve', Z(KK), Z(KK), Z(EE), ALU.mult)
            P.tt('dve', Z(AA), Z(KAP), Z(AA), ALU.mult)
            P.stt('dve', Z(RKR), Z(RR), par(7), Z(KK), ALU.mult, ALU.mult)
            P.memset('pool', Z(EE), 1.0)
            e3 = S.v(Z3(EE)[:, :, 0:1], sub=list(range(Zc(EE) // 512, Zc(EE) // 512 + 4)))
            P.memset('pool', e3, 0.0)
            d0, d1, lo = Z(EE), Z(LW), Z(LL)
            P.op('dve', lambda e, d0=d0, d1=d1, lo=lo: e.tensor_tensor_scan(out=lo.ap, data0=d0.ap, data1=d1.ap, initial=0.0,
                                                                           op0=ALU.mult, op1=ALU.add),
                 reads=_bufs(d0, d1), writes=_bufs(lo))
            P.tt('dve', Z(LW), Z(LL), Z(LW), ALU.subtract)
            gc = self.gc
            lend = S.v(Z3(LL)[:, :, 63:64], sub=list(range(Zc(LL) // 512, Zc(LL) // 512 + 4)))
            P.act(gc.v(gc.t[:, 0:32].rearrange("p (c k) -> p c k", k=1)), lend, AF.Exp)
            P.ts('dve', gc.v(gc.t[:, 32:64]), gc.v(gc.t[:, 0:32]), -1.0, None, ALU.mult)
            P.act(Z(EE), Z(LL), AF.Exp)
            P.tt('dve', Z(RR), Z(RR), Z(EE), ALU.mult)
            P.act(Z(EE), Z(LW), AF.Exp)
            P.tt('dve', Z(KAP), Z(KAP), Z(EE), ALU.mult)
            P.act(Z(EE), Z(LL), AF.Exp, scale=-1.0)
            P.tt('dve', Z(AA), Z(AA), Z(EE), ALU.mult)
            P.tt('dve', Z(KK), Z(KK), Z(EE), ALU.mult)
            allsub = lambda i: list(range(Zc(i) // 512, Zc(i) // 512 + 4))
            ngc_b = gc.t[:, 32:64].rearrange("p (c k) -> p c k", k=1).to_broadcast([128, 32, 64])
            gc_b = gc.t[:, 0:32].rearrange("p (c k) -> p c k", k=1).to_broadcast([128, 32, 64])
            P.tt('dve', S.v(Z3(LW), sub=allsub(LW)), S.v(Z3(AA), sub=allsub(AA)), V(ngc_b, [gc.buf]), ALU.mult)
            P.tt('dve', S.v(Z3(EE), sub=allsub(EE)), S.v(Z3(KK), sub=allsub(KK)), V(gc_b, [gc.buf]), ALU.mult)
            NB, KB = LW, EE

            slot['f'] = 40
            H = [[RT(), RT()] for _ in range(2)]
            for h2 in range(2):
                rows = slice(h2 * 64, (h2 + 1) * 64)
                P.memset('pool', H[h2][0].v(rows, slice(0, 64)), 0.0)
            pending_out = []

            def flush_out():
                while pending_out:
                    yn_p, t0_p = pending_out.pop(0)
                    pso = self.next_psB()
                    P.transpose(pso.v(pso.t[:, 0:128]), yn_p.v(), self.ident[:])
                    P.copy('act', self.yT.v(self.yT.t[:, c, t0_p:t0_p + 128], sub=CH(c)), pso.v(pso.t[:, 0:128]))

            for cp in range(16):
                t0 = cp * 128
                slot['r'] = 16
                slot['f'] = 44
                pst = self.next_psB()
                for j, reg in enumerate((KAP, NB, KB, VV)):
                    P.transpose(pst.v(pst.t[:, j * 128:(j + 1) * 128], sub=j), Z(reg, t0, 128), self.ident[:])
                TMa, TMb = RT(), RT()
                TMc, TMd = RT(), RT('r')
                tms = (TMa, TMb, TMc, TMd)
                for j in range(4):
                    P.copy('act' if j % 2 == 0 else 'dve', tms[j].vr() if j == 3 else tms[j].v(),
                           pst.v(pst.t[:, j * 128:(j + 1) * 128], sub=j))
                KAPt, NBt, KBt, Vt = tms
                st = []
                for h2 in range(2):
                    rows = slice(h2 * 64, (h2 + 1) * 64)
                    d = {'rows': rows, 'h2': h2}
                    d.update(psg=self.next_psB(), psa=self.next_psB(),
                             bh=Zr(AA, rows, t0, 128), kh=Zr(KK, rows, t0, 128),
                             kap=Zr(KAP, rows, t0, 128), rh=Zr(RR, rows, t0, 128))
                    st.append(d)
                for j, (la, ra) in enumerate((('bh', 'kap'), ('bh', 'rh'), ('kh', 'kap'), ('kh', 'rh'))):
                    for d in st:
                        psg = d['psg']
                        P.mm(psg.v(psg.t[:, j * 128:(j + 1) * 128], sub=j), d[la], d[ra])
                for d in st:
                    psa = d['psa']
                    P.mm(psa.v(psa.t[:, 0:128], sub=0), d['kap'], d['bh'])
                flush_out()
                for d in st:
                    psg, psa = d['psg'], d['psa']
                    G4 = [RT('r') for _ in range(4)]
                    for j in range(4):
                        P.tt('dve', G4[j].vr(), psg.v(psg.t[:, j * 128:(j + 1) * 128], sub=j),
                             self.ms4.v(self.ms4.t[:, j * 128:(j + 1) * 128]), ALU.mult)
                    Asb = RT('r')
                    P.tt('dve', Asb.vr(), psa.v(psa.t[:, 0:128], sub=0), self.mst[:], ALU.mult)
                    NTt, NAb, BT, ArT = G4
                    W = RT('r')
                    d.update(W=W, Np=NTt, Ap=Asb, NAb=NAb, ArT=ArT, BT=BT,
                             Wr=[RT('r'), W], Nr=[RT('r'), RT('r')], Ar=[RT('r'), RT('r')])
                for d in st:
                    hs = d['rows']
                    psa = d['psa']
                    P.mm(psa.v(psa.t[:, 128:192], sub=1), d['BT'].vr(), Vt.vr(cols=hs))
                for d in st:
                    hs = d['rows']
                    psa = d['psa']
                    W = d['W']
                    P.copy('act', W.vr(cols=slice(64, 128)), psa.v(psa.t[:, 128:192], sub=1))
                    P.copy('dve', W.vr(cols=slice(0, 64)), KAPt.v(cols=hs))
                for lvl in range(6):
                    for d in st:
                        ps = self.next_psB()
                        d['ps'] = ps
                        P.mm(ps.v(ps.t[:, 0:128], sub=0), d['Np'].vr(), d['W'].vr())
                        if lvl < 5:
                            P.mm(ps.v(ps.t[:, 128:256], sub=1), d['Ap'].vr(), d['Np'].vr())
                        if lvl < 4:
                            P.mm(ps.v(ps.t[:, 256:384], sub=2), d['Np'].vr(), d['Ap'].vr())
                    for d in st:
                        ps = d['ps']
                        Wn = d['Wr'][lvl % 2]
                        P.tt('dve', Wn.vr(), d['W'].v(), ps.v(ps.t[:, 0:128], sub=0),
                             ALU.subtract if lvl == 0 else ALU.add)
                        d['W'] = Wn
                        if lvl < 5:
                            Nn = d['Nr'][lvl % 2]
                            P.copy('act', Nn.vr(), ps.v(ps.t[:, 128:256], sub=1))
                        if lvl < 4:
                            An = d['Ar'][lvl % 2]
                            P.copy('act', An.vr(), ps.v(ps.t[:, 256:384], sub=2))
                            d['Ap'] = An
                        if lvl < 5:
                            d['Np'] = Nn
                ytm = RT()
                for d in st:
                    rows = d['rows']
                    hs = rows
                    W = d['W']
                    ps = self.next_psB()
                    d['ps'] = ps
                    P.mm(ps.v(ps.t[:, 0:64], sub=0), d['ArT'].vr(), Vt.vr(cols=hs), start=True, stop=False)
                    P.mm(ps.v(ps.t[:, 0:64], sub=0), d['NAb'].vr(), W.vr(cols=slice(64, 128)), start=False, stop=True)
                    P.mm(ps.v(ps.t[rows, 128:256], sub=1), W.v(cols=slice(0, 64)), d['NAb'].v())
                    d['ps2'] = []
                    for q in range(2):
                        qs = slice(q * 64, (q + 1) * 64)
                        ps2 = self.next_psB()
                        d['ps2'].append(ps2)
                        P.mm(ps2.v(ps2.t[rows, 0:64], sub=0), W.v(qs, slice(0, 64)), NBt.v(qs, hs))
                        P.mm(ps2.v(ps2.t[rows, 128:192], sub=1), KBt.v(qs, hs), Vt.v(qs, hs), start=True, stop=False)
                        P.mm(ps2.v(ps2.t[rows, 128:192], sub=1), NBt.v(qs, hs), W.v(qs, slice(64, 128)), start=False, stop=True)
                for d in st:
                    rows = d['rows']
                    ps = d['ps']
                    Y0 = RT()
                    P.copy('act', Y0.v(cols=slice(0, 64)), ps.v(ps.t[:, 0:64], sub=0))
                    RH = RT()
                    P.tt('dve', RH.v(rows), Zr(RR, rows, t0, 128), ps.v(ps.t[rows, 128:256], sub=1), ALU.add)
                    d.update(Y0=Y0, RH=RH, GT=[], H0c=[])
                    for q in range(2):
                        ch = 2 * cp + q
                        ps2 = d['ps2'][q]
                        GT = RT()
                        P.stt('dve', GT.v(rows, slice(0, 64)), self.ident.v(self.ident.t[rows, rows]),
                              self.gc.v(self.gc.t[rows, ch:ch + 1]), ps2.v(ps2.t[rows, 0:64], sub=0), ALU.mult, ALU.add)
                        H0c = RT()
                        P.copy('act', H0c.v(rows, slice(0, 64)), ps2.v(ps2.t[rows, 128:192], sub=1))
                        d['GT'].append(GT)
                        d['H0c'].append(H0c)
                for q in range(2):
                    qs = slice(q * 64, (q + 1) * 64)
                    ch = 2 * cp + q
                    for d in st:
                        rows = d['rows']
                        hs = rows
                        h2 = d['h2']
                        ps3 = self.next_psB()
                        d['ps3'] = ps3
                        Hc = H[h2][ch % 2]
                        P.mm(ps3.v(ps3.t[qs, h2 * 64:(h2 + 1) * 64], sub=0),
                             d['RH'].v(rows, qs), Hc.v(rows, slice(0, 64)))
                        P.mm(ps3.v(ps3.t[rows, 128:192], sub=1), d['GT'][q].v(rows, slice(0, 64)), Hc.v(rows, slice(0, 64)))
                    for d in st:
                        rows = d['rows']
                        hs = rows
                        h2 = d['h2']
                        ps3 = d['ps3']
                        Hn = H[h2][(ch + 1) % 2]
                        P.tt('dve', Hn.v(rows, slice(0, 64)), d['H0c'][q].v(rows, slice(0, 64)),
                             ps3.v(ps3.t[rows, 128:192], sub=1), ALU.add)
                        P.tt('dve', ytm.v(qs, hs), d['Y0'].v(qs, slice(0, 64)),
                             ps3.v(ps3.t[qs, h2 * 64:(h2 + 1) * 64], sub=0), ALU.add)
                sm = self.rsm[cp % 2]
                psx = self.next_psB()
                psy = self.next_psB()
                for h2 in range(2):
                    rows = slice(h2 * 64, (h2 + 1) * 64)
                    hs = rows
                    P.op('dve', lambda e, ytm=ytm, hs=hs, h2=h2, sm=sm: e.bn_stats(out=sm.t[:, h2 * 6:(h2 + 1) * 6], in_=ytm.t[:, hs]),
                         reads=[ytm.b], writes=[sm.buf])
                    P.op('dve', lambda e, h2=h2, sm=sm: e.bn_aggr(out=sm.t[:, 12 + 2 * h2:14 + 2 * h2], in_=sm.t[:, h2 * 6:(h2 + 1) * 6]),
                         reads=[sm.buf], writes=[sm.buf])
                    pb = psx if h2 == 0 else psy
                    P.mm(pb.v(pb.t[:, 0:1]), Zr(RKR, rows, t0, 128), self.ones64.v(self.ones64.t[rows, 0:1]))
                var2 = sm.t[:, 12:16].rearrange("p (h k) -> p h k", k=2)[:, :, 1:2]
                rs2 = sm.t[:, 8:10].rearrange("p (h k) -> p h k", k=1)
                P.op('dve', lambda e, var2=var2, rs2=rs2: e.tensor_scalar(out=rs2, in0=var2, scalar1=64e-5, scalar2=None, op0=ALU.add),
                     reads=[sm.buf], writes=[sm.buf])
                P.act(sm.v(sm.t[:, 8:10]), sm.v(sm.t[:, 8:10]), AF.Sqrt)
                P.op('dve', lambda e, sm=sm: e.reciprocal(out=sm.t[:, 8:10], in_=sm.t[:, 8:10]), reads=[sm.buf], writes=[sm.buf])
                P.ts('dve', sm.v(sm.t[:, 10:11]), psx.v(psx.t[:, 0:1]), 64.0, None, ALU.mult)
                P.ts('dve', sm.v(sm.t[:, 11:12]), psy.v(psy.t[:, 0:1]), 64.0, None, ALU.mult)
                yn = RT()
                for h2 in range(2):
                    hs = slice(h2 * 64, (h2 + 1) * 64)
                    P.ts('dve', yn.v(cols=hs), ytm.v(cols=hs), sm.v(sm.t[:, 12 + 2 * h2:13 + 2 * h2]),
                         sm.v(sm.t[:, 8 + h2:9 + h2]), ALU.subtract, ALU.mult)
                gsl = slice(c * 128, (c + 1) * 128)
                P.tt('dve', yn.v(), yn.v(), self.gnb.v(self.gnb.t[:, 0, gsl]), ALU.mult)
                P.tt('dve', yn.v(), yn.v(), self.gnb.v(self.gnb.t[:, 1, gsl]), ALU.add)
                for h2 in range(2):
                    hs = slice(h2 * 64, (h2 + 1) * 64)
                    P.stt('dve', yn.v(cols=hs), Vt.v(cols=hs), sm.v(sm.t[:, 10 + h2:11 + h2]), yn.v(cols=hs), ALU.mult, ALU.add)
                P.mm(psx.v(psx.t[:, 128:256], sub=1), Zr(LR, slice(64, 128), t0, 128),
                     self.lrw.v(self.lrw.t[64:128, c * 128:(c + 1) * 128]))
                P.tt('dve', yn.v(), yn.v(), psx.v(psx.t[:, 128:256], sub=1), ALU.mult)
                pending_out.append((yn, t0))
            flush_out()


_CACHE = {}


def host_consts():
    c = {}
    c["ident"] = np.eye(128, dtype=np.float32)
    perm = np.zeros((128, 128), np.float32)
    for m in range(128):
        d = m % 64
        if d < 32:
            perm[m + 32, m] = -1.0
        else:
            perm[m - 32, m] = 1.0
    c["c_perm"] = perm
    inv = (np.float32(10000.0) ** (-np.arange(0, 64, 2, dtype=np.float32) / np.float32(64))).astype(np.float32)
    ang = (np.arange(T, dtype=np.float32)[:, None] * inv[None, :]).astype(np.float32)
    cos = np.cos(ang.astype(np.float64)).astype(np.float32).T
    sin = np.sin(ang.astype(np.float64)).astype(np.float32).T
    c["c_cos"] = np.ascontiguousarray(np.tile(cos, (4, 1)))
    c["c_sin"] = np.ascontiguousarray(np.tile(sin, (4, 1)))
    bm = np.zeros((128, 32, 8), np.float32)
    no = np.zeros((128, 32, 8), np.float32)
    for idx in range(32):
        qt = idx % 16
        blk = qt // 2
        for n in range(8):
            bm[:, idx, n] = 0.0 if n < blk else -1e30
            no[:, idx, n] = 0.0 if n == blk else -30000.0
    c["c_blockmask"] = bm.reshape(128, 256)
    c["c_notown"] = no.reshape(128, 256)
    es = np.zeros((8, 8, 128), np.float32)
    for n in range(8):
        es[n, n, :] = 1.0
    c["c_esel"] = np.ascontiguousarray(es.transpose(1, 0, 2).reshape(8, 1024)).astype(ml_dtypes.bfloat16)
    p = np.arange(128)[:, None]
    cc = np.arange(256)[None, :]
    cm = np.stack([(r * 128 + p <= cc).astype(np.float32) for r in range(2)], axis=1)
    c["c_cm"] = np.ascontiguousarray(cm.reshape(128, 512)).astype(ml_dtypes.bfloat16)
    retm = np.zeros((4, 128, 5, 512), np.float64)
    col = np.arange(512)[None, :].astype(np.float64)
    pr = np.arange(128)[:, None].astype(np.float64)
    for h in range(4):
        gam = 1.0 - 2.0 ** (-5.0 - h)
        for r in range(4):
            dd = col - pr - 128.0 * r
            retm[h, :, r, :] = np.where(dd >= 0, 0.125 * gam ** np.maximum(dd, 0.0), 0.0)
        retm[h, :, 4, :] = 0.125 * gam ** (col - pr)
    c["c_retm"] = np.ascontiguousarray(retm.reshape(4, 128, 2560).astype(np.float32)).astype(ml_dtypes.bfloat16)
    si = np.arange(128)[:, None]
    ti = np.arange(128)[None, :]
    same = (si // 64) == (ti // 64)
    ms = (same & (si < ti)).astype(np.float32)
    mi = (same & (si <= ti)).astype(np.float32)
    c["c_ms4"] = np.ascontiguousarray(np.concatenate([ms, -mi, ms, mi], axis=1))
    c["c_mst"] = np.ascontiguousarray(ms.T)
    c["c_blkones"] = same.astype(np.float32)
    return c


def relayout_params(inp):
    o = {}
    cw = np.asarray(inp["ffn_conv_w"], np.float32)
    cb = np.asarray(inp["ffn_conv_b"], np.float32)
    cat = np.concatenate([cw, cb[:, None, :]], axis=1)
    o["ffn_convp"] = np.ascontiguousarray(cat.reshape(DEPTH, 4, 44, 128).transpose(0, 3, 2, 1))
    f = lambda k: np.asarray(inp[k], np.float32)
    lr = np.concatenate([f("lru_conv_w"), f("lru_conv_b")[:, None], f("lru_b_r")[:, None], f("lru_b_i")[:, None],
                         f("lru_lambda")[:, None]], axis=1)
    o["lrup"] = np.ascontiguousarray(lr.reshape(DEPTH, 8, 2, 128).transpose(0, 3, 2, 1))
    mu = f("tshift_mu")
    z = np.zeros_like(f("rwkv_w0"))
    rk = f("rwkv_r_k").reshape(DEPTH, 256)
    rw = np.stack([mu[:, 0:256], mu[:, 256:512], mu[:, 512:768], f("rwkv_w0"), f("rwkv_a0"), f("rwkv_k_k"),
                   f("rwkv_k_a"), rk], axis=1)
    o["rwkvp"] = np.ascontiguousarray(rw.reshape(DEPTH, 8, 2, 128).transpose(0, 3, 2, 1))
    o["mulr"] = np.ascontiguousarray(mu[:, 768:896].reshape(DEPTH, 128, 1))
    rp = np.stack([f("ret_gn_g"), f("ret_gn_b")], axis=1)
    o["retp"] = np.ascontiguousarray(rp.reshape(DEPTH, 2, 2, 128).transpose(0, 3, 2, 1))
    return o


def kernel(**inputs):
    if "b" not in _CACHE:
        _CACHE["b"] = Builder()
    b = _CACHE["b"]
    consts = host_consts()
    rel = relayout_params(inputs)
    x = np.ascontiguousarray(np.asarray(inputs["x"], np.float32))
    shared = {}
    for name in b.dram:
        if name in ("x", "out", "xmid", "xl"):
            continue
        if name in consts:
            shared[name] = consts[name]
        elif name in rel:
            shared[name] = rel[name]
        else:
            shared[name] = np.ascontiguousarray(np.asarray(inputs[name], np.float32))
    in_maps = []
    for c in range(NCORES):
        m = dict(shared)
        m["x"] = x[2 * c:2 * c + 2]
        in_maps.append(m)
    res = run_bass_kernel_spmd(b.nc, in_maps, core_ids=list(range(NCORES)))
    outs = [np.asarray(r["out"]) for r in res.results]
    return np.concatenate(outs, axis=0).astype(np.float32)
```
